# Optimizing a Trainium2 kernel written in Bass

```python
import math
import jax, jax.numpy as jnp
from jax import lax
import numpy as np

D_MODEL = 1024
BATCH = 8
SEQ = 4096
DEPTH = 2

N_EVEN = (DEPTH + 1) // 2
N_ODD = DEPTH // 2
GRID_W = 64
EPS = 1e-6
MIX_A = D_MODEL // 2
FNET_HEAD_DIM = 64
FNET_HEADS = MIX_A // FNET_HEAD_DIM
HEAD_DIM = 64
N_Q_HEADS = (D_MODEL - MIX_A) // HEAD_DIM
N_KV_HEADS = 2
GQA_GROUP = N_Q_HEADS // N_KV_HEADS
IN_WIDTH = MIX_A + (N_Q_HEADS + 2 * N_KV_HEADS) * HEAD_DIM
Q_BLOCK = 128
ROPE_THETA = 10000.0
ROPE_AXIS_DIM = HEAD_DIM // 2
SSM_GROUP_CH = 16
SSM_GROUPS = D_MODEL // SSM_GROUP_CH
SSM_STATE = 64
DT_MIN = 0.001
DT_MAX = 0.1
D_FF = 4 * D_MODEL

kernel_name = "hybrid_fnet_gqa_s5_encoder"


def _rmsnorm(x, g):
    x32 = x.astype(jnp.float32)
    y = x32 * lax.rsqrt(jnp.mean(x32 * x32, axis=-1, keepdims=True) + EPS)
    return (y * g.astype(jnp.float32)).astype(x.dtype)


def _axial_angles(seq_len):
    rows = seq_len // GRID_W
    row = jnp.broadcast_to(jnp.arange(rows, dtype=jnp.float32)[:, None], (rows, GRID_W)).reshape(seq_len)
    col = jnp.broadcast_to(jnp.arange(GRID_W, dtype=jnp.float32)[None, :], (rows, GRID_W)).reshape(seq_len)
    inv_freq = ROPE_THETA ** (-jnp.arange(0, ROPE_AXIS_DIM, 2, dtype=jnp.float32) / ROPE_AXIS_DIM)
    ang = jnp.stack([row[:, None] * inv_freq, col[:, None] * inv_freq], axis=1)
    return jnp.cos(ang), jnp.sin(ang)


def _apply_axial_rope(x, cos, sin):
    b, s, h, d = x.shape
    xr = x.astype(jnp.float32).reshape(b, s, h, 2, 2, ROPE_AXIS_DIM // 2)
    x1, x2 = xr[..., 0, :], xr[..., 1, :]
    c = cos[None, :, None]
    sn = sin[None, :, None]
    out = jnp.stack([x1 * c - x2 * sn, x2 * c + x1 * sn], axis=-2)
    return out.reshape(b, s, h, d).astype(x.dtype)


def _fourier_mixer(f, w_fnet):
    b, s, _ = f.shape
    fh = f.astype(jnp.float32).reshape(b, s, FNET_HEADS, FNET_HEAD_DIM)
    spec = jnp.fft.fft2(fh, axes=(1, 3), norm="ortho").real
    out = jnp.einsum('bshc,hcd->bshd', spec, w_fnet.astype(jnp.float32))
    return out.reshape(b, s, MIX_A).astype(f.dtype)


def _gqa_axial(q, k, v, q_norm, k_norm):
    b, s, _ = q.shape
    q = _rmsnorm(q.reshape(b, s, N_Q_HEADS, HEAD_DIM), q_norm)
    k = _rmsnorm(k.reshape(b, s, N_KV_HEADS, HEAD_DIM), k_norm)
    v = v.reshape(b, s, N_KV_HEADS, HEAD_DIM)
    cos, sin = _axial_angles(s)
    q = _apply_axial_rope(q, cos, sin) * (HEAD_DIM ** -0.5)
    k = _apply_axial_rope(k, cos, sin)
    nb = s // Q_BLOCK
    qb = q.reshape(b, nb, Q_BLOCK, N_KV_HEADS, GQA_GROUP, HEAD_DIM).transpose(1, 0, 2, 3, 4, 5)

    def block(qblk):
        sc = jnp.einsum('bqkgd,bskd->bkgqs', qblk, k).astype(jnp.float32)
        p = jax.nn.softmax(sc, axis=-1).astype(v.dtype)
        return jnp.einsum('bkgqs,bskd->bqkgd', p, v)

    o = lax.map(block, qb)
    return o.transpose(1, 0, 2, 3, 4, 5).reshape(b, s, N_Q_HEADS * HEAD_DIM)


def _even_mixer(h, w_in, w_fnet, q_norm, k_norm, w_out):
    z = h @ w_in
    qo = MIX_A
    ko = qo + N_Q_HEADS * HEAD_DIM
    vo = ko + N_KV_HEADS * HEAD_DIM
    fa = _fourier_mixer(z[..., :qo], w_fnet)
    att = _gqa_axial(z[..., qo:ko], z[..., ko:vo], z[..., vo:], q_norm, k_norm)
    return jnp.concatenate([fa, att], axis=-1) @ w_out


def _ssm_combine(e1, e2):
    a1, b1 = e1
    a2, b2 = e2
    return a1 * a2, a2 * b1 + b2


def _s5_mixer(u, lam_re, lam_im, log_dt, b_re, b_im, c_re, c_im, d_skip, w_gate, b_gate):
    f32 = jnp.float32
    b, s, d = u.shape
    u32 = u.astype(f32)
    ug = u32.reshape(b, s, SSM_GROUPS, SSM_GROUP_CH).transpose(1, 0, 2, 3).astype(jnp.complex64)
    y = jnp.zeros((s, b, SSM_GROUPS, SSM_GROUP_CH), f32)
    for direction in range(2):
        lam = lax.complex(lam_re[direction].astype(f32), lam_im[direction].astype(f32))
        dt = jnp.exp(log_dt[direction].astype(f32))[:, None]
        lam_bar = jnp.exp(lam * dt)
        b_mat = lax.complex(b_re[direction].astype(f32), b_im[direction].astype(f32))
        b_bar = ((lam_bar - 1.0) / lam)[..., None] * b_mat
        bu = jnp.einsum('sbgc,gpc->sbgp', ug, b_bar)
        a = jnp.broadcast_to(lam_bar, (s, 1, SSM_GROUPS, SSM_STATE))
        _, states = lax.associative_scan(_ssm_combine, (a, bu), axis=0, reverse=(direction == 1))
        c_mat = lax.complex(c_re[direction].astype(f32), c_im[direction].astype(f32))
        y = y + jnp.einsum('sbgp,gcp->sbgc', states, c_mat).real
    y = y.transpose(1, 0, 2, 3).reshape(b, s, d) + d_skip.astype(f32) * u32
    g = jax.nn.gelu(y)
    out = g * jax.nn.sigmoid(g @ w_gate.astype(f32) + b_gate.astype(f32))
    return out.astype(u.dtype)


def _mlp(h, w1, w2):
    a = jax.nn.relu(h @ w1)
    return (a * a) @ w2


def setup_inputs(seed: int = 0) -> dict:
    key = jax.random.key(seed)
    ks = jax.random.split(key, 24)
    nrm = jax.random.normal
    f32 = jnp.float32
    x = nrm(ks[0], (BATCH, SEQ, D_MODEL), f32)
    norm_mix = 1.0 + 0.02 * nrm(ks[1], (DEPTH, D_MODEL), f32)
    norm_mlp = 1.0 + 0.02 * nrm(ks[2], (DEPTH, D_MODEL), f32)
    mlp_w1 = nrm(ks[3], (DEPTH, D_MODEL, D_FF), f32) * D_MODEL ** -0.5
    mlp_w2 = nrm(ks[4], (DEPTH, D_FF, D_MODEL), f32) * D_FF ** -0.5
    w_in = nrm(ks[5], (N_EVEN, D_MODEL, IN_WIDTH), f32) * D_MODEL ** -0.5
    w_fnet = nrm(ks[6], (N_EVEN, FNET_HEADS, FNET_HEAD_DIM, FNET_HEAD_DIM), f32) * FNET_HEAD_DIM ** -0.5
    q_norm = 1.0 + 0.02 * nrm(ks[7], (N_EVEN, HEAD_DIM), f32)
    k_norm = 1.0 + 0.02 * nrm(ks[8], (N_EVEN, HEAD_DIM), f32)
    w_out = nrm(ks[9], (N_EVEN, D_MODEL, D_MODEL), f32) * D_MODEL ** -0.5
    lam_re = -0.5 + 0.01 * nrm(ks[10], (N_ODD, 2, SSM_GROUPS, SSM_STATE), f32)
    lam_im = jnp.pi * jnp.arange(SSM_STATE, dtype=f32) + 0.01 * nrm(ks[11], (N_ODD, 2, SSM_GROUPS, SSM_STATE), f32)
    log_dt = jax.random.uniform(ks[12], (N_ODD, 2, SSM_GROUPS), f32, math.log(DT_MIN), math.log(DT_MAX))
    b_re = nrm(ks[13], (N_ODD, 2, SSM_GROUPS, SSM_STATE, SSM_GROUP_CH), f32) * (2 * SSM_GROUP_CH) ** -0.5
    b_im = nrm(ks[14], (N_ODD, 2, SSM_GROUPS, SSM_STATE, SSM_GROUP_CH), f32) * (2 * SSM_GROUP_CH) ** -0.5
    c_re = nrm(ks[15], (N_ODD, 2, SSM_GROUPS, SSM_GROUP_CH, SSM_STATE), f32) * SSM_STATE ** -0.5
    c_im = nrm(ks[16], (N_ODD, 2, SSM_GROUPS, SSM_GROUP_CH, SSM_STATE), f32) * SSM_STATE ** -0.5
    d_skip = nrm(ks[17], (N_ODD, D_MODEL), f32)
    w_gate = nrm(ks[18], (N_ODD, D_MODEL, D_MODEL), f32) * D_MODEL ** -0.5
    b_gate = 0.01 * nrm(ks[19], (N_ODD, D_MODEL), f32)
    final_norm = 1.0 + 0.02 * nrm(ks[20], (D_MODEL,), f32)
    return {"x": x, "norm_mix": norm_mix, "norm_mlp": norm_mlp, "mlp_w1": mlp_w1, "mlp_w2": mlp_w2,
            "w_in": w_in, "w_fnet": w_fnet, "q_norm": q_norm, "k_norm": k_norm, "w_out": w_out,
            "lam_re": lam_re, "lam_im": lam_im, "log_dt": log_dt, "b_re": b_re, "b_im": b_im,
            "c_re": c_re, "c_im": c_im, "d_skip": d_skip, "w_gate": w_gate, "b_gate": b_gate,
            "final_norm": final_norm}


def reference(x, norm_mix, norm_mlp, mlp_w1, mlp_w2, w_in, w_fnet, q_norm, k_norm, w_out,
              lam_re, lam_im, log_dt, b_re, b_im, c_re, c_im, d_skip, w_gate, b_gate, final_norm):
    for i in range(DEPTH):
        j = i // 2
        h = _rmsnorm(x, norm_mix[i])
        if i % 2 == 0:
            x = x + _even_mixer(h, w_in[j], w_fnet[j], q_norm[j], k_norm[j], w_out[j])
        else:
            x = x + _s5_mixer(h, lam_re[j], lam_im[j], log_dt[j], b_re[j], b_im[j], c_re[j], c_im[j],
                              d_skip[j], w_gate[j], b_gate[j])
        x = x + _mlp(_rmsnorm(x, norm_mlp[i]), mlp_w1[i], mlp_w2[i])
    return _rmsnorm(x, final_norm)
```

```python
import numpy as np
import ml_dtypes
import concourse.bass as bass
import concourse.mybir as mybir
from concourse.bass_utils import run_bass_kernel_spmd
from contextlib import ExitStack

F32 = mybir.dt.float32
BF16 = mybir.dt.bfloat16
U8 = mybir.dt.uint8
ALU = mybir.AluOpType
AF = mybir.ActivationFunctionType
AX = mybir.AxisListType

S_LEN = 4096
D = 1024
DFF = 4096
EPS = 1e-6
ARENA_BYTES = 206 * 1024


class T:
    __slots__ = ("w", "r")

    def __init__(self):
        self.w = []
        self.r = []


class Sched:
    ENG = ("pe", "act", "dve", "pool", "sp")
    N_DMA_SEMS = 32

    def __init__(self, nc, es):
        self.nc = nc
        self.ops = {e: [] for e in self.ENG}
        self.sem = {e: es.enter_context(nc.semaphore("s_" + e)) for e in ("pe", "act", "dve", "pool")}
        self.dsem = [es.enter_context(nc.semaphore("d%d" % i)) for i in range(self.N_DMA_SEMS)]
        self.dcnt = [0] * self.N_DMA_SEMS
        self.dnext = 0
        self.cnt = {e: 0 for e in self.ENG}
        self.out_tokens = []
        self.barrier_deps = []

    @staticmethod
    def _joinable(t):
        return bool(t.w) and not t.r and all(w[0] == "d" for w in t.w)

    def _deps(self, reads, writes, is_dma=False):
        deps = list(self.barrier_deps)
        for t in reads:
            deps.extend(t.w)
        for t in writes:
            if not (is_dma and self._joinable(t)):
                deps.extend(t.w)
            deps.extend(t.r)
        return deps

    def _mark(self, tok, reads, writes, is_dma=False):
        for t in reads:
            t.r.append(tok)
            if len(t.r) > 64:
                t.r = self._compact(t.r)
        for t in writes:
            if is_dma and self._joinable(t):
                t.w = t.w + [tok]
            else:
                t.w = [tok]
            t.r = []

    @staticmethod
    def _compact(toks):
        best = {}
        for d in toks:
            k = (d[0], d[1])
            if k not in best or best[k][2] < d[2]:
                best[k] = d
        return list(best.values())

    def op(self, eng, fn, reads=(), writes=()):
        deps = self._deps(reads, writes)
        self.cnt[eng] += 1
        tok = ("c", eng, self.cnt[eng])
        self.ops[eng].append((fn, deps, tok))
        self._mark(tok, reads, writes)
        return tok

    def dma(self, q, fn, reads=(), writes=(), is_out=False):
        deps = self._deps(reads, writes, True)
        k = self.dnext
        self.dnext = (self.dnext + 1) % self.N_DMA_SEMS
        if self.dcnt[k]:
            deps.append(("d", k, self.dcnt[k]))
        self.dcnt[k] += 16
        tok = ("d", k, self.dcnt[k])
        self.ops[q].append((fn, deps, tok))
        self._mark(tok, reads, writes, True)
        if is_out:
            self.out_tokens.append(tok)
        return tok

    def barrier(self):
        deps = [("c", e, self.cnt[e]) for e in ("pe", "act", "dve", "pool") if self.cnt[e]]
        deps += [("d", k, self.dcnt[k]) for k in range(self.N_DMA_SEMS) if self.dcnt[k]]
        self.barrier_deps = deps

    def emit(self):
        nc = self.nc
        final_deps = list(self.out_tokens)
        with nc.Block() as block:
            def run(engname, e):
                waited = {}

                def do_wait(tok):
                    if tok[0] == "c":
                        if tok[1] == engname and engname == "pe":
                            return
                        key = ("c", tok[1]); sem = self.sem[tok[1]]
                    else:
                        key = ("d", tok[1]); sem = self.dsem[tok[1]]
                    if waited.get(key, 0) >= tok[2]:
                        return
                    waited[key] = tok[2]
                    e.wait_ge(sem, tok[2])

                for fn, deps, tok in self.ops[engname]:
                    for d in self._compact(deps):
                        do_wait(d)
                    ins = fn(e)
                    if tok[0] == "c":
                        ins.then_inc(self.sem[tok[1]], 1)
                    else:
                        ins.then_inc(self.dsem[tok[1]], 16)
                if engname == "sp":
                    for d in final_deps:
                        do_wait(d)

            @block.sync
            def _(e):
                run("sp", e)

            @block.tensor
            def _(e):
                run("pe", e)

            @block.scalar
            def _(e):
                run("act", e)

            @block.vector
            def _(e):
                run("dve", e)

            @block.gpsimd
            def _(e):
                run("pool", e)


class Arena:
    def __init__(self, ap):
        self.ap = ap
        self.off = 0
        self.top = ARENA_BYTES

    def alloc(self, shape, dt, top=False):
        esz = 2 if dt == BF16 else 4
        n = int(np.prod(shape[1:]))
        nb = n * esz
        nba = (nb + 63) // 64 * 64
        assert self.off + nba <= self.top, ("SBUF arena overflow", self.off, self.top, nb)
        if top:
            self.top -= nba
            v = self.ap[:, self.top:self.top + nb].bitcast(dt)
        else:
            v = self.ap[:, self.off:self.off + nb].bitcast(dt)
            self.off += nba
        if len(shape) > 2:
            names = " ".join("a%d" % i for i in range(len(shape) - 1))
            kw = {"a%d" % i: shape[i + 1] for i in range(len(shape) - 2)}
            v = v.rearrange("p (%s) -> p %s" % (names, names), **kw)
        return v

    def mark(self):
        return self.off

    def release(self, m):
        self.off = m

    def release_top(self):
        self.top = ARENA_BYTES


class Ctx:
    pass


def dump(K, name, ap, reads):
    if not getattr(K, "dbg", False):
        return
    t = K.nc.dram_tensor("dbg_" + name, list(ap.shape), ap.dtype, kind="ExternalOutput").ap()
    K.S.dma("sp", lambda e: e.dma_start(out=t, in_=ap), reads=reads, is_out=True)


def psum_bf16(ps_ap):
    return ps_ap.bitcast(BF16)


def emit_rmsnorm_stats(K, xt_ap, nsub, ss, rstd, t_x, t_ss, junk):
    S = K.S
    for s in range(nsub):
        S.op("act", lambda e, s=s: e.activation(out=junk, in_=xt_ap[:, s, :], func=AF.Square, accum_out=ss[:, s:s + 1]),
             reads=[t_x], writes=[t_ss])
    S.op("dve", lambda e: e.tensor_scalar(out=rstd[:, 0:nsub], in0=ss[:, 0:nsub], scalar1=1.0 / D, scalar2=EPS, op0=ALU.mult, op1=ALU.add),
         reads=[t_ss], writes=[t_ss])
    S.op("act", lambda e: e.activation(out=rstd[:, 0:nsub], in_=rstd[:, 0:nsub], func=AF.Sqrt), reads=[t_ss], writes=[t_ss])
    S.op("dve", lambda e: e.reciprocal(out=rstd[:, 0:nsub], in_=rstd[:, 0:nsub]), reads=[t_ss], writes=[t_ss])


def load_bcast_row(K, q, dst, src_row, t_dst):
    n = dst.shape[1]
    K.S.dma(q, lambda e: e.dma_start(out=dst, in_=src_row.unsqueeze(0).to_broadcast([128, n])), writes=[t_dst])


def phase_mlp(K, xin, xout, w1, w2, gain, final_gain=None):
    S, A, nc = K.S, K.A, K.nc
    m0 = A.mark()
    TT = 256
    NT = S_LEN // TT
    W1b = A.alloc([128, 8, DFF], BF16)
    W2b = A.alloc([128, 32, D], BF16)
    stg = [A.alloc([128, 2048], F32) for _ in range(2)]
    xt = [A.alloc([128, 2, D], F32) for _ in range(2)]
    hb = A.alloc([128, 2, D], BF16)
    hT = [A.alloc([128, 8, TT], BF16) for _ in range(2)]
    a2T = A.alloc([128, 32, TT], BF16)
    rt = [A.alloc([128, TT], F32) for _ in range(2)]
    gt = A.alloc([128, D], F32)
    gft = A.alloc([128, D], F32) if final_gain is not None else None
    junk = A.alloc([128, D], BF16)
    ss = A.alloc([128, 4], F32)
    rstd = A.alloc([128, 4], F32)
    t_stg = [T(), T()]
    t_w1 = [T() for _ in range(16)]
    t_w2 = [T() for _ in range(16)]
    t_xt = [T(), T()]
    t_hb, t_hT, t_a2 = T(), [T(), T()], [T() for _ in range(32)]
    t_rt = [T(), T()]
    t_g, t_ss, t_junk = T(), T(), T()
    t_psT, t_psA, t_psB = [T(), T()], [T(), T()], [T(), T()]
    ps = K.ps
    psT = [psum_bf16(ps[:, 0:512]).rearrange("p (a b) -> p a b", a=4), psum_bf16(ps[:, 512:1024]).rearrange("p (a b) -> p a b", a=4)]
    psA = [ps[:, 1024:1024 + TT], ps[:, 1536:1536 + TT]]
    psB = [ps[:, 2048:2560], ps[:, 2560:3072]]

    load_bcast_row(K, "sp", gt, gain, t_g)
    if gft is not None:
        load_bcast_row(K, "sp", gft, final_gain, t_g)

    w1v = w1.rearrange("(dc p) f -> p dc f", p=128)
    w2v = w2.rearrange("(fc p) d -> p fc d", p=128)
    nst = [0]

    def load_w1(c):
        b = nst[0] % 2; nst[0] += 1
        sv = stg[b].rearrange("p (dc f) -> p dc f", dc=8)
        S.dma("pool", lambda e: e.dma_start(out=sv, in_=w1v[:, :, c * 256:(c + 1) * 256]), writes=[t_stg[b]])
        S.op("pool", lambda e: e.tensor_copy(out=W1b[:, :, c * 256:(c + 1) * 256], in_=sv), reads=[t_stg[b]], writes=[t_w1[c]])

    def load_w2(c):
        b = nst[0] % 2; nst[0] += 1
        sv = stg[b].rearrange("p (fc d) -> p fc d", fc=2)
        S.dma("pool", lambda e: e.dma_start(out=sv, in_=w2v[:, 2 * c:2 * c + 2, :]), writes=[t_stg[b]])
        S.op("pool", lambda e: e.tensor_copy(out=W2b[:, 2 * c:2 * c + 2, :], in_=sv), reads=[t_stg[b]], writes=[t_w2[c]])

    for c in range(16):
        load_w1(c)
    for c in range(16):
        load_w2(c)

    xin_v = xin.rearrange("(n s p) d -> n p s d", s=2, p=128)
    xout_v = xout.rearrange("(n s p) d -> n p s d", s=2, p=128)

    def load_x(n):
        b = n % 2
        S.dma("sp", lambda e: e.dma_start(out=xt[b], in_=xin_v[n]), writes=[t_xt[b]])

    load_x(0)
    load_x(1)
    def do_tile(n):
        b = n % 2
        x_ = xt[b]
        emit_rmsnorm_stats(K, x_, 2, ss, rstd, t_xt[b], t_ss, junk)
        for s in range(2):
            S.op("dve", lambda e, s=s: e.scalar_tensor_tensor(out=hb[:, s, :], in0=x_[:, s, :], scalar=rstd[:, s:s + 1], in1=gt,
                                                               op0=ALU.mult, op1=ALU.mult), reads=[t_xt[b], t_ss, t_g], writes=[t_hb])
        for half in range(2):
            for dcl in range(4):
                dc = half * 4 + dcl
                for s in range(2):
                    S.op("pe", lambda e, half=half, dcl=dcl, dc=dc, s=s: e.transpose(
                        psT[half][:, dcl, s * 128:(s + 1) * 128], hb[:, s, dc * 128:(dc + 1) * 128], K.identb),
                        reads=[t_hb, K.t_const], writes=[t_psT[half]])
            eng = "act" if half == 0 else "dve"
            if eng == "act":
                S.op("act", lambda e, half=half: e.activation(out=hT[b][:, half * 4:half * 4 + 4, :], in_=psT[half], func=AF.Copy),
                     reads=[t_psT[half]], writes=[t_hT[b]])
            else:
                S.op("dve", lambda e, half=half: e.tensor_copy(out=hT[b][:, half * 4:half * 4 + 4, :], in_=psT[half]),
                     reads=[t_psT[half]], writes=[t_hT[b]])
        for fc in range(32):
            pb = fc % 2
            for dc in range(8):
                S.op("pe", lambda e, fc=fc, dc=dc, pb=pb: e.matmul(psA[pb], W1b[:, dc, fc * 128:(fc + 1) * 128], hT[b][:, dc, :],
                                                                  start=(dc == 0), stop=(dc == 7)),
                     reads=[t_w1[fc // 2], t_hT[b]], writes=[t_psA[pb]])
            S.op("act", lambda e, pb=pb: e.activation(out=rt[pb], in_=psA[pb], func=AF.Relu), reads=[t_psA[pb]], writes=[t_rt[pb]])
            S.op("dve", lambda e, fc=fc, pb=pb: e.tensor_tensor(out=a2T[:, fc, :], in0=rt[pb], in1=rt[pb], op=ALU.mult),
                 reads=[t_rt[pb]], writes=[t_a2[fc]])
        k = 0
        for s in range(2):
            for dh in range(2):
                pb = k % 2; k += 1
                for fc in range(32):
                    S.op("pe", lambda e, fc=fc, s=s, dh=dh, pb=pb: e.matmul(psB[pb], a2T[:, fc, s * 128:(s + 1) * 128],
                                                                          W2b[:, fc, dh * 512:(dh + 1) * 512], start=(fc == 0), stop=(fc == 31)),
                         reads=[t_a2[fc], t_w2[fc // 2]], writes=[t_psB[pb]])
                S.op("dve", lambda e, s=s, dh=dh, pb=pb: e.tensor_tensor(out=x_[:, s, dh * 512:(dh + 1) * 512], in0=psB[pb],
                                                                       in1=x_[:, s, dh * 512:(dh + 1) * 512], op=ALU.add),
                     reads=[t_psB[pb]], writes=[t_xt[b]])
        if final_gain is not None:
            emit_rmsnorm_stats(K, x_, 2, ss[:, 2:4], rstd[:, 2:4], t_xt[b], t_ss, junk)
            for s in range(2):
                S.op("dve", lambda e, s=s: e.scalar_tensor_tensor(out=x_[:, s, :], in0=x_[:, s, :], scalar=rstd[:, 2 + s:3 + s], in1=gft,
                                                                   op0=ALU.mult, op1=ALU.mult), reads=[t_ss, t_g], writes=[t_xt[b]])
        S.dma("sp", lambda e, n=n: e.dma_start(out=xout_v[n], in_=x_), reads=[t_xt[b]], is_out=K.is_out(xout))
        if n + 2 < NT:
            load_x(n + 2)

    for n in range(NT):
        do_tile(n)
    S.barrier()
    A.release(m0)


def phase_mix0(K, xin, xout, P):
    S, A, ps = K.S, K.A, K.ps
    m_phase = A.mark()
    PS = lambda b: ps[:, b * 512:(b + 1) * 512]
    PSB = lambda b: psum_bf16(ps[:, b * 512:(b + 1) * 512])
    t_bank = [T() for _ in range(8)]
    qT = A.alloc([128, 4, S_LEN], BF16)
    kT = A.alloc([128, S_LEN], BF16)
    Vaug = A.alloc([128, 32, 2, 128], BF16)
    t_qT, t_kT, t_V = T(), T(), T()
    Wq = A.alloc([128, 8, 512], BF16)
    Wkv = A.alloc([128, 8, 256], BF16)
    WY = [A.alloc([128, 8, 512], BF16) for _ in range(2)]
    gqk = A.alloc([128, 2], F32)
    cm = {}
    for nm in ("rmat", "onesblk", "bdc", "bds", "bdsn", "c64bd", "s64bdn"):
        cm[nm] = A.alloc([128, 128], BF16)
    t_w = T()
    for nm in cm:
        S.dma("sp", lambda e, nm=nm: e.dma_start(out=cm[nm], in_=P[nm]), writes=[t_w])
    for col, src in ((0, P["q_norm"]), (1, P["k_norm"])):
        for hh in range(2):
            S.dma("sp", lambda e, col=col, src=src, hh=hh: e.dma_start(out=gqk[hh * 64:(hh + 1) * 64, col:col + 1], in_=src.unsqueeze(1)),
                  writes=[t_w])
    S.op("pool", lambda e: e.memset(Vaug[:, :, 0, 64:128], 1.0), writes=[t_V])
    S.op("pool", lambda e: e.memset(Vaug[:, :, 1, 0:64], 1.0), writes=[t_V])
    m_prep = A.mark()

    stg = A.alloc([128, 8, 512], F32)
    WAb = A.alloc([128, 8, 512], BF16)
    WAT = A.alloc([128, 4, D], BF16)
    Wf2 = A.alloc([128, 4, 128], F32)
    Wf2b = A.alloc([128, 4, 128], BF16)
    BDG = [A.alloc([128, 4, 128], BF16) for _ in range(2)]
    t_stg, t_wab, t_wat, t_wf, t_bdg = T(), T(), T(), T(), T()
    w_in = P["w_in"]
    for c in range(4):
        for hh in range(2):
            col = 512 + 64 * (c + 4 * hh)
            S.dma("sp", lambda e, c=c, hh=hh, col=col: e.dma_start(out=stg[:, :, c * 128 + hh * 64:c * 128 + hh * 64 + 64],
                                                                   in_=w_in[:, col:col + 64].rearrange("(dc p) d -> p dc d", p=128)),
                  writes=[t_stg])
    S.op("dve", lambda e: e.tensor_copy(out=Wq, in_=stg), reads=[t_stg], writes=[t_w])
    S.dma("sp", lambda e: e.dma_start(out=stg[:, :, 0:256], in_=w_in[:, 1024:1280].rearrange("(dc p) f -> p dc f", p=128)),
          writes=[t_stg])
    S.op("dve", lambda e: e.tensor_copy(out=Wkv, in_=stg[:, :, 0:256]), reads=[t_stg], writes=[t_w])
    S.dma("sp", lambda e: e.dma_start(out=stg, in_=w_in[:, 0:512].rearrange("(dc p) f -> p dc f", p=128)), writes=[t_stg])
    S.op("dve", lambda e: e.tensor_copy(out=WAb, in_=stg), reads=[t_stg], writes=[t_wab])
    for cc in range(4):
        bk = cc % 2
        for dc in range(8):
            S.op("pe", lambda e, cc=cc, dc=dc, bk=bk: e.transpose(PSB(bk)[:, dc * 128:(dc + 1) * 128], WAb[:, dc, cc * 128:(cc + 1) * 128], K.identb),
                 reads=[t_wab, K.t_const], writes=[t_bank[bk]])
        S.op("act", lambda e, cc=cc, bk=bk: e.activation(out=WAT[:, cc, :], in_=PSB(bk), func=AF.Copy), reads=[t_bank[bk]], writes=[t_wat])
    wf = P["w_fnet"].rearrange("(ch hh) c d -> hh c ch d", hh=2)
    S.op("pool", lambda e: e.memset(Wf2, 0.0), writes=[t_wf])
    for hh in range(2):
        S.dma("sp", lambda e, hh=hh: e.dma_start(out=Wf2[hh * 64:(hh + 1) * 64, :, hh * 64:(hh + 1) * 64], in_=wf[hh]), writes=[t_wf])
    S.op("pool", lambda e: e.tensor_copy(out=Wf2b, in_=Wf2), reads=[t_wf], writes=[t_wf])
    for gi, nm in enumerate(("c64bd", "s64bdn")):
        bk = 2 + gi
        for ch in range(4):
            S.op("pe", lambda e, ch=ch, bk=bk, nm=nm: e.matmul(PS(bk)[:, ch * 128:(ch + 1) * 128], cm[nm], Wf2b[:, ch, :], start=True, stop=True),
                 reads=[t_w, t_wf], writes=[t_bank[bk]])
        S.op("dve", lambda e, gi=gi, bk=bk: e.tensor_copy(out=BDG[gi].rearrange("p a b -> p (a b)"), in_=PS(bk)), reads=[t_bank[bk]], writes=[t_bdg])
    k = 0
    for gi in range(2):
        for dc in range(8):
            bk = 4 + (k % 2); k += 1
            for ch in range(4):
                S.op("pe", lambda e, gi=gi, dc=dc, ch=ch, bk=bk: e.matmul(PS(bk)[:, ch * 128:(ch + 1) * 128], WAT[:, ch, dc * 128:(dc + 1) * 128],
                                                                       BDG[gi][:, ch, :], start=True, stop=True),
                     reads=[t_wat, t_bdg], writes=[t_bank[bk]])
            eng = "act" if k % 2 else "dve"
            if eng == "act":
                S.op("act", lambda e, gi=gi, dc=dc, bk=bk: e.activation(out=WY[gi][:, dc, :], in_=PS(bk), func=AF.Copy), reads=[t_bank[bk]], writes=[t_w])
            else:
                S.op("dve", lambda e, gi=gi, dc=dc, bk=bk: e.tensor_copy(out=WY[gi][:, dc, :], in_=PS(bk)), reads=[t_bank[bk]], writes=[t_w])
    S.barrier()
    A.release(m_prep)

    xt = [A.alloc([128, 2, D], F32) for _ in range(2)]
    hb = A.alloc([128, 2, D], BF16)
    hT = [A.alloc([128, 8, 512], BF16) for _ in range(2)]
    gt = A.alloc([128, D], F32)
    junk = A.alloc([128, D], BF16)
    ss = A.alloc([128, 4], F32); rstd = A.alloc([128, 4], F32)
    cst = A.alloc([128, 2, 512], F32)
    sq = [A.alloc([128, 512], BF16) for _ in range(2)]
    rq = [A.alloc([128, 512], F32) for _ in range(2)]
    qn = [A.alloc([128, 512], F32) for _ in range(2)]
    qnb = [A.alloc([128, 512], BF16) for _ in range(2)]
    Ys = [[A.alloc([128, 512], BF16) for _ in range(2)] for _ in range(2)]
    PQs = [A.alloc([128, 2, 512], BF16) for _ in range(2)]
    t_xt, t_hb, t_hT, t_g, t_ss, t_cs = [T(), T()], T(), [T(), T()], T(), T(), T()
    t_sq, t_rq, t_qn, t_qnb = [T(), T()], [T(), T()], [T(), T()], [T(), T()]
    t_Ys, t_PQs, t_pqh = [[T(), T()], [T(), T()]], [T(), T()], T()
    load_bcast_row(K, "sp", gt, P["norm_mix0"], t_g)
    xv = xin.rearrange("(a j b4) d -> j b4 a d", j=32, b4=4)
    xov = xout.rearrange("(a j b4) d -> j b4 a d", j=32, b4=4)
    pqh = P["pqh"]
    pqh_w = pqh.rearrange("(j p) q f -> j p q f", p=128)

    def load_x(h):
        b = h % 2
        for s in range(2):
            for b4 in range(4):
                S.dma("sp", lambda e, s=s, b4=b4: e.dma_start(out=xt[b][b4 * 32:(b4 + 1) * 32, s, :], in_=xv[2 * h + s, b4]), writes=[t_xt[b]])

    def norm_T(h):
        b = h % 2
        st, off = h // 2, (h % 2) * 256
        emit_rmsnorm_stats(K, xt[b], 2, ss, rstd, t_xt[b], t_ss, junk)
        for s in range(2):
            S.op("dve", lambda e, s=s: e.scalar_tensor_tensor(out=hb[:, s, :], in0=xt[b][:, s, :], scalar=rstd[:, s:s + 1], in1=gt,
                                                               op0=ALU.mult, op1=ALU.mult), reads=[t_xt[b], t_ss, t_g], writes=[t_hb])
        hTs = hT[st % 2]
        for half in range(2):
            pv = PSB(half).rearrange("p (a b) -> p a b", a=4)
            for dcl in range(4):
                dc = half * 4 + dcl
                for s in range(2):
                    S.op("pe", lambda e, pv=pv, dcl=dcl, dc=dc, s=s: e.transpose(pv[:, dcl, s * 128:(s + 1) * 128], hb[:, s, dc * 128:(dc + 1) * 128], K.identb),
                         reads=[t_hb, K.t_const], writes=[t_bank[half]])
            if half == 0:
                S.op("act", lambda e, pv=pv: e.activation(out=hTs[:, 0:4, off:off + 256], in_=pv, func=AF.Copy), reads=[t_bank[0]], writes=[t_hT[st % 2]])
            else:
                S.op("dve", lambda e, pv=pv: e.tensor_copy(out=hTs[:, 4:8, off:off + 256], in_=pv), reads=[t_bank[1]], writes=[t_hT[st % 2]])

    def supertile(st):
        hTs, thT = hT[st % 2], t_hT[st % 2]
        n0 = st * 512
        S.dma("sp", lambda e: e.dma_start(out=cst[:, 0, :], in_=P["cosp"][:, n0:n0 + 512]), writes=[t_cs])
        S.dma("sp", lambda e: e.dma_start(out=cst[:, 1, :], in_=P["sinp"][:, n0:n0 + 512]), writes=[t_cs])

        def step1(i):
            bk = 2 + i % 2
            for dc in range(8):
                lhsT = Wq[:, dc, i * 128:(i + 1) * 128] if i < 4 else Wkv[:, dc, 0:128]
                S.op("pe", lambda e, lhsT=lhsT, dc=dc, bk=bk: e.matmul(PS(bk), lhsT, hTs[:, dc, :], start=(dc == 0), stop=(dc == 7)),
                     reads=[t_w, thT], writes=[t_bank[bk]])
            S.op("act", lambda e, bk=bk, i=i: e.activation(out=sq[i % 2], in_=PS(bk), func=AF.Square), reads=[t_bank[bk]], writes=[t_sq[i % 2]])

        def step2(i):
            bk = 2 + i % 2
            j = i % 2
            S.op("pe", lambda e: e.matmul(PS(4), cm["onesblk"], sq[j], start=True, stop=True), reads=[t_sq[j], t_w], writes=[t_bank[4]])
            S.op("act", lambda e: e.activation(out=rq[j], in_=PS(4), func=AF.Sqrt, bias=K.eps_ap, scale=1.0), reads=[t_bank[4], K.t_const], writes=[t_rq[j]])
            S.op("dve", lambda e: e.reciprocal(out=rq[j], in_=rq[j]), reads=[t_rq[j]], writes=[t_rq[j]])
            gcol = gqk[:, 0:1] if i < 4 else gqk[:, 1:2]
            S.op("dve", lambda e: e.scalar_tensor_tensor(out=qn[j], in0=PS(bk), scalar=gcol, in1=rq[j], op0=ALU.mult, op1=ALU.mult),
                 reads=[t_bank[bk], t_rq[j], t_w], writes=[t_qn[j]])
            S.op("pool", lambda e: e.tensor_copy(out=qnb[j], in_=qn[j]), reads=[t_qn[j]], writes=[t_qnb[j]])
            S.op("dve", lambda e: e.tensor_tensor(out=qn[j], in0=qn[j], in1=cst[:, 0, :], op=ALU.mult), reads=[t_cs, t_qnb[j]], writes=[t_qn[j]])

        def step3(i):
            j = i % 2
            S.op("pe", lambda e: e.matmul(PS(5), cm["rmat"], qnb[j], start=True, stop=True), reads=[t_qnb[j], t_w], writes=[t_bank[5]])
            S.op("dve", lambda e: e.tensor_tensor(out=rq[j], in0=PS(5), in1=cst[:, 1, :], op=ALU.mult), reads=[t_bank[5], t_cs], writes=[t_rq[j]])
            dst = qT[:, i, n0:n0 + 512] if i < 4 else kT[:, n0:n0 + 512]
            S.op("pool", lambda e: e.tensor_tensor(out=dst, in0=qn[j], in1=rq[j], op=ALU.add), reads=[t_qn[j], t_rq[j]],
                 writes=[t_qT if i < 4 else t_kT])

        for kk in range(7):
            if kk < 5:
                step1(kk)
            if 0 <= kk - 1 < 5:
                step2(kk - 1)
            if 0 <= kk - 2 < 5:
                step3(kk - 2)
        for tl in range(4):
            jt = st * 4 + tl
            for dc in range(8):
                S.op("pe", lambda e, dc=dc, tl=tl: e.matmul(PS(6)[:, 0:128], hTs[:, dc, tl * 128:(tl + 1) * 128], Wkv[:, dc, 128:256],
                                                           start=(dc == 0), stop=(dc == 7)), reads=[t_w, thT], writes=[t_bank[6]])
            S.op("act", lambda e, jt=jt: e.activation(out=Vaug[:, jt, 0, 0:64], in_=PS(6)[:, 0:64], func=AF.Copy), reads=[t_bank[6]], writes=[t_V])
            S.op("act", lambda e, jt=jt: e.activation(out=Vaug[:, jt, 1, 64:128], in_=PS(6)[:, 64:128], func=AF.Copy), reads=[t_bank[6]], writes=[t_V])
        for tl in range(4):
            jt = st * 4 + tl
            r = jt % 2
            for gi in range(2):
                bk = 6 + gi
                for dc in range(8):
                    S.op("pe", lambda e, gi=gi, dc=dc, tl=tl, bk=bk: e.matmul(PS(bk), hTs[:, dc, tl * 128:(tl + 1) * 128], WY[gi][:, dc, :],
                                                                          start=(dc == 0), stop=(dc == 7)), reads=[t_w, thT], writes=[t_bank[bk]])
                if gi == 0:
                    S.op("act", lambda e, r=r, bk=bk: e.activation(out=Ys[0][r], in_=PS(bk), func=AF.Copy), reads=[t_bank[bk]], writes=[t_Ys[0][r]])
                else:
                    S.op("dve", lambda e, r=r, bk=bk: e.tensor_copy(out=Ys[1][r], in_=PS(bk)), reads=[t_bank[bk]], writes=[t_Ys[1][r]])
            S.op("pe", lambda e, r=r: e.matmul(PS(4), cm["bdc"], Ys[0][r], start=True, stop=False), reads=[t_w, t_Ys[0][r]], writes=[t_bank[4]])
            S.op("pe", lambda e, r=r: e.matmul(PS(4), cm["bds"], Ys[1][r], start=False, stop=True), reads=[t_w, t_Ys[1][r]], writes=[t_bank[4]])
            S.op("pe", lambda e, r=r: e.matmul(PS(5), cm["bdc"], Ys[1][r], start=True, stop=False), reads=[t_w, t_Ys[1][r]], writes=[t_bank[5]])
            S.op("pe", lambda e, r=r: e.matmul(PS(5), cm["bdsn"], Ys[0][r], start=False, stop=True), reads=[t_w, t_Ys[0][r]], writes=[t_bank[5]])
            S.op("act", lambda e, r=r: e.activation(out=PQs[r][:, 0, :], in_=PS(4), func=AF.Copy), reads=[t_bank[4]], writes=[t_PQs[r]])
            S.op("dve", lambda e, r=r: e.tensor_copy(out=PQs[r][:, 1, :], in_=PS(5)), reads=[t_bank[5]], writes=[t_PQs[r]])
            S.dma("sp", lambda e, r=r, jt=jt: e.dma_start(out=pqh_w[jt], in_=PQs[r]), reads=[t_PQs[r]], writes=[t_pqh])

    load_x(0)
    load_x(1)
    for st in range(8):
        norm_T(2 * st)
        if 2 * st + 2 < 16:
            load_x(2 * st + 2)
        norm_T(2 * st + 1)
        if 2 * st + 3 < 16:
            load_x(2 * st + 3)
        supertile(st)
    S.barrier()
    A.release(m_prep)

    faT = A.alloc([128, 4, S_LEN], BF16)
    attT = A.alloc([128, 4, S_LEN], BF16)
    t_fa, t_att = T(), T()
    m_C = A.mark()
    CB = A.alloc([128, S_LEN], BF16); SB = A.alloc([128, S_LEN], BF16)
    Pp = [A.alloc([128, 4, 2, 512], BF16) for _ in range(2)]
    t_cb, t_Pp = T(), [T(), T()]
    S.dma("sp", lambda e: e.dma_start(out=CB, in_=P["cbt"]), writes=[t_cb])
    S.dma("sp", lambda e: e.dma_start(out=SB, in_=P["sbt"]), writes=[t_cb])
    pqh_r = pqh.rearrange("(b tau) q f -> b tau q f", tau=32)
    k = 0
    for tg in range(8):
        r = tg % 2
        S.dma("sp", lambda e, tg=tg, r=r: e.dma_start(out=Pp[r], in_=pqh_r[:, 4 * tg:4 * tg + 4]), reads=[t_pqh], writes=[t_Pp[r]])
        for fc in range(4):
            bk = k % 2; k += 1
            for tl in range(4):
                tau = 4 * tg + tl
                S.op("pe", lambda e, r=r, fc=fc, tl=tl, tau=tau, bk=bk: e.matmul(PS(bk)[:, tl * 128:(tl + 1) * 128], Pp[r][:, tl, 0, fc * 128:(fc + 1) * 128],
                                                                             CB[:, tau::32], start=True, stop=False), reads=[t_Pp[r], t_cb], writes=[t_bank[bk]])
                S.op("pe", lambda e, r=r, fc=fc, tl=tl, tau=tau, bk=bk: e.matmul(PS(bk)[:, tl * 128:(tl + 1) * 128], Pp[r][:, tl, 1, fc * 128:(fc + 1) * 128],
                                                                             SB[:, tau::32], start=False, stop=True), reads=[t_Pp[r], t_cb], writes=[t_bank[bk]])
            src = PS(bk).rearrange("p (tl mq mr) -> p tl mr mq", tl=4, mq=32, mr=4)
            dst = faT[:, fc, :].rearrange("p (mr tau mq) -> p tau mr mq", mr=4, tau=32, mq=32)[:, 4 * tg:4 * tg + 4]
            if k % 2:
                S.op("act", lambda e, src=src, dst=dst: e.activation(out=dst, in_=src, func=AF.Copy), reads=[t_bank[bk]], writes=[t_fa])
            else:
                S.op("dve", lambda e, src=src, dst=dst: e.tensor_copy(out=dst, in_=src), reads=[t_bank[bk]], writes=[t_fa])
    S.barrier()
    A.release(m_C)

    pT = [A.alloc([128, 512], BF16) for _ in range(3)]
    rec = [A.alloc([128, 512], F32) for _ in range(2)]
    t_pT, t_rec = [T(), T(), T()], [T(), T()]
    it = 0
    for c in range(4):
        for hh in range(2):
            lo, hi = hh * 64, hh * 64 + 64
            for qt in range(8):
                ob = 2 + it % 2
                def score(kt, c=c, qt=qt, lo=lo, hi=hi):
                    sbk = kt % 2
                    S.op("pe", lambda e: e.matmul(PS(sbk), kT[lo:hi, kt * 128:(kt + 1) * 128], qT[lo:hi, c, qt * 512:(qt + 1) * 512], start=True, stop=True),
                         reads=[t_kT, t_qT], writes=[t_bank[sbk]])
                score(0)
                for kt in range(32):
                    sbk = kt % 2
                    pb = kt % 3
                    if kt + 1 < 32:
                        score(kt + 1)
                    S.op("act", lambda e, sbk=sbk, pb=pb: e.activation(out=pT[pb], in_=PS(sbk), func=AF.Exp, scale=0.125),
                         reads=[t_bank[sbk]], writes=[t_pT[pb]])
                    S.op("pe", lambda e, kt=kt, pb=pb, hh=hh, ob=ob: e.matmul(PS(ob), Vaug[:, kt, hh, :], pT[pb], start=(kt == 0), stop=(kt == 31)),
                         reads=[t_V, t_pT[pb]], writes=[t_bank[ob]])
                rb = it % 2
                dlo, dhi = (64, 128) if hh == 0 else (0, 64)
                S.op("dve", lambda e, ob=ob, rb=rb, lo=lo, hi=hi, dlo=dlo, dhi=dhi: e.reciprocal(out=rec[rb][lo:hi, :], in_=PS(ob)[dlo:dhi, :]),
                     reads=[t_bank[ob]], writes=[t_rec[rb]])
                S.op("dve", lambda e, ob=ob, rb=rb, lo=lo, hi=hi, c=c, qt=qt: e.tensor_tensor(out=attT[lo:hi, c, qt * 512:(qt + 1) * 512], in0=PS(ob)[lo:hi, :],
                                                                                            in1=rec[rb][lo:hi, :], op=ALU.mult),
                     reads=[t_bank[ob], t_rec[rb]], writes=[t_att])
                it += 1
    S.barrier()
    A.release(m_C)

    Wo = A.alloc([128, 8, D], BF16)
    stg2 = A.alloc([128, 4, D], F32)
    xt2 = [A.alloc([128, D], F32) for _ in range(3)]
    t_wo, t_stg2, t_xt2 = T(), T(), [T(), T(), T()]
    w_out = P["w_out"]
    S.dma("sp", lambda e: e.dma_start(out=stg2, in_=w_out[0:512, :].rearrange("(fc p) n -> p fc n", p=128)), writes=[t_stg2])
    S.op("pool", lambda e: e.tensor_copy(out=Wo[:, 0:4, :], in_=stg2), reads=[t_stg2], writes=[t_wo])
    wo_att = w_out[512:1024, :].rearrange("(hh c d) n -> hh d c n", hh=2, c=4)
    for hh in range(2):
        S.dma("sp", lambda e, hh=hh: e.dma_start(out=stg2[hh * 64:(hh + 1) * 64, :, :], in_=wo_att[hh]), writes=[t_stg2])
    S.op("pool", lambda e: e.tensor_copy(out=Wo[:, 4:8, :], in_=stg2), reads=[t_stg2], writes=[t_wo])

    def load_x2(jt):
        b = jt % 3
        for b4 in range(4):
            S.dma("sp", lambda e, b4=b4: e.dma_start(out=xt2[b][b4 * 32:(b4 + 1) * 32, :], in_=xv[jt, b4]), writes=[t_xt2[b]])

    load_x2(0)
    load_x2(1)
    k = 0
    for jt in range(32):
        b = jt % 3
        if jt + 2 < 32:
            load_x2(jt + 2)
        for dh in range(2):
            bk = k % 2; k += 1
            for fc in range(8):
                lhsT = faT[:, fc, jt * 128:(jt + 1) * 128] if fc < 4 else attT[:, fc - 4, jt * 128:(jt + 1) * 128]
                S.op("pe", lambda e, lhsT=lhsT, fc=fc, dh=dh, bk=bk: e.matmul(PS(bk), lhsT, Wo[:, fc, dh * 512:(dh + 1) * 512], start=(fc == 0), stop=(fc == 7)),
                     reads=[t_fa, t_att, t_wo], writes=[t_bank[bk]])
            S.op("dve", lambda e, b=b, dh=dh, bk=bk: e.tensor_tensor(out=xt2[b][:, dh * 512:(dh + 1) * 512], in0=PS(bk), in1=xt2[b][:, dh * 512:(dh + 1) * 512], op=ALU.add),
                 reads=[t_bank[bk]], writes=[t_xt2[b]])
        for b4 in range(4):
            S.dma("sp", lambda e, b=b, b4=b4, jt=jt: e.dma_start(out=xov[jt, b4], in_=xt2[b][b4 * 32:(b4 + 1) * 32, :]), reads=[t_xt2[b]],
                  is_out=K.is_out(xout))
    S.barrier()
    A.release(m_phase)


MAGIC = 12582912.0
TWO_PI_LO = 6.28318


def phase_mix1(K, xin, xout, P):
    S, A, ps = K.S, K.A, K.ps
    m_phase = A.mark()
    PS = lambda b: ps[:, b * 512:(b + 1) * 512]
    PSB = lambda b: psum_bf16(ps[:, b * 512:(b + 1) * 512])
    t_bank = [T() for _ in range(8)]
    Tm = A.alloc([128, 64, 128], BF16, top=True)
    Vre = A.alloc([128, 64, 128], BF16, top=True)
    Vimn = A.alloc([128, 64, 128], BF16, top=True)
    W1 = A.alloc([128, 64, 2, 128], BF16, top=True)
    r8 = A.alloc([128, 64], F32, top=True)
    y8 = A.alloc([128, 64], F32, top=True)
    iok = A.alloc([128, 512], F32, top=True)
    identf = A.alloc([128, 128], F32, top=True)
    t_U = [T() for _ in range(64)]
    t_prm = T()
    S.dma("sp", lambda e: e.dma_start(out=iok, in_=P["iotak"]), writes=[t_prm])
    S.dma("sp", lambda e: e.dma_start(out=identf, in_=P["identf"]), writes=[t_prm])
    m_prep = A.mark()

    CTr = A.alloc([128, 64, 16], F32); CTi = A.alloc([128, 64, 16], F32)
    Bbr = A.alloc([128, 64, 16], F32); Bbi = A.alloc([128, 64, 16], F32)
    S1r = A.alloc([128, 64, 8], F32); S1i = A.alloc([128, 64, 8], F32)
    S2r = A.alloc([128, 64, 8], F32); S2i = A.alloc([128, 64, 8], F32)
    S3r = A.alloc([128, 64, 8], F32); S3i = A.alloc([128, 64, 8], F32)
    Dm = A.alloc([128, 64], F32); sel = A.alloc([128, 128], F32); Dvec = A.alloc([128, 64], F32)
    mskf = A.alloc([128, 128], F32); mskb = A.alloc([128, 128], F32); dperm = A.alloc([128, 128], F32)
    m_p1 = A.mark()
    rows = A.alloc([128, 3, 128], F32)
    ldt = A.alloc([128, 2], F32)
    LR = A.alloc([128, 64], F32); LI = A.alloc([128, 64], F32); DT = A.alloc([128, 64], F32)
    BR = A.alloc([128, 64, 16], F32); BI = A.alloc([128, 64, 16], F32)
    Crow1 = A.alloc([128, 32, 128], F32)
    Crow = [Crow1, Crow1]
    t_rows, t_l, t_B, t_ct = T(), T(), T(), T()
    t_c1 = T(); t_crow = [t_c1, t_c1]
    v2 = lambda ap: ap.rearrange("dir (G2 g2) p -> (dir G2) (g2 p)", g2=2)
    S.dma("sp", lambda e: e.dma_start(out=rows[0:64, 0, :], in_=v2(P["lam_re"])), writes=[t_rows])
    S.dma("sp", lambda e: e.dma_start(out=rows[0:64, 1, :], in_=v2(P["lam_im"])), writes=[t_rows])
    S.dma("sp", lambda e: e.dma_start(out=ldt[0:64, :], in_=P["log_dt"].rearrange("dir (G2 g2) -> (dir G2) g2", g2=2)), writes=[t_rows])
    S.op("dve", lambda e: e.tensor_copy(out=rows[0:64, 2, :].rearrange("p (g2 q) -> p g2 q", g2=2),
                                        in_=ldt[0:64, :].unsqueeze(2).to_broadcast([64, 2, 64])), reads=[t_rows], writes=[t_rows])
    for i, dst in enumerate((LR, LI, DT)):
        S.op("pe", lambda e, i=i: e.transpose(PS(0)[:, i * 64:(i + 1) * 64], rows[0:64, i, :], identf[0:64, 0:64]), reads=[t_rows, t_prm], writes=[t_bank[0]])
    for i, dst in enumerate((LR, LI, DT)):
        S.op("dve", lambda e, i=i, dst=dst: e.tensor_copy(out=dst, in_=PS(0)[:, i * 64:(i + 1) * 64]), reads=[t_bank[0]], writes=[t_l])
    bsrc = lambda ap: ap.rearrange("dir (G2 g2) p c -> (g2 p) (dir G2) c", g2=2)
    S.dma("sp", lambda e: e.dma_start(out=BR, in_=bsrc(P["b_re"])), writes=[t_B])
    S.dma("sp", lambda e: e.dma_start(out=BI, in_=bsrc(P["b_im"])), writes=[t_B])
    for ri, (nm, dst) in enumerate((("c_re", CTr), ("c_im", CTi))):
        for d in range(2):
            cr = Crow[d]
            S.dma("sp", lambda e, nm=nm, d=d, cr=cr: e.dma_start(out=cr[0:16].rearrange("c G2 (g2 p) -> c G2 g2 p", g2=2),
                                                                in_=P[nm][d].rearrange("(G2 g2) c p -> c G2 g2 p", g2=2)), writes=[t_crow[d]])
            bk = 1 + d
            for G2 in range(32):
                S.op("pe", lambda e, cr=cr, G2=G2, bk=bk: e.transpose(PS(bk)[:, G2 * 16:(G2 + 1) * 16], cr[0:16, G2, :], identf[0:16, 0:16]),
                     reads=[t_crow[d], t_prm], writes=[t_bank[bk]])
            S.op("dve", lambda e, dst=dst, d=d, bk=bk: e.tensor_copy(out=dst[:, d * 32:(d + 1) * 32, :].rearrange("p a b -> p (a b)"), in_=PS(bk)),
                 reads=[t_bank[bk]], writes=[t_ct])
    t_D = T()
    S.dma("sp", lambda e: e.dma_start(out=Dm[0:16, :], in_=P["d_skip"].rearrange("(g c) -> c g", c=16), allow_slow_non_contiguous=True), writes=[t_D])
    S.dma("sp", lambda e: e.dma_start(out=sel[0:16, :], in_=P["sel16"]), writes=[t_D])
    S.dma("sp", lambda e: e.dma_start(out=mskf, in_=P["mskf"]), writes=[t_D])
    S.dma("sp", lambda e: e.dma_start(out=mskb, in_=P["mskb"]), writes=[t_D])
    S.dma("sp", lambda e: e.dma_start(out=dperm, in_=P["dperm"]), writes=[t_D])
    S.op("pe", lambda e: e.matmul(PS(3)[:, 0:64], sel[0:16, :], Dm[0:16, :], start=True, stop=True), reads=[t_D], writes=[t_bank[3]])
    S.op("dve", lambda e: e.tensor_copy(out=Dvec, in_=PS(3)[:, 0:64]), reads=[t_bank[3]], writes=[t_D])

    sm = lambda: A.alloc([128, 64], F32)
    dt_, ar, ai, yf, tmp, tmp2 = sm(), sm(), sm(), sm(), sm(), sm()
    PWr = A.alloc([128, 9, 64], F32); PWi = A.alloc([128, 9, 64], F32)
    NPr = A.alloc([128, 8, 64], F32); NPi = A.alloc([128, 8, 64], F32)
    cs_n = A.alloc([128, 9, 64], F32); sn_n = A.alloc([128, 9, 64], F32)
    t_s = T()
    V = lambda eng, fn: S.op(eng, fn, reads=[t_l, t_s], writes=[t_s])
    V("act", lambda e: e.activation(out=dt_, in_=DT, func=AF.Exp))
    V("dve", lambda e: e.tensor_tensor(out=ar, in0=LR, in1=dt_, op=ALU.mult))
    V("dve", lambda e: e.tensor_tensor(out=ai, in0=LI, in1=dt_, op=ALU.mult))
    V("dve", lambda e: e.tensor_scalar(out=yf, in0=ai, scalar1=1.0 / (2 * np.pi), scalar2=None, op0=ALU.mult))
    V("dve", lambda e: e.tensor_scalar(out=tmp, in0=yf, scalar1=MAGIC, scalar2=MAGIC, op0=ALU.add, op1=ALU.subtract))
    V("dve", lambda e: e.tensor_tensor(out=yf, in0=yf, in1=tmp, op=ALU.subtract))

    def sincos(yv, n, sdst, cdst):
        V("dve", lambda e: e.tensor_scalar(out=tmp, in0=yv, scalar1=float(n), scalar2=None, op0=ALU.mult))
        V("dve", lambda e: e.tensor_scalar(out=tmp2, in0=tmp, scalar1=MAGIC, scalar2=MAGIC, op0=ALU.add, op1=ALU.subtract))
        V("dve", lambda e: e.tensor_tensor(out=tmp2, in0=tmp, in1=tmp2, op=ALU.subtract))
        V("act", lambda e: e.activation(out=sdst, in_=tmp2, func=AF.Sin, scale=TWO_PI_LO))
        V("dve", lambda e: e.tensor_scalar(out=tmp, in0=tmp, scalar1=0.25, scalar2=None, op0=ALU.add))
        V("dve", lambda e: e.tensor_scalar(out=tmp2, in0=tmp, scalar1=MAGIC, scalar2=MAGIC, op0=ALU.add, op1=ALU.subtract))
        V("dve", lambda e: e.tensor_tensor(out=tmp2, in0=tmp, in1=tmp2, op=ALU.subtract))
        V("act", lambda e: e.activation(out=cdst, in_=tmp2, func=AF.Sin, scale=TWO_PI_LO))

    for n in range(9):
        sincos(yf, n, sn_n[:, n, :], cs_n[:, n, :])
        V("act", lambda e, n=n: e.activation(out=tmp, in_=ar, func=AF.Exp, scale=float(n)))
        V("dve", lambda e, n=n: e.tensor_tensor(out=PWr[:, n, :], in0=tmp, in1=cs_n[:, n, :], op=ALU.mult))
        V("dve", lambda e, n=n: e.tensor_tensor(out=PWi[:, n, :], in0=tmp, in1=sn_n[:, n, :], op=ALU.mult))
        if n < 8:
            V("act", lambda e, n=n: e.activation(out=tmp, in_=ar, func=AF.Exp, scale=-float(n)))
            V("dve", lambda e, n=n: e.tensor_tensor(out=NPr[:, n, :], in0=tmp, in1=cs_n[:, n, :], op=ALU.mult))
            V("dve", lambda e, n=n: e.scalar_tensor_tensor(out=NPi[:, n, :], in0=tmp, scalar=-1.0, in1=sn_n[:, n, :], op0=ALU.mult, op1=ALU.mult))
    S.op("act", lambda e: e.activation(out=r8, in_=ar, func=AF.Exp, scale=8.0), reads=[t_s], writes=[t_prm])
    V("dve", lambda e: e.tensor_scalar(out=tmp, in0=yf, scalar1=8.0, scalar2=None, op0=ALU.mult))
    V("dve", lambda e: e.tensor_scalar(out=tmp2, in0=tmp, scalar1=MAGIC, scalar2=MAGIC, op0=ALU.add, op1=ALU.subtract))
    S.op("dve", lambda e: e.tensor_tensor(out=y8, in0=tmp, in1=tmp2, op=ALU.subtract), reads=[t_s], writes=[t_prm])
    cre, cim, den, e1 = sm(), sm(), sm(), sm()
    V("dve", lambda e: e.tensor_scalar(out=e1, in0=PWr[:, 1, :], scalar1=-1.0, scalar2=None, op0=ALU.add))
    V("dve", lambda e: e.tensor_tensor(out=den, in0=LR, in1=LR, op=ALU.mult))
    V("dve", lambda e: e.tensor_tensor(out=tmp, in0=LI, in1=LI, op=ALU.mult))
    V("dve", lambda e: e.tensor_tensor(out=den, in0=den, in1=tmp, op=ALU.add))
    V("dve", lambda e: e.reciprocal(out=den, in_=den))
    V("dve", lambda e: e.tensor_tensor(out=cre, in0=e1, in1=LR, op=ALU.mult))
    V("dve", lambda e: e.tensor_tensor(out=tmp, in0=PWi[:, 1, :], in1=LI, op=ALU.mult))
    V("dve", lambda e: e.tensor_tensor(out=cre, in0=cre, in1=tmp, op=ALU.add))
    V("dve", lambda e: e.tensor_tensor(out=cre, in0=cre, in1=den, op=ALU.mult))
    V("dve", lambda e: e.tensor_tensor(out=cim, in0=PWi[:, 1, :], in1=LR, op=ALU.mult))
    V("dve", lambda e: e.tensor_tensor(out=tmp, in0=e1, in1=LI, op=ALU.mult))
    V("dve", lambda e: e.tensor_tensor(out=cim, in0=cim, in1=tmp, op=ALU.subtract))
    V("dve", lambda e: e.tensor_tensor(out=cim, in0=cim, in1=den, op=ALU.mult))
    tb = A.alloc([128, 64, 16], F32)
    bc = lambda ap: ap.unsqueeze(2).to_broadcast([128, 64, 16])
    VB = lambda eng, fn: S.op(eng, fn, reads=[t_s, t_B, t_ct], writes=[t_s])
    VB("dve", lambda e: e.tensor_tensor(out=Bbr, in0=BR, in1=bc(cre), op=ALU.mult))
    VB("dve", lambda e: e.tensor_tensor(out=tb, in0=BI, in1=bc(cim), op=ALU.mult))
    VB("dve", lambda e: e.tensor_tensor(out=Bbr, in0=Bbr, in1=tb, op=ALU.subtract))
    VB("dve", lambda e: e.tensor_tensor(out=Bbi, in0=BI, in1=bc(cre), op=ALU.mult))
    VB("dve", lambda e: e.tensor_tensor(out=tb, in0=BR, in1=bc(cim), op=ALU.mult))
    VB("dve", lambda e: e.tensor_tensor(out=Bbi, in0=Bbi, in1=tb, op=ALU.add))
    for i in range(8):
        for (dr, di, sr, si, nf, nb) in ((S1r, S1i, PWr, PWi, 7 - i, i), (S2r, S2i, PWr, PWi, i + 1, 8 - i), (S3r, S3i, NPr, NPi, 7 - i, i)):
            V("pool", lambda e, dr=dr, sr=sr, nf=nf, i=i: e.tensor_copy(out=dr[:, 0:32, i], in_=sr[:, nf, 0:32]))
            V("pool", lambda e, dr=dr, sr=sr, nb=nb, i=i: e.tensor_copy(out=dr[:, 32:64, i], in_=sr[:, nb, 32:64]))
            V("pool", lambda e, di=di, si=si, nf=nf, i=i: e.tensor_copy(out=di[:, 0:32, i], in_=si[:, nf, 0:32]))
            V("pool", lambda e, di=di, si=si, nb=nb, i=i: e.tensor_copy(out=di[:, 32:64, i], in_=si[:, nb, 32:64]))
    dump(K, "LR", LR, [t_l]); dump(K, "LI", LI, [t_l]); dump(K, "DT", DT, [t_l])
    dump(K, "PWr", PWr, [t_s]); dump(K, "PWi", PWi, [t_s]); dump(K, "NPr", NPr, [t_s]); dump(K, "NPi", NPi, [t_s])
    dump(K, "cre", cre, [t_s]); dump(K, "cim", cim, [t_s]); dump(K, "r8", r8, [t_prm]); dump(K, "y8", y8, [t_prm])
    dump(K, "Bbr", Bbr, [t_s]); dump(K, "Bbi", Bbi, [t_s]); dump(K, "CTr", CTr, [t_ct]); dump(K, "CTi", CTi, [t_ct])
    dump(K, "S1r", S1r, [t_s]); dump(K, "S2i", S2i, [t_s]); dump(K, "S3r", S3r, [t_s]); dump(K, "Dvec", Dvec, [t_D])
    S.barrier()
    A.release(m_p1)
    W1Tr = A.alloc([128, 64, 128], BF16); W1Ti = A.alloc([128, 64, 128], BF16)
    Vnr = A.alloc([128, 64, 128], BF16); Vnin = A.alloc([128, 64, 128], BF16)
    ta = A.alloc([128, 16, 128], F32); tb2 = A.alloc([128, 16, 128], F32)
    t_s = T()
    VB = lambda eng, fn: S.op(eng, fn, reads=[t_s], writes=[t_s])
    for d in range(4):
        cs = slice(d * 16, (d + 1) * 16)
        pw = lambda ap, cs=cs: ap[:, cs, :].unsqueeze(2).to_broadcast([128, 16, 16, 8])
        bb = lambda ap, cs=cs: ap[:, cs, :].unsqueeze(3).to_broadcast([128, 16, 16, 8])
        o4 = lambda ap: ap.rearrange("p a (c i) -> p a c i", c=16)
        VB("dve", lambda e, pw=pw, bb=bb: e.tensor_tensor(out=o4(ta), in0=pw(S1r), in1=bb(Bbr), op=ALU.mult))
        VB("dve", lambda e, pw=pw, bb=bb: e.tensor_tensor(out=o4(tb2), in0=pw(S1i), in1=bb(Bbi), op=ALU.mult))
        VB("dve", lambda e, cs=cs: e.tensor_tensor(out=W1Tr[:, cs, :], in0=ta, in1=tb2, op=ALU.subtract))
        VB("dve", lambda e, pw=pw, bb=bb: e.tensor_tensor(out=o4(ta), in0=pw(S1r), in1=bb(Bbi), op=ALU.mult))
        VB("dve", lambda e, pw=pw, bb=bb: e.tensor_tensor(out=o4(tb2), in0=pw(S1i), in1=bb(Bbr), op=ALU.mult))
        VB("dve", lambda e, cs=cs: e.tensor_tensor(out=W1Ti[:, cs, :], in0=ta, in1=tb2, op=ALU.add))
        pj = lambda ap, cs=cs: ap[:, cs, :].unsqueeze(3).to_broadcast([128, 16, 8, 16])
        cc = lambda ap, cs=cs: ap[:, cs, :].unsqueeze(2).to_broadcast([128, 16, 8, 16])
        o5 = lambda ap: ap.rearrange("p a (j c) -> p a j c", j=8)
        for (Sr, Si, dre, dimn) in ((S2r, S2i, Vre, Vimn), (S3r, S3i, Vnr, Vnin)):
            VB("dve", lambda e, pj=pj, cc=cc, Sr=Sr: e.tensor_tensor(out=o5(ta), in0=pj(Sr), in1=cc(CTr), op=ALU.mult))
            VB("dve", lambda e, pj=pj, cc=cc, Si=Si: e.tensor_tensor(out=o5(tb2), in0=pj(Si), in1=cc(CTi), op=ALU.mult))
            VB("dve", lambda e, cs=cs, dre=dre: e.tensor_tensor(out=dre[:, cs, :], in0=ta, in1=tb2, op=ALU.subtract))
            VB("dve", lambda e, pj=pj, cc=cc, Si=Si: e.tensor_tensor(out=o5(ta), in0=pj(Si), in1=cc(CTr), op=ALU.mult))
            VB("dve", lambda e, pj=pj, cc=cc, Sr=Sr: e.tensor_tensor(out=o5(tb2), in0=pj(Sr), in1=cc(CTi), op=ALU.mult))
            VB("dve", lambda e: e.tensor_tensor(out=ta, in0=ta, in1=tb2, op=ALU.add))
            VB("dve", lambda e, cs=cs, dimn=dimn: e.tensor_scalar(out=dimn[:, cs, :], in0=ta, scalar1=-1.0, scalar2=None, op0=ALU.mult))
    k = 0
    for ri, src in enumerate((W1Tr, W1Ti)):
        for c8 in range(8):
            bk = 4 + k % 2; k += 1
            for cl in range(8):
                col = c8 * 8 + cl
                S.op("pe", lambda e, src=src, col=col, cl=cl, bk=bk: e.transpose(PSB(bk)[:, cl * 128:(cl + 1) * 128], src[:, col, :], K.identb),
                     reads=[t_s, K.t_const], writes=[t_bank[bk]])
            S.op("act", lambda e, ri=ri, c8=c8, bk=bk: e.activation(out=W1[:, c8 * 8:(c8 + 1) * 8, ri, :], in_=PSB(bk).rearrange("p (a b) -> p a b", a=8), func=AF.Copy),
                 reads=[t_bank[bk]], writes=[t_prm])
    tq = [A.alloc([128, 128], F32) for _ in range(2)]
    t_tq = [T(), T()]
    for g in range(64):
        G2, g2 = g // 2, g % 2
        lo, hi = g2 * 64, g2 * 64 + 64
        bf_, bb_ = 6, 7
        for (bk, col) in ((bf_, G2), (bb_, 32 + G2)):
            S.op("pe", lambda e, bk=bk, col=col, lo=lo, hi=hi: e.matmul(PS(bk)[:, 0:128], W1Tr[lo:hi, col, :], Vnr[lo:hi, col, :], start=True, stop=False),
                 reads=[t_s], writes=[t_bank[bk]])
            S.op("pe", lambda e, bk=bk, col=col, lo=lo, hi=hi: e.matmul(PS(bk)[:, 0:128], W1Ti[lo:hi, col, :], Vnin[lo:hi, col, :], start=False, stop=True),
                 reads=[t_s], writes=[t_bank[bk]])
        j = g % 2
        S.op("dve", lambda e, j=j: e.tensor_tensor(out=tq[j], in0=PS(6)[:, 0:128], in1=mskf, op=ALU.mult), reads=[t_bank[6], t_D], writes=[t_tq[j]])
        S.op("dve", lambda e, j=j: e.tensor_tensor(out=PS(7)[:, 128:256], in0=PS(7)[:, 0:128], in1=mskb, op=ALU.mult), reads=[t_D], writes=[t_bank[7]])
        S.op("dve", lambda e, j=j: e.tensor_tensor(out=tq[j], in0=PS(7)[:, 128:256], in1=tq[j], op=ALU.add), reads=[t_bank[7]], writes=[t_tq[j]])
        S.op("dve", lambda e, j=j, g=g: e.scalar_tensor_tensor(out=Tm[:, g, :], in0=dperm, scalar=Dvec[:, g:g + 1], in1=tq[j], op0=ALU.mult, op1=ALU.add),
             reads=[t_tq[j], t_D, t_prm], writes=[t_prm])
    dump(K, "W1Tr", W1Tr, [t_s]); dump(K, "W1Ti", W1Ti, [t_s]); dump(K, "Vnr", Vnr, [t_s]); dump(K, "Vnin", Vnin, [t_s])
    dump(K, "Vre", Vre, [t_s]); dump(K, "Vimn", Vimn, [t_s]); dump(K, "Tm", Tm, [t_prm]); dump(K, "W1", W1, [t_prm])
    S.barrier()
    A.release(m_prep)

    U = A.alloc([128, 64, 512], BF16)
    m_U = A.mark()
    xt = A.alloc([128, 8, D], F32)
    ub = A.alloc([128, D, 8], BF16)
    gt = A.alloc([128, D], F32)
    junk = A.alloc([128, D], BF16)
    ss = A.alloc([128, 8], F32); rstd = A.alloc([128, 8], F32)
    t_xt, t_ub, t_g, t_ss = T(), T(), T(), T()
    load_bcast_row(K, "sp", gt, P["norm_mix1"], t_g)
    xk = xin.rearrange("(kb kp i) d -> kb kp i d", kp=128, i=8)
    for kb in range(4):
        S.dma("sp", lambda e, kb=kb: e.dma_start(out=xt, in_=xk[kb]), writes=[t_xt])
        emit_rmsnorm_stats(K, xt, 8, ss, rstd, t_xt, t_ss, junk)
        for i in range(8):
            S.op("dve", lambda e, i=i: e.scalar_tensor_tensor(out=ub[:, :, i], in0=xt[:, i, :], scalar=rstd[:, i:i + 1], in1=gt, op0=ALU.mult, op1=ALU.mult),
                 reads=[t_xt, t_ss, t_g], writes=[t_ub])
        ubf = ub.rearrange("p d i -> p (d i)")
        for g8 in range(8):
            bk = g8 % 2
            for gl in range(8):
                g = g8 * 8 + gl
                S.op("pe", lambda e, g=g, gl=gl, bk=bk: e.transpose(PSB(bk)[:, gl * 128:(gl + 1) * 128], ubf[:, g * 128:(g + 1) * 128], K.identb),
                     reads=[t_ub, K.t_const], writes=[t_bank[bk]])
            src = PSB(bk).rearrange("p (a b) -> p a b", a=8)
            dst = U[:, g8 * 8:(g8 + 1) * 8, kb * 128:(kb + 1) * 128]
            wr = [t_U[g8 * 8 + gl] for gl in range(8)]
            if g8 % 2 == 0:
                S.op("act", lambda e, src=src, dst=dst: e.activation(out=dst, in_=src, func=AF.Copy), reads=[t_bank[bk]], writes=wr)
            else:
                S.op("dve", lambda e, src=src, dst=dst: e.tensor_copy(out=dst, in_=src), reads=[t_bank[bk]], writes=wr)
    dump(K, "U", U, t_U)
    S.barrier()
    A.release(m_U)

    NB = 2
    mk = lambda dt=F32: [A.alloc([128, 512], dt) for _ in range(NB)]
    ys_, fs_, sn_, cs_, xr_, xi_, wr_, wi_, zr_, zi_ = [mk() for _ in range(10)]
    m1_, m2_ = ys_, fs_
    t_t = [[T() for _ in range(10)] for _ in range(NB)]
    for tl_ in t_t:
        tl_.extend([tl_[0], tl_[1]])
    t_E = [T() for _ in range(4)]
    Es = [[A.alloc([128, 512], BF16) for _ in range(2)] for _ in range(4)]
    for sl in range(4):
        for ri in range(2):
            S.op("pool", lambda e, sl=sl, ri=ri: e.memset(Es[sl][ri], 0.0), writes=[t_E[sl]])
    it = [0]

    def level1(G2):
        for d in range(2):
            for ri in range(2):
                bk = d * 2 + ri
                for g2 in range(2):
                    g = 2 * G2 + g2
                    S.op("pe", lambda e, bk=bk, d=d, ri=ri, g2=g2, g=g: e.matmul(PS(bk)[g2 * 64:(g2 + 1) * 64, :], W1[:, d * 32 + G2, ri, g2 * 64:(g2 + 1) * 64],
                                                                             U[:, g, :], start=True, stop=True),
                         reads=[t_prm, t_U[g]], writes=[t_bank[bk]])

    def scan(G2, d):
        b = it[0] % NB; it[0] += 1
        col = d * 32 + G2
        tt = t_t[b]
        ys, fs, sn, cs, xr, xi, wr, wi, zr, zi, m1, m2 = [lst[b] for lst in (ys_, fs_, sn_, cs_, xr_, xi_, wr_, wi_, zr_, zi_, m1_, m2_)]
        rv = (lambda ap: ap[:, ::-1]) if d == 1 else (lambda ap: ap)
        S.op("act", lambda e: e.activation(out=xr, in_=rv(PS(d * 2 + 0)), func=AF.Copy), reads=[t_bank[d * 2]], writes=[tt[4]])
        S.op("act", lambda e: e.activation(out=xi, in_=rv(PS(d * 2 + 1)), func=AF.Copy), reads=[t_bank[d * 2 + 1]], writes=[tt[5]])
        S.op("dve", lambda e: e.tensor_scalar(out=ys, in0=iok, scalar1=y8[:, col:col + 1], scalar2=None, op0=ALU.mult), reads=[t_prm], writes=[tt[0]])
        S.op("dve", lambda e: e.tensor_scalar(out=fs, in0=ys, scalar1=MAGIC, scalar2=MAGIC, op0=ALU.add, op1=ALU.subtract), reads=[tt[0]], writes=[tt[1]])
        S.op("dve", lambda e: e.tensor_tensor(out=fs, in0=ys, in1=fs, op=ALU.subtract), reads=[tt[0], tt[1]], writes=[tt[1]])
        S.op("act", lambda e: e.activation(out=sn, in_=fs, func=AF.Sin, scale=TWO_PI_LO), reads=[tt[1]], writes=[tt[2]])
        S.op("dve", lambda e: e.tensor_scalar(out=ys, in0=ys, scalar1=0.25, scalar2=None, op0=ALU.add), reads=[tt[1]], writes=[tt[0]])
        S.op("dve", lambda e: e.tensor_scalar(out=fs, in0=ys, scalar1=MAGIC, scalar2=MAGIC, op0=ALU.add, op1=ALU.subtract), reads=[tt[0]], writes=[tt[1]])
        S.op("dve", lambda e: e.tensor_tensor(out=fs, in0=ys, in1=fs, op=ALU.subtract), reads=[tt[0], tt[1]], writes=[tt[1]])
        S.op("act", lambda e: e.activation(out=cs, in_=fs, func=AF.Sin, scale=TWO_PI_LO), reads=[tt[1]], writes=[tt[3]])
        S.op("dve", lambda e: e.tensor_tensor(out=m1, in0=xr, in1=cs, op=ALU.mult), reads=[tt[4], tt[3]], writes=[tt[10]])
        S.op("dve", lambda e: e.tensor_tensor(out=m2, in0=xi, in1=sn, op=ALU.mult), reads=[tt[5], tt[2]], writes=[tt[11]])
        S.op("pool", lambda e: e.tensor_tensor(out=wr, in0=m1, in1=m2, op=ALU.add), reads=[tt[10], tt[11]], writes=[tt[6]])
        S.op("dve", lambda e: e.tensor_tensor(out=m1, in0=xi, in1=cs, op=ALU.mult), reads=[tt[5], tt[3]], writes=[tt[10]])
        S.op("dve", lambda e: e.tensor_tensor(out=m2, in0=xr, in1=sn, op=ALU.mult), reads=[tt[4], tt[2]], writes=[tt[11]])
        S.op("pool", lambda e: e.tensor_tensor(out=wi, in0=m1, in1=m2, op=ALU.subtract), reads=[tt[10], tt[11]], writes=[tt[7]])
        rb = r8[:, col:col + 1].to_broadcast([128, 512])
        S.op("dve", lambda e: e.tensor_tensor_scan(out=zr, data0=rb, data1=wr, initial=0.0, op0=ALU.mult, op1=ALU.add), reads=[t_prm, tt[6]], writes=[tt[8]])
        S.op("dve", lambda e: e.tensor_tensor_scan(out=zi, data0=rb, data1=wi, initial=0.0, op0=ALU.mult, op1=ALU.add), reads=[t_prm, tt[7]], writes=[tt[9]])
        sl = (G2 % 2) * 2 + d
        if d == 0:
            dre, dim = Es[sl][0][:, 1:512], Es[sl][1][:, 1:512]
        else:
            dre, dim = Es[sl][0][:, 0:511][:, ::-1], Es[sl][1][:, 0:511][:, ::-1]
        S.op("pool", lambda e: e.tensor_tensor(out=xr, in0=zr, in1=cs, op=ALU.mult), reads=[tt[8], tt[3]], writes=[tt[4]])
        S.op("pool", lambda e: e.tensor_tensor(out=xi, in0=zi, in1=sn, op=ALU.mult), reads=[tt[9], tt[2]], writes=[tt[5]])
        S.op("pool", lambda e: e.tensor_tensor(out=dre, in0=xr[:, 0:511], in1=xi[:, 0:511], op=ALU.subtract), reads=[tt[4], tt[5]], writes=[t_E[sl]])
        S.op("pool", lambda e: e.tensor_tensor(out=m1, in0=zi, in1=cs, op=ALU.mult), reads=[tt[9], tt[3]], writes=[tt[10]])
        S.op("pool", lambda e: e.tensor_tensor(out=m2, in0=zr, in1=sn, op=ALU.mult), reads=[tt[8], tt[2]], writes=[tt[11]])
        S.op("pool", lambda e: e.tensor_tensor(out=dim, in0=m1[:, 0:511], in1=m2[:, 0:511], op=ALU.add), reads=[tt[10], tt[11]], writes=[t_E[sl]])

    def output(G2):
        for g2 in range(2):
            g = 2 * G2 + g2
            lo, hi = g2 * 64, g2 * 64 + 64
            bk = 4 + (g % 4)
            for kb in range(4):
                ks = slice(kb * 128, (kb + 1) * 128)
                o = PS(bk)[:, kb * 128:(kb + 1) * 128]
                S.op("pe", lambda e, o=o, g=g, ks=ks: e.matmul(o, U[:, g, ks], Tm[:, g, :], start=True, stop=False), reads=[t_U[g], t_prm], writes=[t_bank[bk]])
                n = 0
                for d in range(2):
                    sl = (G2 % 2) * 2 + d
                    col = d * 32 + G2
                    for ri, Vm in ((0, Vre), (1, Vimn)):
                        n += 1
                        S.op("pe", lambda e, o=o, sl=sl, ri=ri, Vm=Vm, col=col, ks=ks, lo=lo, hi=hi, n=n: e.matmul(o, Es[sl][ri][lo:hi, ks], Vm[lo:hi, col, :],
                                                                                                         start=False, stop=(n == 4)),
                             reads=[t_E[sl], t_prm], writes=[t_bank[bk]])
            if g % 2 == 0:
                S.op("act", lambda e, g=g, bk=bk: e.activation(out=U[:, g, :], in_=PS(bk), func=AF.Copy), reads=[t_bank[bk]], writes=[t_U[g]])
            else:
                S.op("dve", lambda e, g=g, bk=bk: e.tensor_copy(out=U[:, g, :], in_=PS(bk)), reads=[t_bank[bk]], writes=[t_U[g]])

    level1(0)
    for G2 in range(32):
        scan(G2, 0)
        scan(G2, 1)
        if G2 + 1 < 32:
            level1(G2 + 1)
        output(G2)
    dump(K, "Y", U, t_U)
    S.barrier()
    A.release(m_U)
    A.release_top()

    Wg = A.alloc([128, 8, D], BF16)
    stg = A.alloc([128, 4, D], F32)
    bgf = A.alloc([128, D], F32); bgb = A.alloc([128, D], BF16); onesr = A.alloc([128, 128], BF16)
    t_wg, t_stg = T(), T()
    wgv = P["w_gate"].rearrange("(dc p) n -> p dc n", p=128)
    for h in range(2):
        S.dma("sp", lambda e, h=h: e.dma_start(out=stg, in_=wgv[:, 4 * h:4 * h + 4, :]), writes=[t_stg])
        S.op("pool", lambda e, h=h: e.tensor_copy(out=Wg[:, 4 * h:4 * h + 4, :], in_=stg), reads=[t_stg], writes=[t_wg])
    S.dma("sp", lambda e: e.dma_start(out=bgf[0:1, :], in_=P["b_gate"].unsqueeze(0)), writes=[t_stg])
    S.op("pool", lambda e: e.tensor_copy(out=bgb[0:1, :], in_=bgf[0:1, :]), reads=[t_stg], writes=[t_wg])
    S.op("pool", lambda e: e.memset(onesr, 1.0), writes=[t_wg])
    NBG = 2
    sqb = [A.alloc([128, D], F32) for _ in range(NBG)]
    gel = [A.alloc([128, D], F32) for _ in range(NBG)]
    gb = [A.alloc([128, D], BF16) for _ in range(NBG)]
    gT = [A.alloc([128, D], BF16) for _ in range(NBG)]
    sg = [A.alloc([128, 512], F32) for _ in range(2)]
    xt3 = [A.alloc([128, D], F32) for _ in range(3)]
    t_y, t_sq, t_gel, t_gb, t_gT, t_sg, t_x3 = [T(), T()], [T(), T()], [T(), T()], [T(), T()], [T(), T()], [T(), T()], [T(), T(), T()]
    t_Uall = T()
    xs = xin.rearrange("(kb kp j) d -> kb j kp d", kp=128, j=8)
    xo = xout.rearrange("(kb kp j) d -> kb j kp d", kp=128, j=8)
    yall = U.rearrange("p g (kb j c) -> p kb j g c", kb=4, j=8)
    kk = [0]

    def load3(n):
        kb, j = n // 8, n % 8
        S.dma("sp", lambda e: e.dma_start(out=xt3[n % 3], in_=xs[kb, j]), writes=[t_x3[n % 3]])

    load3(0); load3(1)
    for n in range(32):
        kb, j = n // 8, n % 8
        b = n % NBG
        if n + 2 < 32:
            load3(n + 2)
        yv = yall[:, kb, j]
        y3 = lambda ap: ap.rearrange("p (g c) -> p g c", c=16)
        S.op("act", lambda e, yv=yv, b=b: e.activation(out=y3(sqb[b]), in_=yv, func=AF.Square), reads=t_U, writes=[t_sq[b]])
        S.op("dve", lambda e, b=b: e.tensor_scalar(out=sqb[b], in0=sqb[b], scalar1=0.044715, scalar2=1.0, op0=ALU.mult, op1=ALU.add), reads=[t_sq[b]], writes=[t_sq[b]])
        S.op("dve", lambda e, yv=yv, b=b: e.tensor_tensor(out=y3(sqb[b]), in0=y3(sqb[b]), in1=yv, op=ALU.mult), reads=t_U + [t_sq[b]], writes=[t_sq[b]])
        S.op("act", lambda e, b=b: e.activation(out=sqb[b], in_=sqb[b], func=AF.Sigmoid, scale=1.5957691216), reads=[t_sq[b]], writes=[t_sq[b]])
        S.op("dve", lambda e, yv=yv, b=b: e.tensor_tensor(out=y3(gel[b]), in0=y3(sqb[b]), in1=yv, op=ALU.mult), reads=t_U + [t_sq[b]], writes=[t_gel[b]])
        S.op("pool", lambda e, b=b: e.tensor_copy(out=gb[b], in_=gel[b]), reads=[t_gel[b]], writes=[t_gb[b]])
        for dc in range(8):
            S.op("pe", lambda e, b=b, dc=dc: e.transpose(PSB(0)[:, dc * 128:(dc + 1) * 128], gb[b][:, dc * 128:(dc + 1) * 128], K.identb),
                 reads=[t_gb[b], K.t_const], writes=[t_bank[0]])
        S.op("act", lambda e, b=b: e.activation(out=gT[b], in_=PSB(0), func=AF.Copy), reads=[t_bank[0]], writes=[t_gT[b]])
        xb = xt3[n % 3]
        for dh in range(2):
            bk = 1 + kk[0] % 2; kk[0] += 1
            sb_ = kk[0] % 2
            for dc in range(8):
                S.op("pe", lambda e, b=b, dc=dc, dh=dh, bk=bk: e.matmul(PS(bk), gT[b][:, dc * 128:(dc + 1) * 128], Wg[:, dc, dh * 512:(dh + 1) * 512], start=(dc == 0), stop=False),
                     reads=[t_gT[b], t_wg], writes=[t_bank[bk]])
            S.op("pe", lambda e, dh=dh, bk=bk: e.matmul(PS(bk), onesr[0:1, :], bgb[0:1, dh * 512:(dh + 1) * 512], start=False, stop=True), reads=[t_wg], writes=[t_bank[bk]])
            S.op("act", lambda e, bk=bk, sb_=sb_: e.activation(out=sg[sb_], in_=PS(bk), func=AF.Sigmoid), reads=[t_bank[bk]], writes=[t_sg[sb_]])
            S.op("dve", lambda e, b=b, dh=dh, sb_=sb_: e.tensor_tensor(out=sg[sb_], in0=sg[sb_], in1=gel[b][:, dh * 512:(dh + 1) * 512], op=ALU.mult),
                 reads=[t_sg[sb_], t_gel[b]], writes=[t_sg[sb_]])
            S.op("pool", lambda e, xb=xb, dh=dh, sb_=sb_: e.tensor_tensor(out=xb[:, dh * 512:(dh + 1) * 512], in0=xb[:, dh * 512:(dh + 1) * 512], in1=sg[sb_], op=ALU.add),
                 reads=[t_sg[sb_]], writes=[t_x3[n % 3]])
        S.dma("sp", lambda e, xb=xb, kb=kb, j=j: e.dma_start(out=xo[kb, j], in_=xb), reads=[t_x3[n % 3]], is_out=K.is_out(xout))
    S.barrier()
    A.release(m_phase)


def build_program(phases=("mix0", "mlp0", "mix1", "mlp1"), dbg=False):
    nc = bass.Bass("TRN2", target_bir_lowering=False)
    K = Ctx()
    K.dbg = dbg
    K.nc = nc
    din = lambda name, shape, dt=F32: nc.dram_tensor(name, list(shape), dt, kind="ExternalInput").ap()
    x = din("x", [S_LEN, D])
    norm_mix = din("norm_mix", [2, D]); norm_mlp = din("norm_mlp", [2, D])
    w1 = din("mlp_w1", [2, D, DFF]); w2 = din("mlp_w2", [2, DFF, D])
    final_norm = din("final_norm", [D])
    identb_d = din("identb", [128, 128], BF16)
    P = {}
    P["w_in"] = din("w_in", [1, D, 1280])[0]
    P["w_fnet"] = din("w_fnet", [1, 8, 64, 64])[0]
    P["q_norm"] = din("q_norm", [1, 64])[0]
    P["k_norm"] = din("k_norm", [1, 64])[0]
    P["w_out"] = din("w_out", [1, D, D])[0]
    for nm in ("rmat", "onesblk", "bdc", "bds", "bdsn", "c64bd", "s64bdn"):
        P[nm] = din(nm, [128, 128], BF16)
    P["cosp"] = din("cosp", [128, S_LEN]); P["sinp"] = din("sinp", [128, S_LEN])
    P["cbt"] = din("cbt", [128, S_LEN], BF16); P["sbt"] = din("sbt", [128, S_LEN], BF16)
    P["norm_mix0"] = norm_mix[0]
    P["norm_mix1"] = norm_mix[1]
    for nm, shp in (("lam_re", [1, 2, 64, 64]), ("lam_im", [1, 2, 64, 64]), ("log_dt", [1, 2, 64]), ("b_re", [1, 2, 64, 64, 16]), ("b_im", [1, 2, 64, 64, 16]),
                    ("c_re", [1, 2, 64, 16, 64]), ("c_im", [1, 2, 64, 16, 64]), ("d_skip", [1, D]), ("w_gate", [1, D, D]), ("b_gate", [1, D])):
        P[nm] = din(nm, shp)[0]
    P["iotak"] = din("iotak", [128, 512]); P["identf"] = din("identf", [128, 128])
    P["sel16"] = din("sel16", [16, 128]); P["mskf"] = din("mskf", [128, 128]); P["mskb"] = din("mskb", [128, 128]); P["dperm"] = din("dperm", [128, 128])
    P["pqh"] = nc.dram_tensor("pqh", [S_LEN, 2, 512], BF16).ap()
    out = nc.dram_tensor("out", [S_LEN, D], F32, kind="ExternalOutput").ap()
    scr = [nc.dram_tensor("scr%d" % i, [S_LEN, D], F32).ap() for i in range(3)]
    K.out = out
    K.is_out = lambda ap: ap is out
    with ExitStack() as es:
        arena = es.enter_context(nc.sbuf_tensor("arena", [128, ARENA_BYTES + 512], U8))
        K.ps = es.enter_context(nc.psum_tensor("ps", [128, 4096], F32))
        K.S = Sched(nc, es)
        K.A = Arena(arena)
        K.identb = K.A.alloc([128, 128], BF16)
        K.t_const = T()
        K.S.dma("sp", lambda e: e.dma_start(out=K.identb, in_=identb_d), writes=[K.t_const])
        K.eps_ap = K.A.alloc([128, 1], F32)
        K.S.op("pool", lambda e: e.memset(K.eps_ap, EPS), writes=[K.t_const])
        cur = x
        chain = {"mix0": scr[0], "mlp0": scr[1], "mix1": scr[2], "mlp1": out}
        last = phases[-1]
        for ph in phases:
            dst = out if ph == last else chain[ph]
            if ph == "mix0":
                phase_mix0(K, cur, dst, P)
            elif ph == "mix1":
                phase_mix1(K, cur, dst, P)
            elif ph == "mlp0":
                phase_mlp(K, cur, dst, w1[0], w2[0], norm_mlp[0])
            elif ph == "mlp1":
                phase_mlp(K, cur, dst, w1[1], w2[1], norm_mlp[1], final_gain=final_norm)
            cur = dst
        K.S.emit()
    return nc


def make_consts():
    c = {}
    bf = ml_dtypes.bfloat16
    c["identb"] = np.eye(128, dtype=np.float32).astype(bf)
    R = np.zeros((128, 128), np.float32)
    for h in range(2):
        for sec in range(2):
            base = h * 64 + sec * 32
            for i in range(16):
                R[base + 16 + i, base + i] = -1.0
                R[base + i, base + 16 + i] = 1.0
    c["rmat"] = R.astype(bf)
    ob = np.zeros((128, 128), np.float32)
    ob[0:64, 0:64] = 1.0 / 64; ob[64:, 64:] = 1.0 / 64
    c["onesblk"] = ob.astype(bf)
    a = np.arange(32, dtype=np.float64)
    c32 = np.cos(2 * np.pi * np.outer(a, a) / 32); s32 = np.sin(2 * np.pi * np.outer(a, a) / 32)
    bdc = np.zeros((128, 128)); bds = np.zeros((128, 128))
    for b4 in range(4):
        bdc[b4 * 32:(b4 + 1) * 32, b4 * 32:(b4 + 1) * 32] = c32
        bds[b4 * 32:(b4 + 1) * 32, b4 * 32:(b4 + 1) * 32] = s32
    c["bdc"] = bdc.astype(np.float32).astype(bf); c["bds"] = bds.astype(np.float32).astype(bf); c["bdsn"] = (-bds).astype(np.float32).astype(bf)
    k64 = np.arange(64, dtype=np.float64)
    c64 = np.cos(2 * np.pi * np.outer(k64, k64) / 64) / 512.0; s64 = -np.sin(2 * np.pi * np.outer(k64, k64) / 64) / 512.0
    cb = np.zeros((128, 128)); sb_ = np.zeros((128, 128))
    for h in range(2):
        cb[h * 64:(h + 1) * 64, h * 64:(h + 1) * 64] = c64
        sb_[h * 64:(h + 1) * 64, h * 64:(h + 1) * 64] = s64
    c["c64bd"] = cb.astype(np.float32).astype(bf); c["s64bdn"] = sb_.astype(np.float32).astype(bf)
    n = np.arange(S_LEN); t = 128 * (n % 32) + n // 32
    row = (t // 64).astype(np.float64); colp = (t % 64).astype(np.float64)
    inv = 10000.0 ** (-np.arange(0, 32, 2, dtype=np.float64) / 32.0)
    cosp = np.zeros((128, S_LEN)); sinp = np.zeros((128, S_LEN))
    for p in range(128):
        d = p % 64
        pos = row if d < 32 else colp
        ang = pos * inv[d % 16]
        cosp[p] = np.cos(ang); sinp[p] = np.sin(ang)
    c["cosp"] = cosp.astype(np.float32); c["sinp"] = sinp.astype(np.float32)
    b = np.arange(128, dtype=np.float64)[:, None]; tt = np.arange(S_LEN, dtype=np.float64)[None, :]
    ph = 2 * np.pi * ((b * tt) % S_LEN) / S_LEN
    c["cbt"] = np.cos(ph).astype(np.float32).astype(bf); c["sbt"] = np.sin(ph).astype(np.float32).astype(bf)
    c["iotak"] = np.tile(np.arange(512, dtype=np.float32)[None, :], (128, 1))
    c["identf"] = np.eye(128, dtype=np.float32)
    sel = np.zeros((16, 128), np.float32)
    for cc in range(16):
        sel[cc, cc * 8:(cc + 1) * 8] = 1.0
    c["sel16"] = sel
    ii = (np.arange(128) % 8)[:, None]
    jj = (np.arange(128) // 16)[None, :]
    c["mskf"] = (ii <= jj).astype(np.float32); c["mskb"] = (ii >= jj).astype(np.float32)
    ci_ = (np.arange(128) // 8)[:, None]; co_ = (np.arange(128) % 16)[None, :]
    c["dperm"] = ((ii == jj) & (ci_ == co_)).astype(np.float32)
    return c


_USED = ("norm_mix", "norm_mlp", "mlp_w1", "mlp_w2", "final_norm", "w_in", "w_fnet", "q_norm", "k_norm", "w_out",
         "lam_re", "lam_im", "log_dt", "b_re", "b_im", "c_re", "c_im", "d_skip", "w_gate", "b_gate")


def kernel(**inputs):
    nc = build_program()
    consts = make_consts()
    x = np.ascontiguousarray(inputs["x"], dtype=np.float32)
    shared = {k: np.ascontiguousarray(inputs[k], dtype=np.float32) for k in _USED}
    shared.update(consts)
    in_maps = []
    for i in range(8):
        m = dict(shared)
        m["x"] = x[i]
        in_maps.append(m)
    res = run_bass_kernel_spmd(nc, in_maps, core_ids=list(range(8)))
    return np.stack([np.asarray(r["out"], dtype=np.float32) for r in res.results], axis=0)
```

```python
import numpy as np
import ml_dtypes
import concourse.bass as bass
import concourse.mybir as mybir
from concourse.bass_utils import run_bass_kernel_spmd
from contextlib import ExitStack

F32 = mybir.dt.float32
BF16 = mybir.dt.bfloat16
U8 = mybir.dt.uint8
ALU = mybir.AluOpType
AF = mybir.ActivationFunctionType
AX = mybir.AxisListType

S_LEN = 4096
D = 1024
DFF = 4096
EPS = 1e-6
ARENA_BYTES = 206 * 1024


class T:
    __slots__ = ("w", "r")

    def __init__(self):
        self.w = []
        self.r = []


class Sched:
    ENG = ("pe", "act", "dve", "pool", "sp")
    N_DMA_SEMS = 32

    def __init__(self, nc, es):
        self.nc = nc
        self.ops = {e: [] for e in self.ENG}
        self.sem = {e: es.enter_context(nc.semaphore("s_" + e)) for e in ("pe", "act", "dve", "pool")}
        self.dsem = [es.enter_context(nc.semaphore("d%d" % i)) for i in range(self.N_DMA_SEMS)]
        self.dcnt = [0] * self.N_DMA_SEMS
        self.dnext = 0
        self.cnt = {e: 0 for e in self.ENG}
        self.out_tokens = []
        self.barrier_deps = []

    @staticmethod
    def _joinable(t):
        return bool(t.w) and not t.r and all(w[0] == "d" for w in t.w)

    def _deps(self, reads, writes, is_dma=False):
        deps = list(self.barrier_deps)
        for t in reads:
            deps.extend(t.w)
        for t in writes:
            if not (is_dma and self._joinable(t)):
                deps.extend(t.w)
            deps.extend(t.r)
        return deps

    def _mark(self, tok, reads, writes, is_dma=False):
        for t in reads:
            t.r.append(tok)
            if len(t.r) > 64:
                t.r = self._compact(t.r)
        for t in writes:
            if is_dma and self._joinable(t):
                t.w = t.w + [tok]
            else:
                t.w = [tok]
            t.r = []

    @staticmethod
    def _compact(toks):
        best = {}
        for d in toks:
            k = (d[0], d[1])
            if k not in best or best[k][2] < d[2]:
                best[k] = d
        return list(best.values())

    def op(self, eng, fn, reads=(), writes=()):
        deps = self._deps(reads, writes)
        self.cnt[eng] += 1
        tok = ("c", eng, self.cnt[eng])
        self.ops[eng].append((fn, deps, tok))
        self._mark(tok, reads, writes)
        return tok

    def dma(self, q, fn, reads=(), writes=(), is_out=False):
        deps = self._deps(reads, writes, True)
        k = self.dnext
        self.dnext = (self.dnext + 1) % self.N_DMA_SEMS
        if self.dcnt[k]:
            deps.append(("d", k, self.dcnt[k]))
        self.dcnt[k] += 16
        tok = ("d", k, self.dcnt[k])
        self.ops[q].append((fn, deps, tok))
        self._mark(tok, reads, writes, True)
        if is_out:
            self.out_tokens.append(tok)
        return tok

    def barrier(self):
        deps = [("c", e, self.cnt[e]) for e in ("pe", "act", "dve", "pool") if self.cnt[e]]
        deps += [("d", k, self.dcnt[k]) for k in range(self.N_DMA_SEMS) if self.dcnt[k]]
        self.barrier_deps = deps

    def emit(self):
        nc = self.nc
        final_deps = list(self.out_tokens)
        with nc.Block() as block:
            def run(engname, e):
                waited = {}

                def do_wait(tok):
                    if tok[0] == "c":
                        if tok[1] == engname and engname == "pe":
                            return
                        key = ("c", tok[1]); sem = self.sem[tok[1]]
                    else:
                        key = ("d", tok[1]); sem = self.dsem[tok[1]]
                    if waited.get(key, 0) >= tok[2]:
                        return
                    waited[key] = tok[2]
                    e.wait_ge(sem, tok[2])

                for fn, deps, tok in self.ops[engname]:
                    for d in self._compact(deps):
                        do_wait(d)
                    ins = fn(e)
                    if tok[0] == "c":
                        ins.then_inc(self.sem[tok[1]], 1)
                    else:
                        ins.then_inc(self.dsem[tok[1]], 16)
                if engname == "sp":
                    for d in final_deps:
                        do_wait(d)

            @block.sync
            def _(e):
                run("sp", e)

            @block.tensor
            def _(e):
                run("pe", e)

            @block.scalar
            def _(e):
                run("act", e)

            @block.vector
            def _(e):
                run("dve", e)

            @block.gpsimd
            def _(e):
                run("pool", e)


class Arena:
    def __init__(self, ap):
        self.ap = ap
        self.off = 0
        self.top = ARENA_BYTES

    def alloc(self, shape, dt, top=False):
        esz = 2 if dt == BF16 else 4
        n = int(np.prod(shape[1:]))
        nb = n * esz
        nba = (nb + 63) // 64 * 64
        assert self.off + nba <= self.top, ("SBUF arena overflow", self.off, self.top, nb)
        if top:
            self.top -= nba
            v = self.ap[:, self.top:self.top + nb].bitcast(dt)
        else:
            v = self.ap[:, self.off:self.off + nb].bitcast(dt)
            self.off += nba
        if len(shape) > 2:
            names = " ".join("a%d" % i for i in range(len(shape) - 1))
            kw = {"a%d" % i: shape[i + 1] for i in range(len(shape) - 2)}
            v = v.rearrange("p (%s) -> p %s" % (names, names), **kw)
        return v

    def mark(self):
        return self.off

    def release(self, m):
        self.off = m

    def release_top(self):
        self.top = ARENA_BYTES


class Ctx:
    pass


def dump(K, name, ap, reads):
    if not getattr(K, "dbg", False):
        return
    t = K.nc.dram_tensor("dbg_" + name, list(ap.shape), ap.dtype, kind="ExternalOutput").ap()
    K.S.dma("sp", lambda e: e.dma_start(out=t, in_=ap), reads=reads, is_out=True)


def psum_bf16(ps_ap):
    return ps_ap.bitcast(BF16)


def emit_rmsnorm_stats(K, xt_ap, nsub, ss, rstd, t_x, t_ss, junk):
    S = K.S
    for s in range(nsub):
        S.op("act", lambda e, s=s: e.activation(out=junk, in_=xt_ap[:, s, :], func=AF.Square, accum_out=ss[:, s:s + 1]),
             reads=[t_x], writes=[t_ss])
    S.op("dve", lambda e: e.tensor_scalar(out=rstd[:, 0:nsub], in0=ss[:, 0:nsub], scalar1=1.0 / D, scalar2=EPS, op0=ALU.mult, op1=ALU.add),
         reads=[t_ss], writes=[t_ss])
    S.op("act", lambda e: e.activation(out=rstd[:, 0:nsub], in_=rstd[:, 0:nsub], func=AF.Sqrt), reads=[t_ss], writes=[t_ss])
    S.op("dve", lambda e: e.reciprocal(out=rstd[:, 0:nsub], in_=rstd[:, 0:nsub]), reads=[t_ss], writes=[t_ss])


def load_bcast_row(K, q, dst, src_row, t_dst):
    n = dst.shape[1]
    K.S.dma(q, lambda e: e.dma_start(out=dst, in_=src_row.unsqueeze(0).to_broadcast([128, n])), writes=[t_dst])


def make_cast_jobs(K, layers):
    jobs = []
    for l in layers:
        w1v = K.w1[l].rearrange("(dc p) f -> p dc f", p=128)
        w1bv = K.w1b[l].rearrange("(dc p) f -> p dc f", p=128)
        for c in range(16):
            jobs.append((w1v[:, :, c * 256:(c + 1) * 256], w1bv[:, :, c * 256:(c + 1) * 256], [128, 8, 256], K.t_w1b[l][c]))
    for l in layers:
        w2v = K.w2[l].rearrange("(fc p) d -> p fc d", p=128)
        w2bv = K.w2b[l].rearrange("(fc p) d -> p fc d", p=128)
        for c in range(16):
            jobs.append((w2v[:, 2 * c:2 * c + 2, :], w2bv[:, 2 * c:2 * c + 2, :], [128, 2, 1024], K.t_w2b[l][c]))
    return jobs


class Caster:
    def __init__(self, K, jobs):
        self.K, self.jobs, self.i = K, jobs, 0
        A = K.A
        self.sf = [A.alloc([128, 2048], F32) for _ in range(2)]
        self.sb = [A.alloc([128, 2048], BF16) for _ in range(2)]
        self.tf = [T(), T()]
        self.tb = [T(), T()]

    def step(self, n=1):
        S = self.K.S
        for _ in range(n):
            if self.i >= len(self.jobs):
                return
            src, dst, shp, tdst = self.jobs[self.i]
            b = self.i % 2
            self.i += 1
            names = "p (a b) -> p a b"
            sf = self.sf[b].rearrange(names, a=shp[1]); sb = self.sb[b].rearrange(names, a=shp[1])
            S.dma("sp", lambda e, sf=sf, src=src: e.dma_start(out=sf, in_=src), writes=[self.tf[b]])
            S.op("dve", lambda e, sf=sf, sb=sb: e.tensor_copy(out=sb, in_=sf), reads=[self.tf[b]], writes=[self.tb[b]])
            S.dma("sp", lambda e, sb=sb, dst=dst: e.dma_start(out=dst, in_=sb), reads=[self.tb[b]], writes=[tdst])

    def flush(self):
        self.step(len(self.jobs))


def phase_mlp(K, xin, xout, layer, gain, final_gain=None):
    S, A, nc = K.S, K.A, K.nc
    m0 = A.mark()
    TT = 256
    NT = S_LEN // TT
    W1b = A.alloc([128, 8, DFF], BF16)
    W2b = A.alloc([128, 32, D], BF16)
    xt = [A.alloc([128, 2, D], F32) for _ in range(2)]
    hb = A.alloc([128, 2, D], BF16)
    hT = [A.alloc([128, 8, TT], BF16) for _ in range(2)]
    a2T = A.alloc([128, 32, TT], BF16)
    rt = [A.alloc([128, TT], F32) for _ in range(2)]
    gt = A.alloc([128, D], F32)
    gft = A.alloc([128, D], F32) if final_gain is not None else None
    junk = A.alloc([128, D], BF16)
    ss = A.alloc([128, 4], F32)
    rstd = A.alloc([128, 4], F32)
    t_w1 = [T() for _ in range(16)]
    t_w2 = [T() for _ in range(16)]
    t_xt = [T(), T()]
    t_hb, t_hT, t_a2 = T(), [T(), T()], [T() for _ in range(32)]
    t_rt = [T(), T()]
    t_g, t_ss, t_junk = T(), T(), T()
    t_psT, t_psA, t_psB = [T(), T()], [T(), T()], [T(), T()]
    ps = K.ps
    psT = [psum_bf16(ps[:, 0:512]).rearrange("p (a b) -> p a b", a=4), psum_bf16(ps[:, 512:1024]).rearrange("p (a b) -> p a b", a=4)]
    psA = [ps[:, 1024:1024 + TT], ps[:, 1536:1536 + TT]]
    psB = [ps[:, 2048:2560], ps[:, 2560:3072]]

    load_bcast_row(K, "sp", gt, gain, t_g)
    if gft is not None:
        load_bcast_row(K, "sp", gft, final_gain, t_g)

    w1bv = K.w1b[layer].rearrange("(dc p) f -> p dc f", p=128)
    w2bv = K.w2b[layer].rearrange("(fc p) d -> p fc d", p=128)
    for c in range(16):
        S.dma("pool", lambda e, c=c: e.dma_start(out=W1b[:, :, c * 256:(c + 1) * 256], in_=w1bv[:, :, c * 256:(c + 1) * 256]),
              reads=[K.t_w1b[layer][c]], writes=[t_w1[c]])
    for c in range(16):
        S.dma("pool", lambda e, c=c: e.dma_start(out=W2b[:, 2 * c:2 * c + 2, :], in_=w2bv[:, 2 * c:2 * c + 2, :]),
              reads=[K.t_w2b[layer][c]], writes=[t_w2[c]])

    xin_v = xin.rearrange("(n s p) d -> n p s d", s=2, p=128)
    xout_v = xout.rearrange("(n s p) d -> n p s d", s=2, p=128)

    def load_x(n):
        b = n % 2
        S.dma("sp", lambda e: e.dma_start(out=xt[b], in_=xin_v[n]), writes=[t_xt[b]])

    load_x(0)
    load_x(1)
    def do_tile(n):
        b = n % 2
        x_ = xt[b]
        emit_rmsnorm_stats(K, x_, 2, ss, rstd, t_xt[b], t_ss, junk)
        for s in range(2):
            S.op("dve", lambda e, s=s: e.scalar_tensor_tensor(out=hb[:, s, :], in0=x_[:, s, :], scalar=rstd[:, s:s + 1], in1=gt,
                                                               op0=ALU.mult, op1=ALU.mult), reads=[t_xt[b], t_ss, t_g], writes=[t_hb])
        for half in range(2):
            for dcl in range(4):
                dc = half * 4 + dcl
                for s in range(2):
                    S.op("pe", lambda e, half=half, dcl=dcl, dc=dc, s=s: e.transpose(
                        psT[half][:, dcl, s * 128:(s + 1) * 128], hb[:, s, dc * 128:(dc + 1) * 128], K.identb),
                        reads=[t_hb, K.t_const], writes=[t_psT[half]])
            eng = "act" if half == 0 else "dve"
            if eng == "act":
                S.op("act", lambda e, half=half: e.activation(out=hT[b][:, half * 4:half * 4 + 4, :], in_=psT[half], func=AF.Copy),
                     reads=[t_psT[half]], writes=[t_hT[b]])
            else:
                S.op("dve", lambda e, half=half: e.tensor_copy(out=hT[b][:, half * 4:half * 4 + 4, :], in_=psT[half]),
                     reads=[t_psT[half]], writes=[t_hT[b]])
        for fc in range(32):
            pb = fc % 2
            for dc in range(8):
                S.op("pe", lambda e, fc=fc, dc=dc, pb=pb: e.matmul(psA[pb], W1b[:, dc, fc * 128:(fc + 1) * 128], hT[b][:, dc, :],
                                                                  start=(dc == 0), stop=(dc == 7)),
                     reads=[t_w1[fc // 2], t_hT[b]], writes=[t_psA[pb]])
            S.op("act", lambda e, pb=pb: e.activation(out=rt[pb], in_=psA[pb], func=AF.Relu), reads=[t_psA[pb]], writes=[t_rt[pb]])
            S.op("dve", lambda e, fc=fc, pb=pb: e.tensor_tensor(out=a2T[:, fc, :], in0=rt[pb], in1=rt[pb], op=ALU.mult),
                 reads=[t_rt[pb]], writes=[t_a2[fc]])
        k = 0
        for s in range(2):
            for dh in range(2):
                pb = k % 2; k += 1
                for fc in range(32):
                    S.op("pe", lambda e, fc=fc, s=s, dh=dh, pb=pb: e.matmul(psB[pb], a2T[:, fc, s * 128:(s + 1) * 128],
                                                                          W2b[:, fc, dh * 512:(dh + 1) * 512], start=(fc == 0), stop=(fc == 31)),
                         reads=[t_a2[fc], t_w2[fc // 2]], writes=[t_psB[pb]])
                S.op("dve", lambda e, s=s, dh=dh, pb=pb: e.tensor_tensor(out=x_[:, s, dh * 512:(dh + 1) * 512], in0=psB[pb],
                                                                       in1=x_[:, s, dh * 512:(dh + 1) * 512], op=ALU.add),
                     reads=[t_psB[pb]], writes=[t_xt[b]])
        if final_gain is not None:
            emit_rmsnorm_stats(K, x_, 2, ss[:, 2:4], rstd[:, 2:4], t_xt[b], t_ss, junk)
            for s in range(2):
                S.op("dve", lambda e, s=s: e.scalar_tensor_tensor(out=x_[:, s, :], in0=x_[:, s, :], scalar=rstd[:, 2 + s:3 + s], in1=gft,
                                                                   op0=ALU.mult, op1=ALU.mult), reads=[t_ss, t_g], writes=[t_xt[b]])
        S.dma("sp", lambda e, n=n: e.dma_start(out=xout_v[n], in_=x_), reads=[t_xt[b]], is_out=K.is_out(xout))
        if n + 2 < NT:
            load_x(n + 2)

    for n in range(NT):
        do_tile(n)
    S.barrier()
    A.release(m0)


def phase_mix0(K, xin, xout, P):
    S, A, ps = K.S, K.A, K.ps
    m_phase = A.mark()
    PS = lambda b: ps[:, b * 512:(b + 1) * 512]
    PSB = lambda b: psum_bf16(ps[:, b * 512:(b + 1) * 512])
    t_bank = [T() for _ in range(8)]
    qT = A.alloc([128, 4, S_LEN], BF16)
    kT = A.alloc([128, 2, S_LEN], BF16)
    Vaug = A.alloc([128, 32, 2, 128], BF16)
    t_qT, t_kT, t_V = T(), T(), T()
    Wq = A.alloc([128, 8, 512], BF16)
    Wkv = A.alloc([128, 8, 256], BF16)
    WY = [A.alloc([128, 8, 512], BF16) for _ in range(2)]
    gqk = A.alloc([128, 2], F32)
    cm = {}
    for nm in ("rmat", "onesblk", "bdc", "bds", "bdsn", "c64bd", "s64bdn"):
        cm[nm] = A.alloc([128, 128], BF16)
    t_w = T()
    for nm in cm:
        S.dma("sp", lambda e, nm=nm: e.dma_start(out=cm[nm], in_=P[nm]), writes=[t_w])
    for col, src in ((0, P["q_norm"]), (1, P["k_norm"])):
        for hh in range(2):
            S.dma("sp", lambda e, col=col, src=src, hh=hh: e.dma_start(out=gqk[hh * 64:(hh + 1) * 64, col:col + 1], in_=src.unsqueeze(1)),
                  writes=[t_w])
    S.op("pool", lambda e: e.memset(kT[64:128, 0, :], 0.0), writes=[t_kT])
    S.op("pool", lambda e: e.memset(kT[0:64, 1, :], 0.0), writes=[t_kT])
    S.op("pool", lambda e: e.memset(Vaug[:, :, 0, 64:128], 1.0), writes=[t_V])
    S.op("pool", lambda e: e.memset(Vaug[:, :, 1, 0:64], 1.0), writes=[t_V])
    m_prep = A.mark()

    stg = A.alloc([128, 8, 512], F32)
    WAb = A.alloc([128, 8, 512], BF16)
    WAT = A.alloc([128, 4, D], BF16)
    Wf2 = A.alloc([128, 4, 128], F32)
    Wf2b = A.alloc([128, 4, 128], BF16)
    BDG = [A.alloc([128, 4, 128], BF16) for _ in range(2)]
    t_stg, t_wab, t_wat, t_wf, t_bdg = T(), T(), T(), T(), T()
    w_in = P["w_in"]
    for c in range(4):
        for hh in range(2):
            col = 512 + 64 * (c + 4 * hh)
            S.dma("sp", lambda e, c=c, hh=hh, col=col: e.dma_start(out=stg[:, :, c * 128 + hh * 64:c * 128 + hh * 64 + 64],
                                                                   in_=w_in[:, col:col + 64].rearrange("(dc p) d -> p dc d", p=128)),
                  writes=[t_stg])
    S.op("dve", lambda e: e.tensor_copy(out=Wq, in_=stg), reads=[t_stg], writes=[t_w])
    S.dma("sp", lambda e: e.dma_start(out=stg[:, :, 0:256], in_=w_in[:, 1024:1280].rearrange("(dc p) f -> p dc f", p=128)),
          writes=[t_stg])
    S.op("dve", lambda e: e.tensor_copy(out=Wkv, in_=stg[:, :, 0:256]), reads=[t_stg], writes=[t_w])
    S.dma("sp", lambda e: e.dma_start(out=stg, in_=w_in[:, 0:512].rearrange("(dc p) f -> p dc f", p=128)), writes=[t_stg])
    S.op("dve", lambda e: e.tensor_copy(out=WAb, in_=stg), reads=[t_stg], writes=[t_wab])
    for cc in range(4):
        bk = cc % 2
        for dc in range(8):
            S.op("pe", lambda e, cc=cc, dc=dc, bk=bk: e.transpose(PSB(bk)[:, dc * 128:(dc + 1) * 128], WAb[:, dc, cc * 128:(cc + 1) * 128], K.identb),
                 reads=[t_wab, K.t_const], writes=[t_bank[bk]])
        S.op("act", lambda e, cc=cc, bk=bk: e.activation(out=WAT[:, cc, :], in_=PSB(bk), func=AF.Copy), reads=[t_bank[bk]], writes=[t_wat])
    wf = P["w_fnet"].rearrange("(ch hh) c d -> hh c ch d", hh=2)
    S.op("pool", lambda e: e.memset(Wf2, 0.0), writes=[t_wf])
    for hh in range(2):
        S.dma("sp", lambda e, hh=hh: e.dma_start(out=Wf2[hh * 64:(hh + 1) * 64, :, hh * 64:(hh + 1) * 64], in_=wf[hh]), writes=[t_wf])
    S.op("pool", lambda e: e.tensor_copy(out=Wf2b, in_=Wf2), reads=[t_wf], writes=[t_wf])
    for gi, nm in enumerate(("c64bd", "s64bdn")):
        bk = 2 + gi
        for ch in range(4):
            S.op("pe", lambda e, ch=ch, bk=bk, nm=nm: e.matmul(PS(bk)[:, ch * 128:(ch + 1) * 128], cm[nm], Wf2b[:, ch, :], start=True, stop=True),
                 reads=[t_w, t_wf], writes=[t_bank[bk]])
        S.op("dve", lambda e, gi=gi, bk=bk: e.tensor_copy(out=BDG[gi].rearrange("p a b -> p (a b)"), in_=PS(bk)), reads=[t_bank[bk]], writes=[t_bdg])
    k = 0
    for gi in range(2):
        for dc in range(8):
            bk = 4 + (k % 2); k += 1
            for ch in range(4):
                S.op("pe", lambda e, gi=gi, dc=dc, ch=ch, bk=bk: e.matmul(PS(bk)[:, ch * 128:(ch + 1) * 128], WAT[:, ch, dc * 128:(dc + 1) * 128],
                                                                       BDG[gi][:, ch, :], start=True, stop=True),
                     reads=[t_wat, t_bdg], writes=[t_bank[bk]])
            eng = "act" if k % 2 else "dve"
            if eng == "act":
                S.op("act", lambda e, gi=gi, dc=dc, bk=bk: e.activation(out=WY[gi][:, dc, :], in_=PS(bk), func=AF.Copy), reads=[t_bank[bk]], writes=[t_w])
            else:
                S.op("dve", lambda e, gi=gi, dc=dc, bk=bk: e.tensor_copy(out=WY[gi][:, dc, :], in_=PS(bk)), reads=[t_bank[bk]], writes=[t_w])
    S.barrier()
    A.release(m_prep)

    xt = [A.alloc([128, 2, D], F32) for _ in range(2)]
    hb = A.alloc([128, 2, D], BF16)
    hT = [A.alloc([128, 8, 512], BF16) for _ in range(2)]
    gt = A.alloc([128, D], F32)
    junk = A.alloc([128, D], BF16)
    ss = A.alloc([128, 4], F32); rstd = A.alloc([128, 4], F32)
    cst = A.alloc([128, 2, 512], F32)
    sq = [A.alloc([128, 512], BF16) for _ in range(2)]
    rq = [A.alloc([128, 512], F32) for _ in range(2)]
    qn = [A.alloc([128, 512], F32) for _ in range(2)]
    qnb = [A.alloc([128, 512], BF16) for _ in range(2)]
    Ys = [[A.alloc([128, 512], BF16) for _ in range(2)] for _ in range(2)]
    PQs = [A.alloc([128, 2, 512], BF16) for _ in range(2)]
    t_xt, t_hb, t_hT, t_g, t_ss, t_cs = [T(), T()], T(), [T(), T()], T(), T(), T()
    t_sq, t_rq, t_qn, t_qnb = [T(), T()], [T(), T()], [T(), T()], [T(), T()]
    t_Ys, t_PQs, t_pqh = [[T(), T()], [T(), T()]], [T(), T()], T()
    load_bcast_row(K, "sp", gt, P["norm_mix0"], t_g)
    xv = xin.rearrange("(a j b4) d -> j b4 a d", j=32, b4=4)
    xov = xout.rearrange("(a j b4) d -> j b4 a d", j=32, b4=4)
    pqh = P["pqh"]
    pqh_w = pqh.rearrange("(j p) q f -> j p q f", p=128)

    def load_x(h):
        b = h % 2
        for s in range(2):
            for b4 in range(4):
                S.dma("sp", lambda e, s=s, b4=b4: e.dma_start(out=xt[b][b4 * 32:(b4 + 1) * 32, s, :], in_=xv[2 * h + s, b4]), writes=[t_xt[b]])

    def norm_T(h):
        b = h % 2
        st, off = h // 2, (h % 2) * 256
        emit_rmsnorm_stats(K, xt[b], 2, ss, rstd, t_xt[b], t_ss, junk)
        for s in range(2):
            S.op("dve", lambda e, s=s: e.scalar_tensor_tensor(out=hb[:, s, :], in0=xt[b][:, s, :], scalar=rstd[:, s:s + 1], in1=gt,
                                                               op0=ALU.mult, op1=ALU.mult), reads=[t_xt[b], t_ss, t_g], writes=[t_hb])
        hTs = hT[st % 2]
        for half in range(2):
            pv = PSB(half).rearrange("p (a b) -> p a b", a=4)
            for dcl in range(4):
                dc = half * 4 + dcl
                for s in range(2):
                    S.op("pe", lambda e, pv=pv, dcl=dcl, dc=dc, s=s: e.transpose(pv[:, dcl, s * 128:(s + 1) * 128], hb[:, s, dc * 128:(dc + 1) * 128], K.identb),
                         reads=[t_hb, K.t_const], writes=[t_bank[half]])
            if half == 0:
                S.op("act", lambda e, pv=pv: e.activation(out=hTs[:, 0:4, off:off + 256], in_=pv, func=AF.Copy), reads=[t_bank[0]], writes=[t_hT[st % 2]])
            else:
                S.op("dve", lambda e, pv=pv: e.tensor_copy(out=hTs[:, 4:8, off:off + 256], in_=pv), reads=[t_bank[1]], writes=[t_hT[st % 2]])

    def supertile(st):
        hTs, thT = hT[st % 2], t_hT[st % 2]
        n0 = st * 512
        S.dma("sp", lambda e: e.dma_start(out=cst[:, 0, :], in_=P["cosp"][:, n0:n0 + 512]), writes=[t_cs])
        S.dma("sp", lambda e: e.dma_start(out=cst[:, 1, :], in_=P["sinp"][:, n0:n0 + 512]), writes=[t_cs])

        def step1(i):
            bk = 2 + i % 2
            for dc in range(8):
                lhsT = Wq[:, dc, i * 128:(i + 1) * 128] if i < 4 else Wkv[:, dc, 0:128]
                S.op("pe", lambda e, lhsT=lhsT, dc=dc, bk=bk: e.matmul(PS(bk), lhsT, hTs[:, dc, :], start=(dc == 0), stop=(dc == 7)),
                     reads=[t_w, thT], writes=[t_bank[bk]])
            S.op("act", lambda e, bk=bk, i=i: e.activation(out=sq[i % 2], in_=PS(bk), func=AF.Square), reads=[t_bank[bk]], writes=[t_sq[i % 2]])

        def step2(i):
            bk = 2 + i % 2
            j = i % 2
            S.op("pe", lambda e: e.matmul(PS(4), cm["onesblk"], sq[j], start=True, stop=True), reads=[t_sq[j], t_w], writes=[t_bank[4]])
            S.op("act", lambda e: e.activation(out=rq[j], in_=PS(4), func=AF.Sqrt, bias=K.eps_ap, scale=1.0), reads=[t_bank[4], K.t_const], writes=[t_rq[j]])
            S.op("dve", lambda e: e.reciprocal(out=rq[j], in_=rq[j]), reads=[t_rq[j]], writes=[t_rq[j]])
            gcol = gqk[:, 0:1] if i < 4 else gqk[:, 1:2]
            S.op("dve", lambda e: e.scalar_tensor_tensor(out=qn[j], in0=PS(bk), scalar=gcol, in1=rq[j], op0=ALU.mult, op1=ALU.mult),
                 reads=[t_bank[bk], t_rq[j], t_w], writes=[t_qn[j]])
            S.op("pool", lambda e: e.tensor_copy(out=qnb[j], in_=qn[j]), reads=[t_qn[j]], writes=[t_qnb[j]])
            S.op("dve", lambda e: e.tensor_tensor(out=qn[j], in0=qn[j], in1=cst[:, 0, :], op=ALU.mult), reads=[t_cs, t_qnb[j]], writes=[t_qn[j]])

        def step3(i):
            j = i % 2
            S.op("pe", lambda e: e.matmul(PS(5), cm["rmat"], qnb[j], start=True, stop=True), reads=[t_qnb[j], t_w], writes=[t_bank[5]])
            S.op("dve", lambda e: e.tensor_tensor(out=rq[j], in0=PS(5), in1=cst[:, 1, :], op=ALU.mult), reads=[t_bank[5], t_cs], writes=[t_rq[j]])
            if i < 4:
                S.op("pool", lambda e: e.tensor_tensor(out=qT[:, i, n0:n0 + 512], in0=qn[j], in1=rq[j], op=ALU.add), reads=[t_qn[j], t_rq[j]], writes=[t_qT])
            else:
                for kv in range(2):
                    S.op("pool", lambda e, kv=kv: e.tensor_tensor(out=kT[kv * 64:(kv + 1) * 64, kv, n0:n0 + 512], in0=qn[j][kv * 64:(kv + 1) * 64, :],
                                                                  in1=rq[j][kv * 64:(kv + 1) * 64, :], op=ALU.add), reads=[t_qn[j], t_rq[j]], writes=[t_kT])

        for kk in range(7):
            if kk < 5:
                step1(kk)
            if 0 <= kk - 1 < 5:
                step2(kk - 1)
            if 0 <= kk - 2 < 5:
                step3(kk - 2)
        for tl in range(4):
            jt = st * 4 + tl
            for dc in range(8):
                S.op("pe", lambda e, dc=dc, tl=tl: e.matmul(PS(6)[:, 0:128], hTs[:, dc, tl * 128:(tl + 1) * 128], Wkv[:, dc, 128:256],
                                                           start=(dc == 0), stop=(dc == 7)), reads=[t_w, thT], writes=[t_bank[6]])
            S.op("act", lambda e, jt=jt: e.activation(out=Vaug[:, jt, 0, 0:64], in_=PS(6)[:, 0:64], func=AF.Copy), reads=[t_bank[6]], writes=[t_V])
            S.op("act", lambda e, jt=jt: e.activation(out=Vaug[:, jt, 1, 64:128], in_=PS(6)[:, 64:128], func=AF.Copy), reads=[t_bank[6]], writes=[t_V])
        for tl in range(4):
            jt = st * 4 + tl
            r = jt % 2
            for gi in range(2):
                bk = 6 + gi
                for dc in range(8):
                    S.op("pe", lambda e, gi=gi, dc=dc, tl=tl, bk=bk: e.matmul(PS(bk), hTs[:, dc, tl * 128:(tl + 1) * 128], WY[gi][:, dc, :],
                                                                          start=(dc == 0), stop=(dc == 7)), reads=[t_w, thT], writes=[t_bank[bk]])
                if gi == 0:
                    S.op("act", lambda e, r=r, bk=bk: e.activation(out=Ys[0][r], in_=PS(bk), func=AF.Copy), reads=[t_bank[bk]], writes=[t_Ys[0][r]])
                else:
                    S.op("dve", lambda e, r=r, bk=bk: e.tensor_copy(out=Ys[1][r], in_=PS(bk)), reads=[t_bank[bk]], writes=[t_Ys[1][r]])
            S.op("pe", lambda e, r=r: e.matmul(PS(4), cm["bdc"], Ys[0][r], start=True, stop=False), reads=[t_w, t_Ys[0][r]], writes=[t_bank[4]])
            S.op("pe", lambda e, r=r: e.matmul(PS(4), cm["bds"], Ys[1][r], start=False, stop=True), reads=[t_w, t_Ys[1][r]], writes=[t_bank[4]])
            S.op("pe", lambda e, r=r: e.matmul(PS(5), cm["bdc"], Ys[1][r], start=True, stop=False), reads=[t_w, t_Ys[1][r]], writes=[t_bank[5]])
            S.op("pe", lambda e, r=r: e.matmul(PS(5), cm["bdsn"], Ys[0][r], start=False, stop=True), reads=[t_w, t_Ys[0][r]], writes=[t_bank[5]])
            S.op("act", lambda e, r=r: e.activation(out=PQs[r][:, 0, :], in_=PS(4), func=AF.Copy), reads=[t_bank[4]], writes=[t_PQs[r]])
            S.op("dve", lambda e, r=r: e.tensor_copy(out=PQs[r][:, 1, :], in_=PS(5)), reads=[t_bank[5]], writes=[t_PQs[r]])
            S.dma("sp", lambda e, r=r, jt=jt: e.dma_start(out=pqh_w[jt], in_=PQs[r]), reads=[t_PQs[r]], writes=[t_pqh])

    load_x(0)
    load_x(1)
    nh = [0]

    def prep_half():
        h = nh[0]; nh[0] += 1
        if h < 16:
            norm_T(h)
            if h + 2 < 16:
                load_x(h + 2)

    prep_half(); prep_half()
    for st in range(8):
        prep_half(); prep_half()
        supertile(st)
    S.barrier()
    A.release(m_prep)

    faT = A.alloc([128, 4, S_LEN], BF16)
    attT = A.alloc([128, 4, S_LEN], BF16)
    t_fa, t_att = T(), T()
    m_C = A.mark()
    CB = A.alloc([128, S_LEN], BF16); SB = A.alloc([128, S_LEN], BF16)
    Pp = [A.alloc([128, 4, 2, 512], BF16) for _ in range(2)]
    t_cb, t_Pp = T(), [T(), T()]
    S.dma("sp", lambda e: e.dma_start(out=CB, in_=P["cbt"]), writes=[t_cb])
    S.dma("sp", lambda e: e.dma_start(out=SB, in_=P["sbt"]), writes=[t_cb])
    pqh_r = pqh.rearrange("(b tau) q f -> b tau q f", tau=32)
    k = 0
    for tg in range(8):
        r = tg % 2
        S.dma("sp", lambda e, tg=tg, r=r: e.dma_start(out=Pp[r], in_=pqh_r[:, 4 * tg:4 * tg + 4]), reads=[t_pqh], writes=[t_Pp[r]])
        for fc in range(4):
            bk = k % 2; k += 1
            for tl in range(4):
                tau = 4 * tg + tl
                S.op("pe", lambda e, r=r, fc=fc, tl=tl, tau=tau, bk=bk: e.matmul(PS(bk)[:, tl * 128:(tl + 1) * 128], Pp[r][:, tl, 0, fc * 128:(fc + 1) * 128],
                                                                             CB[:, tau::32], start=True, stop=False), reads=[t_Pp[r], t_cb], writes=[t_bank[bk]])
                S.op("pe", lambda e, r=r, fc=fc, tl=tl, tau=tau, bk=bk: e.matmul(PS(bk)[:, tl * 128:(tl + 1) * 128], Pp[r][:, tl, 1, fc * 128:(fc + 1) * 128],
                                                                             SB[:, tau::32], start=False, stop=True), reads=[t_Pp[r], t_cb], writes=[t_bank[bk]])
            src = PS(bk).rearrange("p (tl mq mr) -> p tl mr mq", tl=4, mq=32, mr=4)
            dst = faT[:, fc, :].rearrange("p (mr tau mq) -> p tau mr mq", mr=4, tau=32, mq=32)[:, 4 * tg:4 * tg + 4]
            if k % 2:
                S.op("act", lambda e, src=src, dst=dst: e.activation(out=dst, in_=src, func=AF.Copy), reads=[t_bank[bk]], writes=[t_fa])
            else:
                S.op("dve", lambda e, src=src, dst=dst: e.tensor_copy(out=dst, in_=src), reads=[t_bank[bk]], writes=[t_fa])
    S.barrier()
    A.release(m_C)

    pT = [A.alloc([128, 1024], BF16) for _ in range(3)]
    rec = [A.alloc([128, 512], F32) for _ in range(2)]
    t_pT, t_rec = [T(), T(), T()], [T(), T()]
    t_sp = [T(), T()]
    caster = Caster(K, K.cast_jobs)
    PSP = lambda a: ps[:, a * 1024:(a + 1) * 1024]
    it = 0
    for c in range(4):
        for hh in range(2):
            lo, hi = hh * 64, hh * 64 + 64
            for qt in range(8):
                ob = 4 + it % 2
                def score(kp, c=c, qt=qt, hh=hh):
                    a = kp % 2
                    for u in range(2):
                        kt = 2 * kp + u
                        S.op("pe", lambda e, kt=kt, u=u: e.matmul(PSP(a)[:, u * 512:(u + 1) * 512], kT[:, hh, kt * 128:(kt + 1) * 128], qT[:, c, qt * 512:(qt + 1) * 512],
                                                                 start=True, stop=True), reads=[t_kT, t_qT], writes=[t_sp[a]])
                score(0)
                for kp in range(16):
                    a = kp % 2
                    pb = kp % 3
                    if kp + 1 < 16:
                        score(kp + 1)
                    for u in range(2):
                        S.op("act", lambda e, a=a, pb=pb, u=u: e.activation(out=pT[pb][:, u * 512:(u + 1) * 512], in_=PSP(a)[:, u * 512:(u + 1) * 512], func=AF.Exp, scale=0.125),
                             reads=[t_sp[a]], writes=[t_pT[pb]])
                    for u in range(2):
                        kt = 2 * kp + u
                        S.op("pe", lambda e, kt=kt, u=u, pb=pb, hh=hh, ob=ob: e.matmul(PS(ob), Vaug[:, kt, hh, :], pT[pb][:, u * 512:(u + 1) * 512], start=(kt == 0), stop=(kt == 31)),
                             reads=[t_V, t_pT[pb]], writes=[t_bank[ob]])
                rb = it % 2
                dlo, dhi = (64, 128) if hh == 0 else (0, 64)
                S.op("dve", lambda e, ob=ob, rb=rb, lo=lo, hi=hi, dlo=dlo, dhi=dhi: e.reciprocal(out=rec[rb][lo:hi, :], in_=PS(ob)[dlo:dhi, :]),
                     reads=[t_bank[ob]], writes=[t_rec[rb]])
                S.op("dve", lambda e, ob=ob, rb=rb, lo=lo, hi=hi, c=c, qt=qt: e.tensor_tensor(out=attT[lo:hi, c, qt * 512:(qt + 1) * 512], in0=PS(ob)[lo:hi, :],
                                                                                            in1=rec[rb][lo:hi, :], op=ALU.mult),
                     reads=[t_bank[ob], t_rec[rb]], writes=[t_att])
                it += 1
                caster.step(1)
    caster.flush()
    S.barrier()
    A.release(m_C)

    Wo = A.alloc([128, 8, D], BF16)
    stg2 = A.alloc([128, 4, D], F32)
    xt2 = [A.alloc([128, D], F32) for _ in range(3)]
    t_wo, t_stg2, t_xt2 = T(), T(), [T(), T(), T()]
    w_out = P["w_out"]
    S.dma("sp", lambda e: e.dma_start(out=stg2, in_=w_out[0:512, :].rearrange("(fc p) n -> p fc n", p=128)), writes=[t_stg2])
    S.op("pool", lambda e: e.tensor_copy(out=Wo[:, 0:4, :], in_=stg2), reads=[t_stg2], writes=[t_wo])
    wo_att = w_out[512:1024, :].rearrange("(hh c d) n -> hh d c n", hh=2, c=4)
    for hh in range(2):
        S.dma("sp", lambda e, hh=hh: e.dma_start(out=stg2[hh * 64:(hh + 1) * 64, :, :], in_=wo_att[hh]), writes=[t_stg2])
    S.op("pool", lambda e: e.tensor_copy(out=Wo[:, 4:8, :], in_=stg2), reads=[t_stg2], writes=[t_wo])

    def load_x2(jt):
        b = jt % 3
        for b4 in range(4):
            S.dma("sp", lambda e, b4=b4: e.dma_start(out=xt2[b][b4 * 32:(b4 + 1) * 32, :], in_=xv[jt, b4]), writes=[t_xt2[b]])

    load_x2(0)
    load_x2(1)
    k = 0
    for jt in range(32):
        b = jt % 3
        if jt + 2 < 32:
            load_x2(jt + 2)
        for dh in range(2):
            bk = k % 2; k += 1
            for fc in range(8):
                lhsT = faT[:, fc, jt * 128:(jt + 1) * 128] if fc < 4 else attT[:, fc - 4, jt * 128:(jt + 1) * 128]
                S.op("pe", lambda e, lhsT=lhsT, fc=fc, dh=dh, bk=bk: e.matmul(PS(bk), lhsT, Wo[:, fc, dh * 512:(dh + 1) * 512], start=(fc == 0), stop=(fc == 7)),
                     reads=[t_fa, t_att, t_wo], writes=[t_bank[bk]])
            S.op("dve", lambda e, b=b, dh=dh, bk=bk: e.tensor_tensor(out=xt2[b][:, dh * 512:(dh + 1) * 512], in0=PS(bk), in1=xt2[b][:, dh * 512:(dh + 1) * 512], op=ALU.add),
                 reads=[t_bank[bk]], writes=[t_xt2[b]])
        for b4 in range(4):
            S.dma("sp", lambda e, b=b, b4=b4, jt=jt: e.dma_start(out=xov[jt, b4], in_=xt2[b][b4 * 32:(b4 + 1) * 32, :]), reads=[t_xt2[b]],
                  is_out=K.is_out(xout))
    S.barrier()
    A.release(m_phase)


MAGIC = 12582912.0
TWO_PI_LO = 6.28318


def phase_mix1(K, xin, xout, P):
    S, A, ps = K.S, K.A, K.ps
    m_phase = A.mark()
    PS = lambda b: ps[:, b * 512:(b + 1) * 512]
    PSB = lambda b: psum_bf16(ps[:, b * 512:(b + 1) * 512])
    t_bank = [T() for _ in range(8)]
    Tm = A.alloc([128, 64, 128], BF16, top=True)
    Vre = A.alloc([128, 64, 128], BF16, top=True)
    Vimn = A.alloc([128, 64, 128], BF16, top=True)
    W1 = A.alloc([128, 64, 2, 128], BF16, top=True)
    r8 = A.alloc([128, 64], F32, top=True)
    y8 = A.alloc([128, 64], F32, top=True)
    iok = A.alloc([128, 512], F32, top=True)
    identf = A.alloc([128, 128], F32, top=True)
    t_U = [T() for _ in range(64)]
    t_prm = T()
    S.dma("sp", lambda e: e.dma_start(out=iok, in_=P["iotak"]), writes=[t_prm])
    S.dma("sp", lambda e: e.dma_start(out=identf, in_=P["identf"]), writes=[t_prm])
    m_prep = A.mark()

    CTr = A.alloc([128, 64, 16], F32); CTi = A.alloc([128, 64, 16], F32)
    Bbr = A.alloc([128, 64, 16], F32); Bbi = A.alloc([128, 64, 16], F32)
    S1r = A.alloc([128, 64, 8], F32); S1i = A.alloc([128, 64, 8], F32)
    S2r = A.alloc([128, 64, 8], F32); S2i = A.alloc([128, 64, 8], F32)
    S3r = A.alloc([128, 64, 8], F32); S3i = A.alloc([128, 64, 8], F32)
    Dm = A.alloc([128, 64], F32); sel = A.alloc([128, 128], F32); Dvec = A.alloc([128, 64], F32)
    mskf = A.alloc([128, 128], F32); mskb = A.alloc([128, 128], F32); dperm = A.alloc([128, 128], F32)
    m_p1 = A.mark()
    rows = A.alloc([128, 3, 128], F32)
    ldt = A.alloc([128, 2], F32)
    LR = A.alloc([128, 64], F32); LI = A.alloc([128, 64], F32); DT = A.alloc([128, 64], F32)
    BR = A.alloc([128, 64, 16], F32); BI = A.alloc([128, 64, 16], F32)
    Crow1 = A.alloc([128, 32, 128], F32)
    Crow = [Crow1, Crow1]
    t_rows, t_l, t_B, t_ct = T(), T(), T(), T()
    t_c1 = T(); t_crow = [t_c1, t_c1]
    v2 = lambda ap: ap.rearrange("dir (G2 g2) p -> (dir G2) (g2 p)", g2=2)
    S.dma("sp", lambda e: e.dma_start(out=rows[0:64, 0, :], in_=v2(P["lam_re"])), writes=[t_rows])
    S.dma("sp", lambda e: e.dma_start(out=rows[0:64, 1, :], in_=v2(P["lam_im"])), writes=[t_rows])
    S.dma("sp", lambda e: e.dma_start(out=ldt[0:64, :], in_=P["log_dt"].rearrange("dir (G2 g2) -> (dir G2) g2", g2=2)), writes=[t_rows])
    S.op("dve", lambda e: e.tensor_copy(out=rows[0:64, 2, :].rearrange("p (g2 q) -> p g2 q", g2=2),
                                        in_=ldt[0:64, :].unsqueeze(2).to_broadcast([64, 2, 64])), reads=[t_rows], writes=[t_rows])
    for i, dst in enumerate((LR, LI, DT)):
        S.op("pe", lambda e, i=i: e.transpose(PS(0)[:, i * 64:(i + 1) * 64], rows[0:64, i, :], identf[0:64, 0:64]), reads=[t_rows, t_prm], writes=[t_bank[0]])
    for i, dst in enumerate((LR, LI, DT)):
        S.op("dve", lambda e, i=i, dst=dst: e.tensor_copy(out=dst, in_=PS(0)[:, i * 64:(i + 1) * 64]), reads=[t_bank[0]], writes=[t_l])
    bsrc = lambda ap: ap.rearrange("dir (G2 g2) p c -> (g2 p) (dir G2) c", g2=2)
    S.dma("sp", lambda e: e.dma_start(out=BR, in_=bsrc(P["b_re"])), writes=[t_B])
    S.dma("sp", lambda e: e.dma_start(out=BI, in_=bsrc(P["b_im"])), writes=[t_B])
    for ri, (nm, dst) in enumerate((("c_re", CTr), ("c_im", CTi))):
        for d in range(2):
            cr = Crow[d]
            S.dma("sp", lambda e, nm=nm, d=d, cr=cr: e.dma_start(out=cr[0:16].rearrange("c G2 (g2 p) -> c G2 g2 p", g2=2),
                                                                in_=P[nm][d].rearrange("(G2 g2) c p -> c G2 g2 p", g2=2)), writes=[t_crow[d]])
            bk = 1 + d
            for G2 in range(32):
                S.op("pe", lambda e, cr=cr, G2=G2, bk=bk: e.transpose(PS(bk)[:, G2 * 16:(G2 + 1) * 16], cr[0:16, G2, :], identf[0:16, 0:16]),
                     reads=[t_crow[d], t_prm], writes=[t_bank[bk]])
            S.op("dve", lambda e, dst=dst, d=d, bk=bk: e.tensor_copy(out=dst[:, d * 32:(d + 1) * 32, :].rearrange("p a b -> p (a b)"), in_=PS(bk)),
                 reads=[t_bank[bk]], writes=[t_ct])
    t_D = T()
    S.dma("sp", lambda e: e.dma_start(out=Dm[0:16, :], in_=P["d_skip"].rearrange("(g c) -> c g", c=16), allow_slow_non_contiguous=True), writes=[t_D])
    S.dma("sp", lambda e: e.dma_start(out=sel[0:16, :], in_=P["sel16"]), writes=[t_D])
    S.dma("sp", lambda e: e.dma_start(out=mskf, in_=P["mskf"]), writes=[t_D])
    S.dma("sp", lambda e: e.dma_start(out=mskb, in_=P["mskb"]), writes=[t_D])
    S.dma("sp", lambda e: e.dma_start(out=dperm, in_=P["dperm"]), writes=[t_D])
    S.op("pe", lambda e: e.matmul(PS(3)[:, 0:64], sel[0:16, :], Dm[0:16, :], start=True, stop=True), reads=[t_D], writes=[t_bank[3]])
    S.op("dve", lambda e: e.tensor_copy(out=Dvec, in_=PS(3)[:, 0:64]), reads=[t_bank[3]], writes=[t_D])

    sm = lambda: A.alloc([128, 64], F32)
    dt_, ar, ai, yf, tmp, tmp2 = sm(), sm(), sm(), sm(), sm(), sm()
    PWr = A.alloc([128, 9, 64], F32); PWi = A.alloc([128, 9, 64], F32)
    NPr = A.alloc([128, 8, 64], F32); NPi = A.alloc([128, 8, 64], F32)
    cs_n = A.alloc([128, 9, 64], F32); sn_n = A.alloc([128, 9, 64], F32)
    t_s = T()
    V = lambda eng, fn: S.op(eng, fn, reads=[t_l, t_s], writes=[t_s])
    V("act", lambda e: e.activation(out=dt_, in_=DT, func=AF.Exp))
    V("dve", lambda e: e.tensor_tensor(out=ar, in0=LR, in1=dt_, op=ALU.mult))
    V("dve", lambda e: e.tensor_tensor(out=ai, in0=LI, in1=dt_, op=ALU.mult))
    V("dve", lambda e: e.tensor_scalar(out=yf, in0=ai, scalar1=1.0 / (2 * np.pi), scalar2=None, op0=ALU.mult))
    V("dve", lambda e: e.tensor_scalar(out=tmp, in0=yf, scalar1=MAGIC, scalar2=MAGIC, op0=ALU.add, op1=ALU.subtract))
    V("dve", lambda e: e.tensor_tensor(out=yf, in0=yf, in1=tmp, op=ALU.subtract))

    def sincos(yv, n, sdst, cdst):
        V("dve", lambda e: e.tensor_scalar(out=tmp, in0=yv, scalar1=float(n), scalar2=None, op0=ALU.mult))
        V("dve", lambda e: e.tensor_scalar(out=tmp2, in0=tmp, scalar1=MAGIC, scalar2=MAGIC, op0=ALU.add, op1=ALU.subtract))
        V("dve", lambda e: e.tensor_tensor(out=tmp2, in0=tmp, in1=tmp2, op=ALU.subtract))
        V("act", lambda e: e.activation(out=sdst, in_=tmp2, func=AF.Sin, scale=TWO_PI_LO))
        V("dve", lambda e: e.tensor_scalar(out=tmp, in0=tmp, scalar1=0.25, scalar2=None, op0=ALU.add))
        V("dve", lambda e: e.tensor_scalar(out=tmp2, in0=tmp, scalar1=MAGIC, scalar2=MAGIC, op0=ALU.add, op1=ALU.subtract))
        V("dve", lambda e: e.tensor_tensor(out=tmp2, in0=tmp, in1=tmp2, op=ALU.subtract))
        V("act", lambda e: e.activation(out=cdst, in_=tmp2, func=AF.Sin, scale=TWO_PI_LO))

    for n in range(9):
        sincos(yf, n, sn_n[:, n, :], cs_n[:, n, :])
        V("act", lambda e, n=n: e.activation(out=tmp, in_=ar, func=AF.Exp, scale=float(n)))
        V("dve", lambda e, n=n: e.tensor_tensor(out=PWr[:, n, :], in0=tmp, in1=cs_n[:, n, :], op=ALU.mult))
        V("dve", lambda e, n=n: e.tensor_tensor(out=PWi[:, n, :], in0=tmp, in1=sn_n[:, n, :], op=ALU.mult))
        if n < 8:
            V("act", lambda e, n=n: e.activation(out=tmp, in_=ar, func=AF.Exp, scale=-float(n)))
            V("dve", lambda e, n=n: e.tensor_tensor(out=NPr[:, n, :], in0=tmp, in1=cs_n[:, n, :], op=ALU.mult))
            V("dve", lambda e, n=n: e.scalar_tensor_tensor(out=NPi[:, n, :], in0=tmp, scalar=-1.0, in1=sn_n[:, n, :], op0=ALU.mult, op1=ALU.mult))
    S.op("act", lambda e: e.activation(out=r8, in_=ar, func=AF.Exp, scale=8.0), reads=[t_s], writes=[t_prm])
    V("dve", lambda e: e.tensor_scalar(out=tmp, in0=yf, scalar1=8.0, scalar2=None, op0=ALU.mult))
    V("dve", lambda e: e.tensor_scalar(out=tmp2, in0=tmp, scalar1=MAGIC, scalar2=MAGIC, op0=ALU.add, op1=ALU.subtract))
    S.op("dve", lambda e: e.tensor_tensor(out=y8, in0=tmp, in1=tmp2, op=ALU.subtract), reads=[t_s], writes=[t_prm])
    cre, cim, den, e1 = sm(), sm(), sm(), sm()
    V("dve", lambda e: e.tensor_scalar(out=e1, in0=PWr[:, 1, :], scalar1=-1.0, scalar2=None, op0=ALU.add))
    V("dve", lambda e: e.tensor_tensor(out=den, in0=LR, in1=LR, op=ALU.mult))
    V("dve", lambda e: e.tensor_tensor(out=tmp, in0=LI, in1=LI, op=ALU.mult))
    V("dve", lambda e: e.tensor_tensor(out=den, in0=den, in1=tmp, op=ALU.add))
    V("dve", lambda e: e.reciprocal(out=den, in_=den))
    V("dve", lambda e: e.tensor_tensor(out=cre, in0=e1, in1=LR, op=ALU.mult))
    V("dve", lambda e: e.tensor_tensor(out=tmp, in0=PWi[:, 1, :], in1=LI, op=ALU.mult))
    V("dve", lambda e: e.tensor_tensor(out=cre, in0=cre, in1=tmp, op=ALU.add))
    V("dve", lambda e: e.tensor_tensor(out=cre, in0=cre, in1=den, op=ALU.mult))
    V("dve", lambda e: e.tensor_tensor(out=cim, in0=PWi[:, 1, :], in1=LR, op=ALU.mult))
    V("dve", lambda e: e.tensor_tensor(out=tmp, in0=e1, in1=LI, op=ALU.mult))
    V("dve", lambda e: e.tensor_tensor(out=cim, in0=cim, in1=tmp, op=ALU.subtract))
    V("dve", lambda e: e.tensor_tensor(out=cim, in0=cim, in1=den, op=ALU.mult))
    tb = A.alloc([128, 64, 16], F32)
    bc = lambda ap: ap.unsqueeze(2).to_broadcast([128, 64, 16])
    VB = lambda eng, fn: S.op(eng, fn, reads=[t_s, t_B, t_ct], writes=[t_s])
    VB("dve", lambda e: e.tensor_tensor(out=Bbr, in0=BR, in1=bc(cre), op=ALU.mult))
    VB("dve", lambda e: e.tensor_tensor(out=tb, in0=BI, in1=bc(cim), op=ALU.mult))
    VB("dve", lambda e: e.tensor_tensor(out=Bbr, in0=Bbr, in1=tb, op=ALU.subtract))
    VB("dve", lambda e: e.tensor_tensor(out=Bbi, in0=BI, in1=bc(cre), op=ALU.mult))
    VB("dve", lambda e: e.tensor_tensor(out=tb, in0=BR, in1=bc(cim), op=ALU.mult))
    VB("dve", lambda e: e.tensor_tensor(out=Bbi, in0=Bbi, in1=tb, op=ALU.add))
    for i in range(8):
        for (dr, di, sr, si, nf, nb) in ((S1r, S1i, PWr, PWi, 7 - i, i), (S2r, S2i, PWr, PWi, i + 1, 8 - i), (S3r, S3i, NPr, NPi, 7 - i, i)):
            V("pool", lambda e, dr=dr, sr=sr, nf=nf, i=i: e.tensor_copy(out=dr[:, 0:32, i], in_=sr[:, nf, 0:32]))
            V("pool", lambda e, dr=dr, sr=sr, nb=nb, i=i: e.tensor_copy(out=dr[:, 32:64, i], in_=sr[:, nb, 32:64]))
            V("pool", lambda e, di=di, si=si, nf=nf, i=i: e.tensor_copy(out=di[:, 0:32, i], in_=si[:, nf, 0:32]))
            V("pool", lambda e, di=di, si=si, nb=nb, i=i: e.tensor_copy(out=di[:, 32:64, i], in_=si[:, nb, 32:64]))
    dump(K, "LR", LR, [t_l]); dump(K, "LI", LI, [t_l]); dump(K, "DT", DT, [t_l])
    dump(K, "PWr", PWr, [t_s]); dump(K, "PWi", PWi, [t_s]); dump(K, "NPr", NPr, [t_s]); dump(K, "NPi", NPi, [t_s])
    dump(K, "cre", cre, [t_s]); dump(K, "cim", cim, [t_s]); dump(K, "r8", r8, [t_prm]); dump(K, "y8", y8, [t_prm])
    dump(K, "Bbr", Bbr, [t_s]); dump(K, "Bbi", Bbi, [t_s]); dump(K, "CTr", CTr, [t_ct]); dump(K, "CTi", CTi, [t_ct])
    dump(K, "S1r", S1r, [t_s]); dump(K, "S2i", S2i, [t_s]); dump(K, "S3r", S3r, [t_s]); dump(K, "Dvec", Dvec, [t_D])
    S.barrier()
    A.release(m_p1)
    W1Tr = A.alloc([128, 64, 128], BF16); W1Ti = A.alloc([128, 64, 128], BF16)
    Vnr = A.alloc([128, 64, 128], BF16); Vnin = A.alloc([128, 64, 128], BF16)
    ta = A.alloc([128, 16, 128], F32); tb2 = A.alloc([128, 16, 128], F32)
    t_s = T()
    VB = lambda eng, fn: S.op(eng, fn, reads=[t_s], writes=[t_s])
    for d in range(4):
        cs = slice(d * 16, (d + 1) * 16)
        pw = lambda ap, cs=cs: ap[:, cs, :].unsqueeze(2).to_broadcast([128, 16, 16, 8])
        bb = lambda ap, cs=cs: ap[:, cs, :].unsqueeze(3).to_broadcast([128, 16, 16, 8])
        o4 = lambda ap: ap.rearrange("p a (c i) -> p a c i", c=16)
        VB("dve", lambda e, pw=pw, bb=bb: e.tensor_tensor(out=o4(ta), in0=pw(S1r), in1=bb(Bbr), op=ALU.mult))
        VB("dve", lambda e, pw=pw, bb=bb: e.tensor_tensor(out=o4(tb2), in0=pw(S1i), in1=bb(Bbi), op=ALU.mult))
        VB("dve", lambda e, cs=cs: e.tensor_tensor(out=W1Tr[:, cs, :], in0=ta, in1=tb2, op=ALU.subtract))
        VB("dve", lambda e, pw=pw, bb=bb: e.tensor_tensor(out=o4(ta), in0=pw(S1r), in1=bb(Bbi), op=ALU.mult))
        VB("dve", lambda e, pw=pw, bb=bb: e.tensor_tensor(out=o4(tb2), in0=pw(S1i), in1=bb(Bbr), op=ALU.mult))
        VB("dve", lambda e, cs=cs: e.tensor_tensor(out=W1Ti[:, cs, :], in0=ta, in1=tb2, op=ALU.add))
        pj = lambda ap, cs=cs: ap[:, cs, :].unsqueeze(3).to_broadcast([128, 16, 8, 16])
        cc = lambda ap, cs=cs: ap[:, cs, :].unsqueeze(2).to_broadcast([128, 16, 8, 16])
        o5 = lambda ap: ap.rearrange("p a (j c) -> p a j c", j=8)
        for (Sr, Si, dre, dimn) in ((S2r, S2i, Vre, Vimn), (S3r, S3i, Vnr, Vnin)):
            VB("dve", lambda e, pj=pj, cc=cc, Sr=Sr: e.tensor_tensor(out=o5(ta), in0=pj(Sr), in1=cc(CTr), op=ALU.mult))
            VB("dve", lambda e, pj=pj, cc=cc, Si=Si: e.tensor_tensor(out=o5(tb2), in0=pj(Si), in1=cc(CTi), op=ALU.mult))
            VB("dve", lambda e, cs=cs, dre=dre: e.tensor_tensor(out=dre[:, cs, :], in0=ta, in1=tb2, op=ALU.subtract))
            VB("dve", lambda e, pj=pj, cc=cc, Si=Si: e.tensor_tensor(out=o5(ta), in0=pj(Si), in1=cc(CTr), op=ALU.mult))
            VB("dve", lambda e, pj=pj, cc=cc, Sr=Sr: e.tensor_tensor(out=o5(tb2), in0=pj(Sr), in1=cc(CTi), op=ALU.mult))
            VB("dve", lambda e: e.tensor_tensor(out=ta, in0=ta, in1=tb2, op=ALU.add))
            VB("dve", lambda e, cs=cs, dimn=dimn: e.tensor_scalar(out=dimn[:, cs, :], in0=ta, scalar1=-1.0, scalar2=None, op0=ALU.mult))
    k = 0
    for ri, src in enumerate((W1Tr, W1Ti)):
        for c8 in range(8):
            bk = 4 + k % 2; k += 1
            for cl in range(8):
                col = c8 * 8 + cl
                S.op("pe", lambda e, src=src, col=col, cl=cl, bk=bk: e.transpose(PSB(bk)[:, cl * 128:(cl + 1) * 128], src[:, col, :], K.identb),
                     reads=[t_s, K.t_const], writes=[t_bank[bk]])
            S.op("act", lambda e, ri=ri, c8=c8, bk=bk: e.activation(out=W1[:, c8 * 8:(c8 + 1) * 8, ri, :], in_=PSB(bk).rearrange("p (a b) -> p a b", a=8), func=AF.Copy),
                 reads=[t_bank[bk]], writes=[t_prm])
    tq = [A.alloc([128, 128], F32) for _ in range(2)]
    t_tq = [T(), T()]
    for g in range(64):
        G2, g2 = g // 2, g % 2
        lo, hi = g2 * 64, g2 * 64 + 64
        bf_, bb_ = 6, 7
        for (bk, col) in ((bf_, G2), (bb_, 32 + G2)):
            S.op("pe", lambda e, bk=bk, col=col, lo=lo, hi=hi: e.matmul(PS(bk)[:, 0:128], W1Tr[lo:hi, col, :], Vnr[lo:hi, col, :], start=True, stop=False),
                 reads=[t_s], writes=[t_bank[bk]])
            S.op("pe", lambda e, bk=bk, col=col, lo=lo, hi=hi: e.matmul(PS(bk)[:, 0:128], W1Ti[lo:hi, col, :], Vnin[lo:hi, col, :], start=False, stop=True),
                 reads=[t_s], writes=[t_bank[bk]])
        j = g % 2
        S.op("dve", lambda e, j=j: e.tensor_tensor(out=tq[j], in0=PS(6)[:, 0:128], in1=mskf, op=ALU.mult), reads=[t_bank[6], t_D], writes=[t_tq[j]])
        S.op("dve", lambda e, j=j: e.tensor_tensor(out=PS(7)[:, 128:256], in0=PS(7)[:, 0:128], in1=mskb, op=ALU.mult), reads=[t_D], writes=[t_bank[7]])
        S.op("dve", lambda e, j=j: e.tensor_tensor(out=tq[j], in0=PS(7)[:, 128:256], in1=tq[j], op=ALU.add), reads=[t_bank[7]], writes=[t_tq[j]])
        S.op("dve", lambda e, j=j, g=g: e.scalar_tensor_tensor(out=Tm[:, g, :], in0=dperm, scalar=Dvec[:, g:g + 1], in1=tq[j], op0=ALU.mult, op1=ALU.add),
             reads=[t_tq[j], t_D, t_prm], writes=[t_prm])
    dump(K, "W1Tr", W1Tr, [t_s]); dump(K, "W1Ti", W1Ti, [t_s]); dump(K, "Vnr", Vnr, [t_s]); dump(K, "Vnin", Vnin, [t_s])
    dump(K, "Vre", Vre, [t_s]); dump(K, "Vimn", Vimn, [t_s]); dump(K, "Tm", Tm, [t_prm]); dump(K, "W1", W1, [t_prm])
    S.barrier()
    A.release(m_prep)

    U = A.alloc([128, 64, 512], BF16)
    m_U = A.mark()
    xt = A.alloc([128, 8, D], F32)
    ub = A.alloc([128, D, 8], BF16)
    gt = A.alloc([128, D], F32)
    junk = A.alloc([128, D], BF16)
    ss = A.alloc([128, 8], F32); rstd = A.alloc([128, 8], F32)
    t_xt, t_ub, t_g, t_ss = T(), T(), T(), T()
    load_bcast_row(K, "sp", gt, P["norm_mix1"], t_g)
    xk = xin.rearrange("(kb kp i) d -> kb kp i d", kp=128, i=8)
    for kb in range(4):
        S.dma("sp", lambda e, kb=kb: e.dma_start(out=xt, in_=xk[kb]), writes=[t_xt])
        emit_rmsnorm_stats(K, xt, 8, ss, rstd, t_xt, t_ss, junk)
        for i in range(8):
            S.op("dve", lambda e, i=i: e.scalar_tensor_tensor(out=ub[:, :, i], in0=xt[:, i, :], scalar=rstd[:, i:i + 1], in1=gt, op0=ALU.mult, op1=ALU.mult),
                 reads=[t_xt, t_ss, t_g], writes=[t_ub])
        ubf = ub.rearrange("p d i -> p (d i)")
        for g8 in range(8):
            bk = g8 % 2
            for gl in range(8):
                g = g8 * 8 + gl
                S.op("pe", lambda e, g=g, gl=gl, bk=bk: e.transpose(PSB(bk)[:, gl * 128:(gl + 1) * 128], ubf[:, g * 128:(g + 1) * 128], K.identb),
                     reads=[t_ub, K.t_const], writes=[t_bank[bk]])
            src = PSB(bk).rearrange("p (a b) -> p a b", a=8)
            dst = U[:, g8 * 8:(g8 + 1) * 8, kb * 128:(kb + 1) * 128]
            wr = [t_U[g8 * 8 + gl] for gl in range(8)]
            if g8 % 2 == 0:
                S.op("act", lambda e, src=src, dst=dst: e.activation(out=dst, in_=src, func=AF.Copy), reads=[t_bank[bk]], writes=wr)
            else:
                S.op("dve", lambda e, src=src, dst=dst: e.tensor_copy(out=dst, in_=src), reads=[t_bank[bk]], writes=wr)
    dump(K, "U", U, t_U)
    S.barrier()
    A.release(m_U)

    NB = 2
    mk = lambda dt=F32: [A.alloc([128, 512], dt) for _ in range(NB)]
    ys_, fs_, sn_, cs_, xr_, xi_, wr_, wi_, zr_, zi_ = [mk() for _ in range(10)]
    m1_, m2_ = ys_, fs_
    t_t = [[T() for _ in range(10)] for _ in range(NB)]
    for tl_ in t_t:
        tl_.extend([tl_[0], tl_[1]])
    t_E = [T() for _ in range(4)]
    Es = [[A.alloc([128, 512], BF16) for _ in range(2)] for _ in range(4)]
    for sl in range(4):
        for ri in range(2):
            S.op("pool", lambda e, sl=sl, ri=ri: e.memset(Es[sl][ri], 0.0), writes=[t_E[sl]])
    it = [0]

    def level1(G2):
        for d in range(2):
            for ri in range(2):
                bk = d * 2 + ri
                for g2 in range(2):
                    g = 2 * G2 + g2
                    S.op("pe", lambda e, bk=bk, d=d, ri=ri, g2=g2, g=g: e.matmul(PS(bk)[g2 * 64:(g2 + 1) * 64, :], W1[:, d * 32 + G2, ri, g2 * 64:(g2 + 1) * 64],
                                                                             U[:, g, :], start=True, stop=True),
                         reads=[t_prm, t_U[g]], writes=[t_bank[bk]])

    def scan(G2, d):
        b = it[0] % NB; it[0] += 1
        col = d * 32 + G2
        tt = t_t[b]
        ys, fs, sn, cs, xr, xi, wr, wi, zr, zi, m1, m2 = [lst[b] for lst in (ys_, fs_, sn_, cs_, xr_, xi_, wr_, wi_, zr_, zi_, m1_, m2_)]
        rv = (lambda ap: ap[:, ::-1]) if d == 1 else (lambda ap: ap)
        S.op("act", lambda e: e.activation(out=xr, in_=rv(PS(d * 2 + 0)), func=AF.Copy), reads=[t_bank[d * 2]], writes=[tt[4]])
        S.op("act", lambda e: e.activation(out=xi, in_=rv(PS(d * 2 + 1)), func=AF.Copy), reads=[t_bank[d * 2 + 1]], writes=[tt[5]])
        S.op("dve", lambda e: e.tensor_scalar(out=ys, in0=iok, scalar1=y8[:, col:col + 1], scalar2=None, op0=ALU.mult), reads=[t_prm], writes=[tt[0]])
        S.op("dve", lambda e: e.tensor_scalar(out=fs, in0=ys, scalar1=MAGIC, scalar2=MAGIC, op0=ALU.add, op1=ALU.subtract), reads=[tt[0]], writes=[tt[1]])
        S.op("dve", lambda e: e.tensor_tensor(out=fs, in0=ys, in1=fs, op=ALU.subtract), reads=[tt[0], tt[1]], writes=[tt[1]])
        S.op("act", lambda e: e.activation(out=sn, in_=fs, func=AF.Sin, scale=TWO_PI_LO), reads=[tt[1]], writes=[tt[2]])
        S.op("dve", lambda e: e.tensor_scalar(out=ys, in0=ys, scalar1=0.25, scalar2=None, op0=ALU.add), reads=[tt[1]], writes=[tt[0]])
        S.op("dve", lambda e: e.tensor_scalar(out=fs, in0=ys, scalar1=MAGIC, scalar2=MAGIC, op0=ALU.add, op1=ALU.subtract), reads=[tt[0]], writes=[tt[1]])
        S.op("dve", lambda e: e.tensor_tensor(out=fs, in0=ys, in1=fs, op=ALU.subtract), reads=[tt[0], tt[1]], writes=[tt[1]])
        S.op("act", lambda e: e.activation(out=cs, in_=fs, func=AF.Sin, scale=TWO_PI_LO), reads=[tt[1]], writes=[tt[3]])
        S.op("dve", lambda e: e.tensor_tensor(out=m1, in0=xr, in1=cs, op=ALU.mult), reads=[tt[4], tt[3]], writes=[tt[10]])
        S.op("dve", lambda e: e.tensor_tensor(out=m2, in0=xi, in1=sn, op=ALU.mult), reads=[tt[5], tt[2]], writes=[tt[11]])
        S.op("pool", lambda e: e.tensor_tensor(out=wr, in0=m1, in1=m2, op=ALU.add), reads=[tt[10], tt[11]], writes=[tt[6]])
        S.op("dve", lambda e: e.tensor_tensor(out=m1, in0=xi, in1=cs, op=ALU.mult), reads=[tt[5], tt[3]], writes=[tt[10]])
        S.op("dve", lambda e: e.tensor_tensor(out=m2, in0=xr, in1=sn, op=ALU.mult), reads=[tt[4], tt[2]], writes=[tt[11]])
        S.op("pool", lambda e: e.tensor_tensor(out=wi, in0=m1, in1=m2, op=ALU.subtract), reads=[tt[10], tt[11]], writes=[tt[7]])
        rb = r8[:, col:col + 1].to_broadcast([128, 512])
        S.op("dve", lambda e: e.tensor_tensor_scan(out=zr, data0=rb, data1=wr, initial=0.0, op0=ALU.mult, op1=ALU.add), reads=[t_prm, tt[6]], writes=[tt[8]])
        S.op("dve", lambda e: e.tensor_tensor_scan(out=zi, data0=rb, data1=wi, initial=0.0, op0=ALU.mult, op1=ALU.add), reads=[t_prm, tt[7]], writes=[tt[9]])
        sl = (G2 % 2) * 2 + d
        if d == 0:
            dre, dim = Es[sl][0][:, 1:512], Es[sl][1][:, 1:512]
        else:
            dre, dim = Es[sl][0][:, 0:511][:, ::-1], Es[sl][1][:, 0:511][:, ::-1]
        S.op("pool", lambda e: e.tensor_tensor(out=xr, in0=zr, in1=cs, op=ALU.mult), reads=[tt[8], tt[3]], writes=[tt[4]])
        S.op("pool", lambda e: e.tensor_tensor(out=xi, in0=zi, in1=sn, op=ALU.mult), reads=[tt[9], tt[2]], writes=[tt[5]])
        S.op("pool", lambda e: e.tensor_tensor(out=dre, in0=xr[:, 0:511], in1=xi[:, 0:511], op=ALU.subtract), reads=[tt[4], tt[5]], writes=[t_E[sl]])
        S.op("pool", lambda e: e.tensor_tensor(out=m1, in0=zi, in1=cs, op=ALU.mult), reads=[tt[9], tt[3]], writes=[tt[10]])
        S.op("pool", lambda e: e.tensor_tensor(out=m2, in0=zr, in1=sn, op=ALU.mult), reads=[tt[8], tt[2]], writes=[tt[11]])
        S.op("pool", lambda e: e.tensor_tensor(out=dim, in0=m1[:, 0:511], in1=m2[:, 0:511], op=ALU.add), reads=[tt[10], tt[11]], writes=[t_E[sl]])

    def output(G2):
        for g2 in range(2):
            g = 2 * G2 + g2
            lo, hi = g2 * 64, g2 * 64 + 64
            bk = 4 + (g % 4)
            for kb in range(4):
                ks = slice(kb * 128, (kb + 1) * 128)
                o = PS(bk)[:, kb * 128:(kb + 1) * 128]
                S.op("pe", lambda e, o=o, g=g, ks=ks: e.matmul(o, U[:, g, ks], Tm[:, g, :], start=True, stop=False), reads=[t_U[g], t_prm], writes=[t_bank[bk]])
                n = 0
                for d in range(2):
                    sl = (G2 % 2) * 2 + d
                    col = d * 32 + G2
                    for ri, Vm in ((0, Vre), (1, Vimn)):
                        n += 1
                        S.op("pe", lambda e, o=o, sl=sl, ri=ri, Vm=Vm, col=col, ks=ks, lo=lo, hi=hi, n=n: e.matmul(o, Es[sl][ri][lo:hi, ks], Vm[lo:hi, col, :],
                                                                                                         start=False, stop=(n == 4)),
                             reads=[t_E[sl], t_prm], writes=[t_bank[bk]])
            if g % 2 == 0:
                S.op("act", lambda e, g=g, bk=bk: e.activation(out=U[:, g, :], in_=PS(bk), func=AF.Copy), reads=[t_bank[bk]], writes=[t_U[g]])
            else:
                S.op("dve", lambda e, g=g, bk=bk: e.tensor_copy(out=U[:, g, :], in_=PS(bk)), reads=[t_bank[bk]], writes=[t_U[g]])

    level1(0)
    for G2 in range(32):
        scan(G2, 0)
        scan(G2, 1)
        if G2 + 1 < 32:
            level1(G2 + 1)
        output(G2)
    dump(K, "Y", U, t_U)
    S.barrier()
    A.release(m_U)
    A.release_top()

    Wg = A.alloc([128, 8, D], BF16)
    stg = A.alloc([128, 4, D], F32)
    bgf = A.alloc([128, D], F32); bgb = A.alloc([128, D], BF16); onesr = A.alloc([128, 128], BF16)
    t_wg, t_stg = T(), T()
    wgv = P["w_gate"].rearrange("(dc p) n -> p dc n", p=128)
    for h in range(2):
        S.dma("sp", lambda e, h=h: e.dma_start(out=stg, in_=wgv[:, 4 * h:4 * h + 4, :]), writes=[t_stg])
        S.op("pool", lambda e, h=h: e.tensor_copy(out=Wg[:, 4 * h:4 * h + 4, :], in_=stg), reads=[t_stg], writes=[t_wg])
    S.dma("sp", lambda e: e.dma_start(out=bgf[0:1, :], in_=P["b_gate"].unsqueeze(0)), writes=[t_stg])
    S.op("pool", lambda e: e.tensor_copy(out=bgb[0:1, :], in_=bgf[0:1, :]), reads=[t_stg], writes=[t_wg])
    S.op("pool", lambda e: e.memset(onesr, 1.0), writes=[t_wg])
    NBG = 2
    sqb = [A.alloc([128, D], F32) for _ in range(NBG)]
    gel = [A.alloc([128, D], F32) for _ in range(NBG)]
    gb = [A.alloc([128, D], BF16) for _ in range(NBG)]
    gT = [A.alloc([128, D], BF16) for _ in range(NBG)]
    sg = [A.alloc([128, 512], F32) for _ in range(2)]
    xt3 = [A.alloc([128, D], F32) for _ in range(3)]
    t_y, t_sq, t_gel, t_gb, t_gT, t_sg, t_x3 = [T(), T()], [T(), T()], [T(), T()], [T(), T()], [T(), T()], [T(), T()], [T(), T(), T()]
    t_Uall = T()
    xs = xin.rearrange("(kb kp j) d -> kb j kp d", kp=128, j=8)
    xo = xout.rearrange("(kb kp j) d -> kb j kp d", kp=128, j=8)
    yall = U.rearrange("p g (kb j c) -> p kb j g c", kb=4, j=8)
    kk = [0]

    def load3(n):
        kb, j = n // 8, n % 8
        S.dma("sp", lambda e: e.dma_start(out=xt3[n % 3], in_=xs[kb, j]), writes=[t_x3[n % 3]])

    load3(0); load3(1)
    for n in range(32):
        kb, j = n // 8, n % 8
        b = n % NBG
        if n + 2 < 32:
            load3(n + 2)
        yv = yall[:, kb, j]
        y3 = lambda ap: ap.rearrange("p (g c) -> p g c", c=16)
        S.op("act", lambda e, yv=yv, b=b: e.activation(out=y3(sqb[b]), in_=yv, func=AF.Square), reads=t_U, writes=[t_sq[b]])
        S.op("dve", lambda e, b=b: e.tensor_scalar(out=sqb[b], in0=sqb[b], scalar1=0.044715, scalar2=1.0, op0=ALU.mult, op1=ALU.add), reads=[t_sq[b]], writes=[t_sq[b]])
        S.op("dve", lambda e, yv=yv, b=b: e.tensor_tensor(out=y3(sqb[b]), in0=y3(sqb[b]), in1=yv, op=ALU.mult), reads=t_U + [t_sq[b]], writes=[t_sq[b]])
        S.op("act", lambda e, b=b: e.activation(out=sqb[b], in_=sqb[b], func=AF.Sigmoid, scale=1.5957691216), reads=[t_sq[b]], writes=[t_sq[b]])
        S.op("dve", lambda e, yv=yv, b=b: e.tensor_tensor(out=y3(gel[b]), in0=y3(sqb[b]), in1=yv, op=ALU.mult), reads=t_U + [t_sq[b]], writes=[t_gel[b]])
        S.op("pool", lambda e, b=b: e.tensor_copy(out=gb[b], in_=gel[b]), reads=[t_gel[b]], writes=[t_gb[b]])
        for dc in range(8):
            S.op("pe", lambda e, b=b, dc=dc: e.transpose(PSB(0)[:, dc * 128:(dc + 1) * 128], gb[b][:, dc * 128:(dc + 1) * 128], K.identb),
                 reads=[t_gb[b], K.t_const], writes=[t_bank[0]])
        S.op("act", lambda e, b=b: e.activation(out=gT[b], in_=PSB(0), func=AF.Copy), reads=[t_bank[0]], writes=[t_gT[b]])
        xb = xt3[n % 3]
        for dh in range(2):
            bk = 1 + kk[0] % 2; kk[0] += 1
            sb_ = kk[0] % 2
            for dc in range(8):
                S.op("pe", lambda e, b=b, dc=dc, dh=dh, bk=bk: e.matmul(PS(bk), gT[b][:, dc * 128:(dc + 1) * 128], Wg[:, dc, dh * 512:(dh + 1) * 512], start=(dc == 0), stop=False),
                     reads=[t_gT[b], t_wg], writes=[t_bank[bk]])
            S.op("pe", lambda e, dh=dh, bk=bk: e.matmul(PS(bk), onesr[0:1, :], bgb[0:1, dh * 512:(dh + 1) * 512], start=False, stop=True), reads=[t_wg], writes=[t_bank[bk]])
            S.op("act", lambda e, bk=bk, sb_=sb_: e.activation(out=sg[sb_], in_=PS(bk), func=AF.Sigmoid), reads=[t_bank[bk]], writes=[t_sg[sb_]])
            S.op("dve", lambda e, b=b, dh=dh, sb_=sb_: e.tensor_tensor(out=sg[sb_], in0=sg[sb_], in1=gel[b][:, dh * 512:(dh + 1) * 512], op=ALU.mult),
                 reads=[t_sg[sb_], t_gel[b]], writes=[t_sg[sb_]])
            S.op("pool", lambda e, xb=xb, dh=dh, sb_=sb_: e.tensor_tensor(out=xb[:, dh * 512:(dh + 1) * 512], in0=xb[:, dh * 512:(dh + 1) * 512], in1=sg[sb_], op=ALU.add),
                 reads=[t_sg[sb_]], writes=[t_x3[n % 3]])
        S.dma("sp", lambda e, xb=xb, kb=kb, j=j: e.dma_start(out=xo[kb, j], in_=xb), reads=[t_x3[n % 3]], is_out=K.is_out(xout))
    S.barrier()
    A.release(m_phase)


def build_program(phases=("mix0", "mlp0", "mix1", "mlp1"), dbg=False):
    nc = bass.Bass("TRN2", target_bir_lowering=False)
    K = Ctx()
    K.dbg = dbg
    K.nc = nc
    din = lambda name, shape, dt=F32: nc.dram_tensor(name, list(shape), dt, kind="ExternalInput").ap()
    x = din("x", [S_LEN, D])
    norm_mix = din("norm_mix", [2, D]); norm_mlp = din("norm_mlp", [2, D])
    w1 = din("mlp_w1", [2, D, DFF]); w2 = din("mlp_w2", [2, DFF, D])
    final_norm = din("final_norm", [D])
    identb_d = din("identb", [128, 128], BF16)
    P = {}
    P["w_in"] = din("w_in", [1, D, 1280])[0]
    P["w_fnet"] = din("w_fnet", [1, 8, 64, 64])[0]
    P["q_norm"] = din("q_norm", [1, 64])[0]
    P["k_norm"] = din("k_norm", [1, 64])[0]
    P["w_out"] = din("w_out", [1, D, D])[0]
    for nm in ("rmat", "onesblk", "bdc", "bds", "bdsn", "c64bd", "s64bdn"):
        P[nm] = din(nm, [128, 128], BF16)
    P["cosp"] = din("cosp", [128, S_LEN]); P["sinp"] = din("sinp", [128, S_LEN])
    P["cbt"] = din("cbt", [128, S_LEN], BF16); P["sbt"] = din("sbt", [128, S_LEN], BF16)
    P["norm_mix0"] = norm_mix[0]
    P["norm_mix1"] = norm_mix[1]
    for nm, shp in (("lam_re", [1, 2, 64, 64]), ("lam_im", [1, 2, 64, 64]), ("log_dt", [1, 2, 64]), ("b_re", [1, 2, 64, 64, 16]), ("b_im", [1, 2, 64, 64, 16]),
                    ("c_re", [1, 2, 64, 16, 64]), ("c_im", [1, 2, 64, 16, 64]), ("d_skip", [1, D]), ("w_gate", [1, D, D]), ("b_gate", [1, D])):
        P[nm] = din(nm, shp)[0]
    P["iotak"] = din("iotak", [128, 512]); P["identf"] = din("identf", [128, 128])
    P["sel16"] = din("sel16", [16, 128]); P["mskf"] = din("mskf", [128, 128]); P["mskb"] = din("mskb", [128, 128]); P["dperm"] = din("dperm", [128, 128])
    P["pqh"] = nc.dram_tensor("pqh", [S_LEN, 2, 512], BF16).ap()
    out = nc.dram_tensor("out", [S_LEN, D], F32, kind="ExternalOutput").ap()
    scr = [nc.dram_tensor("scr%d" % i, [S_LEN, D], F32).ap() for i in range(3)]
    K.out = out
    K.w1 = [w1[0], w1[1]]; K.w2 = [w2[0], w2[1]]
    K.w1b = [nc.dram_tensor("w1b%d" % l, [D, DFF], BF16).ap() for l in range(2)]
    K.w2b = [nc.dram_tensor("w2b%d" % l, [DFF, D], BF16).ap() for l in range(2)]
    K.t_w1b = [[T() for _ in range(16)] for _ in range(2)]
    K.t_w2b = [[T() for _ in range(16)] for _ in range(2)]
    K.is_out = lambda ap: ap is out
    with ExitStack() as es:
        arena = es.enter_context(nc.sbuf_tensor("arena", [128, ARENA_BYTES + 512], U8))
        K.ps = es.enter_context(nc.psum_tensor("ps", [128, 4096], F32))
        K.S = Sched(nc, es)
        K.A = Arena(arena)
        K.identb = K.A.alloc([128, 128], BF16)
        K.t_const = T()
        K.S.dma("sp", lambda e: e.dma_start(out=K.identb, in_=identb_d), writes=[K.t_const])
        K.eps_ap = K.A.alloc([128, 1], F32)
        K.S.op("pool", lambda e: e.memset(K.eps_ap, EPS), writes=[K.t_const])
        cur = x
        chain = {"mix0": scr[0], "mlp0": scr[1], "mix1": scr[2], "mlp1": out}
        last = phases[-1]
        layers = [l for l in range(2) if "mlp%d" % l in phases]
        K.cast_jobs = make_cast_jobs(K, layers)
        if "mix0" not in phases:
            m = K.A.mark()
            Caster(K, K.cast_jobs).flush()
            K.S.barrier()
            K.A.release(m)
        for ph in phases:
            dst = out if ph == last else chain[ph]
            if ph == "mix0":
                phase_mix0(K, cur, dst, P)
            elif ph == "mix1":
                phase_mix1(K, cur, dst, P)
            elif ph == "mlp0":
                phase_mlp(K, cur, dst, 0, norm_mlp[0])
            elif ph == "mlp1":
                phase_mlp(K, cur, dst, 1, norm_mlp[1], final_gain=final_norm)
            cur = dst
        K.S.emit()
    return nc


def make_consts():
    c = {}
    bf = ml_dtypes.bfloat16
    c["identb"] = np.eye(128, dtype=np.float32).astype(bf)
    R = np.zeros((128, 128), np.float32)
    for h in range(2):
        for sec in range(2):
            base = h * 64 + sec * 32
            for i in range(16):
                R[base + 16 + i, base + i] = -1.0
                R[base + i, base + 16 + i] = 1.0
    c["rmat"] = R.astype(bf)
    ob = np.zeros((128, 128), np.float32)
    ob[0:64, 0:64] = 1.0 / 64; ob[64:, 64:] = 1.0 / 64
    c["onesblk"] = ob.astype(bf)
    a = np.arange(32, dtype=np.float64)
    c32 = np.cos(2 * np.pi * np.outer(a, a) / 32); s32 = np.sin(2 * np.pi * np.outer(a, a) / 32)
    bdc = np.zeros((128, 128)); bds = np.zeros((128, 128))
    for b4 in range(4):
        bdc[b4 * 32:(b4 + 1) * 32, b4 * 32:(b4 + 1) * 32] = c32
        bds[b4 * 32:(b4 + 1) * 32, b4 * 32:(b4 + 1) * 32] = s32
    c["bdc"] = bdc.astype(np.float32).astype(bf); c["bds"] = bds.astype(np.float32).astype(bf); c["bdsn"] = (-bds).astype(np.float32).astype(bf)
    k64 = np.arange(64, dtype=np.float64)
    c64 = np.cos(2 * np.pi * np.outer(k64, k64) / 64) / 512.0; s64 = -np.sin(2 * np.pi * np.outer(k64, k64) / 64) / 512.0
    cb = np.zeros((128, 128)); sb_ = np.zeros((128, 128))
    for h in range(2):
        cb[h * 64:(h + 1) * 64, h * 64:(h + 1) * 64] = c64
        sb_[h * 64:(h + 1) * 64, h * 64:(h + 1) * 64] = s64
    c["c64bd"] = cb.astype(np.float32).astype(bf); c["s64bdn"] = sb_.astype(np.float32).astype(bf)
    n = np.arange(S_LEN); t = 128 * (n % 32) + n // 32
    row = (t // 64).astype(np.float64); colp = (t % 64).astype(np.float64)
    inv = 10000.0 ** (-np.arange(0, 32, 2, dtype=np.float64) / 32.0)
    cosp = np.zeros((128, S_LEN)); sinp = np.zeros((128, S_LEN))
    for p in range(128):
        d = p % 64
        pos = row if d < 32 else colp
        ang = pos * inv[d % 16]
        cosp[p] = np.cos(ang); sinp[p] = np.sin(ang)
    c["cosp"] = cosp.astype(np.float32); c["sinp"] = sinp.astype(np.float32)
    b = np.arange(128, dtype=np.float64)[:, None]; tt = np.arange(S_LEN, dtype=np.float64)[None, :]
    ph = 2 * np.pi * ((b * tt) % S_LEN) / S_LEN
    c["cbt"] = np.cos(ph).astype(np.float32).astype(bf); c["sbt"] = np.sin(ph).astype(np.float32).astype(bf)
    c["iotak"] = np.tile(np.arange(512, dtype=np.float32)[None, :], (128, 1))
    c["identf"] = np.eye(128, dtype=np.float32)
    sel = np.zeros((16, 128), np.float32)
    for cc in range(16):
        sel[cc, cc * 8:(cc + 1) * 8] = 1.0
    c["sel16"] = sel
    ii = (np.arange(128) % 8)[:, None]
    jj = (np.arange(128) // 16)[None, :]
    c["mskf"] = (ii <= jj).astype(np.float32); c["mskb"] = (ii >= jj).astype(np.float32)
    ci_ = (np.arange(128) // 8)[:, None]; co_ = (np.arange(128) % 16)[None, :]
    c["dperm"] = ((ii == jj) & (ci_ == co_)).astype(np.float32)
    return c


_USED = ("norm_mix", "norm_mlp", "mlp_w1", "mlp_w2", "final_norm", "w_in", "w_fnet", "q_norm", "k_norm", "w_out",
         "lam_re", "lam_im", "log_dt", "b_re", "b_im", "c_re", "c_im", "d_skip", "w_gate", "b_gate")


def kernel(**inputs):
    nc = build_program()
    consts = make_consts()
    x = np.ascontiguousarray(inputs["x"], dtype=np.float32)
    shared = {k: np.ascontiguousarray(inputs[k], dtype=np.float32) for k in _USED}
    shared.update(consts)
    in_maps = []
    for i in range(8):
        m = dict(shared)
        m["x"] = x[i]
        in_maps.append(m)
    res = run_bass_kernel_spmd(nc, in_maps, core_ids=list(range(8)))
    return np.stack([np.asarray(r["out"], dtype=np.float32) for r in res.results], axis=0)
```

```python
import numpy as np
import ml_dtypes
import concourse.bass as bass
import concourse.mybir as mybir
from concourse.bass_utils import run_bass_kernel_spmd
from contextlib import ExitStack

F32 = mybir.dt.float32
BF16 = mybir.dt.bfloat16
U8 = mybir.dt.uint8
ALU = mybir.AluOpType
AF = mybir.ActivationFunctionType
AX = mybir.AxisListType

S_LEN = 4096
D = 1024
DFF = 4096
EPS = 1e-6
ARENA_BYTES = 206 * 1024


class T:
    __slots__ = ("w", "r")

    def __init__(self):
        self.w = []
        self.r = []


class Sched:
    ENG = ("pe", "act", "dve", "pool", "sp")
    N_DMA_SEMS = 32

    def __init__(self, nc, es):
        self.nc = nc
        self.ops = {e: [] for e in self.ENG}
        self.sem = {e: es.enter_context(nc.semaphore("s_" + e)) for e in ("pe", "act", "dve", "pool")}
        self.dsem = [es.enter_context(nc.semaphore("d%d" % i)) for i in range(self.N_DMA_SEMS)]
        self.dcnt = [0] * self.N_DMA_SEMS
        self.dnext = 0
        self.cnt = {e: 0 for e in self.ENG}
        self.out_tokens = []
        self.barrier_deps = []

    @staticmethod
    def _joinable(t):
        return bool(t.w) and not t.r and all(w[0] == "d" for w in t.w)

    def _deps(self, reads, writes, is_dma=False):
        deps = list(self.barrier_deps)
        for t in reads:
            deps.extend(t.w)
        for t in writes:
            if not (is_dma and self._joinable(t)):
                deps.extend(t.w)
            deps.extend(t.r)
        return deps

    def _mark(self, tok, reads, writes, is_dma=False):
        for t in reads:
            t.r.append(tok)
            if len(t.r) > 64:
                t.r = self._compact(t.r)
        for t in writes:
            if is_dma and self._joinable(t):
                t.w = t.w + [tok]
            else:
                t.w = [tok]
            t.r = []

    @staticmethod
    def _compact(toks):
        best = {}
        for d in toks:
            k = (d[0], d[1])
            if k not in best or best[k][2] < d[2]:
                best[k] = d
        return list(best.values())

    def op(self, eng, fn, reads=(), writes=()):
        deps = self._deps(reads, writes)
        self.cnt[eng] += 1
        tok = ("c", eng, self.cnt[eng])
        self.ops[eng].append((fn, deps, tok))
        self._mark(tok, reads, writes)
        return tok

    def dma(self, q, fn, reads=(), writes=(), is_out=False):
        deps = self._deps(reads, writes, True)
        k = self.dnext
        self.dnext = (self.dnext + 1) % self.N_DMA_SEMS
        if self.dcnt[k]:
            deps.append(("d", k, self.dcnt[k]))
        self.dcnt[k] += 16
        tok = ("d", k, self.dcnt[k])
        self.ops[q].append((fn, deps, tok))
        self._mark(tok, reads, writes, True)
        if is_out:
            self.out_tokens.append(tok)
        return tok

    def barrier(self):
        deps = [("c", e, self.cnt[e]) for e in ("pe", "act", "dve", "pool") if self.cnt[e]]
        deps += [("d", k, self.dcnt[k]) for k in range(self.N_DMA_SEMS) if self.dcnt[k]]
        self.barrier_deps = deps

    def emit(self):
        nc = self.nc
        final_deps = list(self.out_tokens)
        with nc.Block() as block:
            def run(engname, e):
                waited = {}

                def do_wait(tok):
                    if tok[0] == "c":
                        if tok[1] == engname and engname == "pe":
                            return
                        key = ("c", tok[1]); sem = self.sem[tok[1]]
                    else:
                        key = ("d", tok[1]); sem = self.dsem[tok[1]]
                    if waited.get(key, 0) >= tok[2]:
                        return
                    waited[key] = tok[2]
                    e.wait_ge(sem, tok[2])

                for fn, deps, tok in self.ops[engname]:
                    for d in self._compact(deps):
                        do_wait(d)
                    ins = fn(e)
                    if tok[0] == "c":
                        ins.then_inc(self.sem[tok[1]], 1)
                    else:
                        ins.then_inc(self.dsem[tok[1]], 16)
                if engname == "sp":
                    for d in final_deps:
                        do_wait(d)

            @block.sync
            def _(e):
                run("sp", e)

            @block.tensor
            def _(e):
                run("pe", e)

            @block.scalar
            def _(e):
                run("act", e)

            @block.vector
            def _(e):
                run("dve", e)

            @block.gpsimd
            def _(e):
                run("pool", e)


class Arena:
    def __init__(self, ap):
        self.ap = ap
        self.off = 0
        self.top = ARENA_BYTES

    def alloc(self, shape, dt, top=False):
        esz = 2 if dt == BF16 else 4
        n = int(np.prod(shape[1:]))
        nb = n * esz
        nba = (nb + 63) // 64 * 64
        assert self.off + nba <= self.top, ("SBUF arena overflow", self.off, self.top, nb)
        if top:
            self.top -= nba
            v = self.ap[:, self.top:self.top + nb].bitcast(dt)
        else:
            v = self.ap[:, self.off:self.off + nb].bitcast(dt)
            self.off += nba
        if len(shape) > 2:
            names = " ".join("a%d" % i for i in range(len(shape) - 1))
            kw = {"a%d" % i: shape[i + 1] for i in range(len(shape) - 2)}
            v = v.rearrange("p (%s) -> p %s" % (names, names), **kw)
        return v

    def mark(self):
        return self.off

    def release(self, m):
        self.off = m

    def release_top(self):
        self.top = ARENA_BYTES


class Ctx:
    pass


def dump(K, name, ap, reads):
    if not getattr(K, "dbg", False):
        return
    t = K.nc.dram_tensor("dbg_" + name, list(ap.shape), ap.dtype, kind="ExternalOutput").ap()
    K.S.dma("sp", lambda e: e.dma_start(out=t, in_=ap), reads=reads, is_out=True)


def psum_bf16(ps_ap):
    return ps_ap.bitcast(BF16)


def emit_rmsnorm_stats(K, xt_ap, nsub, ss, rstd, t_x, t_ss, junk):
    S = K.S
    for s in range(nsub):
        S.op("act", lambda e, s=s: e.activation(out=junk, in_=xt_ap[:, s, :], func=AF.Square, accum_out=ss[:, s:s + 1]),
             reads=[t_x], writes=[t_ss])
    S.op("dve", lambda e: e.tensor_scalar(out=rstd[:, 0:nsub], in0=ss[:, 0:nsub], scalar1=1.0 / D, scalar2=EPS, op0=ALU.mult, op1=ALU.add),
         reads=[t_ss], writes=[t_ss])
    S.op("act", lambda e: e.activation(out=rstd[:, 0:nsub], in_=rstd[:, 0:nsub], func=AF.Sqrt), reads=[t_ss], writes=[t_ss])
    S.op("dve", lambda e: e.reciprocal(out=rstd[:, 0:nsub], in_=rstd[:, 0:nsub]), reads=[t_ss], writes=[t_ss])


def load_bcast_row(K, q, dst, src_row, t_dst):
    n = dst.shape[1]
    K.S.dma(q, lambda e: e.dma_start(out=dst, in_=src_row.unsqueeze(0).to_broadcast([128, n])), writes=[t_dst])


def make_cast_jobs(K, layers):
    jobs = []
    for l in layers:
        w1v = K.w1[l].rearrange("(dc p) f -> p dc f", p=128)
        w1bv = K.w1b[l].rearrange("(dc p) f -> p dc f", p=128)
        for c in range(16):
            jobs.append((w1v[:, :, c * 256:(c + 1) * 256], w1bv[:, :, c * 256:(c + 1) * 256], [128, 8, 256], K.t_w1b[l][c]))
    for l in layers:
        w2v = K.w2[l].rearrange("(fc p) d -> p fc d", p=128)
        w2bv = K.w2b[l].rearrange("(fc p) d -> p fc d", p=128)
        for c in range(16):
            jobs.append((w2v[:, 2 * c:2 * c + 2, :], w2bv[:, 2 * c:2 * c + 2, :], [128, 2, 1024], K.t_w2b[l][c]))
    return jobs


class Caster:
    def __init__(self, K, jobs):
        self.K, self.jobs, self.i = K, jobs, 0
        A = K.A
        self.sf = [A.alloc([128, 2048], F32) for _ in range(2)]
        self.sb = [A.alloc([128, 2048], BF16) for _ in range(2)]
        self.tf = [T(), T()]
        self.tb = [T(), T()]

    def step(self, n=1):
        S = self.K.S
        for _ in range(n):
            if self.i >= len(self.jobs):
                return
            src, dst, shp, tdst = self.jobs[self.i]
            b = self.i % 2
            self.i += 1
            names = "p (a b) -> p a b"
            sf = self.sf[b].rearrange(names, a=shp[1]); sb = self.sb[b].rearrange(names, a=shp[1])
            S.dma("sp", lambda e, sf=sf, src=src: e.dma_start(out=sf, in_=src), writes=[self.tf[b]])
            S.op("dve", lambda e, sf=sf, sb=sb: e.tensor_copy(out=sb, in_=sf), reads=[self.tf[b]], writes=[self.tb[b]])
            S.dma("sp", lambda e, sb=sb, dst=dst: e.dma_start(out=dst, in_=sb), reads=[self.tb[b]], writes=[tdst])

    def flush(self):
        self.step(len(self.jobs))


def phase_mlp(K, xin, xout, layer, gain, final_gain=None):
    S, A, nc = K.S, K.A, K.nc
    m0 = A.mark()
    TT = 256
    NT = S_LEN // TT
    W1b = A.alloc([128, 8, DFF], BF16)
    W2b = A.alloc([128, 32, D], BF16)
    xt = [A.alloc([128, 2, D], F32) for _ in range(2)]
    hb = A.alloc([128, 2, D], BF16)
    hT = [A.alloc([128, 8, TT], BF16) for _ in range(2)]
    a2T = A.alloc([128, 32, TT], BF16)
    rt = [A.alloc([128, TT], F32) for _ in range(2)]
    gt = A.alloc([128, D], F32)
    gft = A.alloc([128, D], F32) if final_gain is not None else None
    junk = A.alloc([128, D], BF16)
    ss = A.alloc([128, 4], F32)
    rstd = A.alloc([128, 4], F32)
    t_w1 = [T() for _ in range(16)]
    t_w2 = [T() for _ in range(16)]
    t_xt = [T(), T()]
    t_hb, t_hT, t_a2 = T(), [T(), T()], [T() for _ in range(32)]
    t_rt = [T(), T()]
    t_g, t_ss, t_junk = T(), T(), T()
    t_psT, t_psA, t_psB = [T(), T()], [T(), T()], [T(), T()]
    ps = K.ps
    psT = [psum_bf16(ps[:, 0:512]).rearrange("p (a b) -> p a b", a=4), psum_bf16(ps[:, 512:1024]).rearrange("p (a b) -> p a b", a=4)]
    psA = [ps[:, 1024:1024 + TT], ps[:, 1536:1536 + TT]]
    psB = [ps[:, 2048:2560], ps[:, 2560:3072]]

    load_bcast_row(K, "sp", gt, gain, t_g)
    if gft is not None:
        load_bcast_row(K, "sp", gft, final_gain, t_g)

    w1bv = K.w1b[layer].rearrange("(dc p) f -> p dc f", p=128)
    w2bv = K.w2b[layer].rearrange("(fc p) d -> p fc d", p=128)
    for c in range(16):
        S.dma("pool", lambda e, c=c: e.dma_start(out=W1b[:, :, c * 256:(c + 1) * 256], in_=w1bv[:, :, c * 256:(c + 1) * 256]),
              reads=[K.t_w1b[layer][c]], writes=[t_w1[c]])
    for c in range(16):
        S.dma("pool", lambda e, c=c: e.dma_start(out=W2b[:, 2 * c:2 * c + 2, :], in_=w2bv[:, 2 * c:2 * c + 2, :]),
              reads=[K.t_w2b[layer][c]], writes=[t_w2[c]])

    xin_v = xin.rearrange("(n s p) d -> n p s d", s=2, p=128)
    xout_v = xout.rearrange("(n s p) d -> n p s d", s=2, p=128)

    def load_x(n):
        b = n % 2
        S.dma("sp", lambda e: e.dma_start(out=xt[b], in_=xin_v[n]), writes=[t_xt[b]])

    load_x(0)
    load_x(1)
    def do_tile(n):
        b = n % 2
        x_ = xt[b]
        emit_rmsnorm_stats(K, x_, 2, ss, rstd, t_xt[b], t_ss, junk)
        for s in range(2):
            S.op("dve", lambda e, s=s: e.scalar_tensor_tensor(out=hb[:, s, :], in0=x_[:, s, :], scalar=rstd[:, s:s + 1], in1=gt,
                                                               op0=ALU.mult, op1=ALU.mult), reads=[t_xt[b], t_ss, t_g], writes=[t_hb])
        for half in range(2):
            for dcl in range(4):
                dc = half * 4 + dcl
                for s in range(2):
                    S.op("pe", lambda e, half=half, dcl=dcl, dc=dc, s=s: e.transpose(
                        psT[half][:, dcl, s * 128:(s + 1) * 128], hb[:, s, dc * 128:(dc + 1) * 128], K.identb),
                        reads=[t_hb, K.t_const], writes=[t_psT[half]])
            eng = "act" if half == 0 else "dve"
            if eng == "act":
                S.op("act", lambda e, half=half: e.activation(out=hT[b][:, half * 4:half * 4 + 4, :], in_=psT[half], func=AF.Copy),
                     reads=[t_psT[half]], writes=[t_hT[b]])
            else:
                S.op("dve", lambda e, half=half: e.tensor_copy(out=hT[b][:, half * 4:half * 4 + 4, :], in_=psT[half]),
                     reads=[t_psT[half]], writes=[t_hT[b]])
        for fc in range(32):
            pb = fc % 2
            for dc in range(8):
                S.op("pe", lambda e, fc=fc, dc=dc, pb=pb: e.matmul(psA[pb], W1b[:, dc, fc * 128:(fc + 1) * 128], hT[b][:, dc, :],
                                                                  start=(dc == 0), stop=(dc == 7)),
                     reads=[t_w1[fc // 2], t_hT[b]], writes=[t_psA[pb]])
            S.op("act", lambda e, pb=pb: e.activation(out=rt[pb], in_=psA[pb], func=AF.Relu), reads=[t_psA[pb]], writes=[t_rt[pb]])
            S.op("dve", lambda e, fc=fc, pb=pb: e.tensor_tensor(out=a2T[:, fc, :], in0=rt[pb], in1=rt[pb], op=ALU.mult),
                 reads=[t_rt[pb]], writes=[t_a2[fc]])
        k = 0
        for s in range(2):
            for dh in range(2):
                pb = k % 2; k += 1
                for fc in range(32):
                    S.op("pe", lambda e, fc=fc, s=s, dh=dh, pb=pb: e.matmul(psB[pb], a2T[:, fc, s * 128:(s + 1) * 128],
                                                                          W2b[:, fc, dh * 512:(dh + 1) * 512], start=(fc == 0), stop=(fc == 31)),
                         reads=[t_a2[fc], t_w2[fc // 2]], writes=[t_psB[pb]])
                S.op("dve", lambda e, s=s, dh=dh, pb=pb: e.tensor_tensor(out=x_[:, s, dh * 512:(dh + 1) * 512], in0=psB[pb],
                                                                       in1=x_[:, s, dh * 512:(dh + 1) * 512], op=ALU.add),
                     reads=[t_psB[pb]], writes=[t_xt[b]])
        if final_gain is not None:
            emit_rmsnorm_stats(K, x_, 2, ss[:, 2:4], rstd[:, 2:4], t_xt[b], t_ss, junk)
            for s in range(2):
                S.op("dve", lambda e, s=s: e.scalar_tensor_tensor(out=x_[:, s, :], in0=x_[:, s, :], scalar=rstd[:, 2 + s:3 + s], in1=gft,
                                                                   op0=ALU.mult, op1=ALU.mult), reads=[t_ss, t_g], writes=[t_xt[b]])
        S.dma("sp", lambda e, n=n: e.dma_start(out=xout_v[n], in_=x_), reads=[t_xt[b]], is_out=K.is_out(xout))
        if n + 2 < NT:
            load_x(n + 2)

    for n in range(NT):
        do_tile(n)
    S.barrier()
    A.release(m0)


def phase_mix0(K, xin, xout, P):
    S, A, ps = K.S, K.A, K.ps
    m_phase = A.mark()
    PS = lambda b: ps[:, b * 512:(b + 1) * 512]
    PSB = lambda b: psum_bf16(ps[:, b * 512:(b + 1) * 512])
    t_bank = [T() for _ in range(8)]
    qT = A.alloc([128, 4, S_LEN], BF16)
    kT = A.alloc([128, 2, S_LEN], BF16)
    Vaug = A.alloc([128, 32, 2, 128], BF16)
    t_qT, t_kT, t_V = T(), T(), T()
    Wq = A.alloc([128, 8, 512], BF16)
    Wkv = A.alloc([128, 8, 256], BF16)
    WY = [A.alloc([128, 8, 512], BF16) for _ in range(2)]
    gqk = A.alloc([128, 2], F32)
    cm = {}
    for nm in ("rmat", "onesblk", "bdc", "bds", "bdsn", "c64bd", "s64bdn"):
        cm[nm] = A.alloc([128, 128], BF16)
    t_w = T()
    for nm in cm:
        S.dma("sp", lambda e, nm=nm: e.dma_start(out=cm[nm], in_=P[nm]), writes=[t_w])
    for col, src in ((0, P["q_norm"]), (1, P["k_norm"])):
        for hh in range(2):
            S.dma("sp", lambda e, col=col, src=src, hh=hh: e.dma_start(out=gqk[hh * 64:(hh + 1) * 64, col:col + 1], in_=src.unsqueeze(1)),
                  writes=[t_w])
    S.op("pool", lambda e: e.memset(kT[64:128, 0, :], 0.0), writes=[t_kT])
    S.op("pool", lambda e: e.memset(kT[0:64, 1, :], 0.0), writes=[t_kT])
    S.op("pool", lambda e: e.memset(Vaug[:, :, 0, 64:128], 1.0), writes=[t_V])
    S.op("pool", lambda e: e.memset(Vaug[:, :, 1, 0:64], 1.0), writes=[t_V])
    m_prep = A.mark()

    stg = A.alloc([128, 8, 512], F32)
    WAb = A.alloc([128, 8, 512], BF16)
    WAT = A.alloc([128, 4, D], BF16)
    Wf2 = A.alloc([128, 4, 128], F32)
    Wf2b = A.alloc([128, 4, 128], BF16)
    BDG = [A.alloc([128, 4, 128], BF16) for _ in range(2)]
    t_stg, t_wab, t_wat, t_wf, t_bdg = T(), T(), T(), T(), T()
    w_in = P["w_in"]
    for c in range(4):
        for hh in range(2):
            col = 512 + 64 * (c + 4 * hh)
            S.dma("sp", lambda e, c=c, hh=hh, col=col: e.dma_start(out=stg[:, :, c * 128 + hh * 64:c * 128 + hh * 64 + 64],
                                                                   in_=w_in[:, col:col + 64].rearrange("(dc p) d -> p dc d", p=128)),
                  writes=[t_stg])
    S.op("dve", lambda e: e.tensor_copy(out=Wq, in_=stg), reads=[t_stg], writes=[t_w])
    S.dma("sp", lambda e: e.dma_start(out=stg[:, :, 0:256], in_=w_in[:, 1024:1280].rearrange("(dc p) f -> p dc f", p=128)),
          writes=[t_stg])
    S.op("dve", lambda e: e.tensor_copy(out=Wkv, in_=stg[:, :, 0:256]), reads=[t_stg], writes=[t_w])
    S.dma("sp", lambda e: e.dma_start(out=stg, in_=w_in[:, 0:512].rearrange("(dc p) f -> p dc f", p=128)), writes=[t_stg])
    S.op("dve", lambda e: e.tensor_copy(out=WAb, in_=stg), reads=[t_stg], writes=[t_wab])
    for cc in range(4):
        bk = cc % 2
        for dc in range(8):
            S.op("pe", lambda e, cc=cc, dc=dc, bk=bk: e.transpose(PSB(bk)[:, dc * 128:(dc + 1) * 128], WAb[:, dc, cc * 128:(cc + 1) * 128], K.identb),
                 reads=[t_wab, K.t_const], writes=[t_bank[bk]])
        S.op("act", lambda e, cc=cc, bk=bk: e.activation(out=WAT[:, cc, :], in_=PSB(bk), func=AF.Copy), reads=[t_bank[bk]], writes=[t_wat])
    wf = P["w_fnet"].rearrange("(ch hh) c d -> hh c ch d", hh=2)
    S.op("pool", lambda e: e.memset(Wf2, 0.0), writes=[t_wf])
    for hh in range(2):
        S.dma("sp", lambda e, hh=hh: e.dma_start(out=Wf2[hh * 64:(hh + 1) * 64, :, hh * 64:(hh + 1) * 64], in_=wf[hh]), writes=[t_wf])
    S.op("pool", lambda e: e.tensor_copy(out=Wf2b, in_=Wf2), reads=[t_wf], writes=[t_wf])
    for gi, nm in enumerate(("c64bd", "s64bdn")):
        bk = 2 + gi
        for ch in range(4):
            S.op("pe", lambda e, ch=ch, bk=bk, nm=nm: e.matmul(PS(bk)[:, ch * 128:(ch + 1) * 128], cm[nm], Wf2b[:, ch, :], start=True, stop=True),
                 reads=[t_w, t_wf], writes=[t_bank[bk]])
        S.op("dve", lambda e, gi=gi, bk=bk: e.tensor_copy(out=BDG[gi].rearrange("p a b -> p (a b)"), in_=PS(bk)), reads=[t_bank[bk]], writes=[t_bdg])
    k = 0
    for gi in range(2):
        for dc in range(8):
            bk = 4 + (k % 2); k += 1
            for ch in range(4):
                S.op("pe", lambda e, gi=gi, dc=dc, ch=ch, bk=bk: e.matmul(PS(bk)[:, ch * 128:(ch + 1) * 128], WAT[:, ch, dc * 128:(dc + 1) * 128],
                                                                       BDG[gi][:, ch, :], start=True, stop=True),
                     reads=[t_wat, t_bdg], writes=[t_bank[bk]])
            eng = "act" if k % 2 else "dve"
            if eng == "act":
                S.op("act", lambda e, gi=gi, dc=dc, bk=bk: e.activation(out=WY[gi][:, dc, :], in_=PS(bk), func=AF.Copy), reads=[t_bank[bk]], writes=[t_w])
            else:
                S.op("dve", lambda e, gi=gi, dc=dc, bk=bk: e.tensor_copy(out=WY[gi][:, dc, :], in_=PS(bk)), reads=[t_bank[bk]], writes=[t_w])
    S.barrier()
    A.release(m_prep)

    xt = [A.alloc([128, 2, D], F32) for _ in range(2)]
    hb = A.alloc([128, 2, D], BF16)
    hT = [A.alloc([128, 8, 512], BF16) for _ in range(2)]
    gt = A.alloc([128, D], F32)
    junk = A.alloc([128, D], BF16)
    ss = A.alloc([128, 4], F32); rstd = A.alloc([128, 4], F32)
    cst = A.alloc([128, 2, 512], F32)
    sq = [A.alloc([128, 512], BF16) for _ in range(2)]
    rq = [A.alloc([128, 512], F32) for _ in range(2)]
    qn = [A.alloc([128, 512], F32) for _ in range(2)]
    qnb = [A.alloc([128, 512], BF16) for _ in range(2)]
    Ys = [[A.alloc([128, 512], BF16) for _ in range(2)] for _ in range(2)]
    PQs = [A.alloc([128, 2, 512], BF16) for _ in range(2)]
    t_xt, t_hb, t_hT, t_g, t_ss, t_cs = [T(), T()], T(), [T(), T()], T(), T(), T()
    t_sq, t_rq, t_qn, t_qnb = [T(), T()], [T(), T()], [T(), T()], [T(), T()]
    t_Ys, t_PQs, t_pqh = [[T(), T()], [T(), T()]], [T(), T()], T()
    load_bcast_row(K, "sp", gt, P["norm_mix0"], t_g)
    xv = xin.rearrange("(a j b4) d -> j b4 a d", j=32, b4=4)
    xov = xout.rearrange("(a j b4) d -> j b4 a d", j=32, b4=4)
    pqh = P["pqh"]
    pqh_w = pqh.rearrange("(j p) q f -> j p q f", p=128)

    def load_x(h):
        b = h % 2
        for s in range(2):
            for b4 in range(4):
                S.dma("sp", lambda e, s=s, b4=b4: e.dma_start(out=xt[b][b4 * 32:(b4 + 1) * 32, s, :], in_=xv[2 * h + s, b4]), writes=[t_xt[b]])

    def norm_T(h):
        b = h % 2
        st, off = h // 2, (h % 2) * 256
        emit_rmsnorm_stats(K, xt[b], 2, ss, rstd, t_xt[b], t_ss, junk)
        for s in range(2):
            S.op("dve", lambda e, s=s: e.scalar_tensor_tensor(out=hb[:, s, :], in0=xt[b][:, s, :], scalar=rstd[:, s:s + 1], in1=gt,
                                                               op0=ALU.mult, op1=ALU.mult), reads=[t_xt[b], t_ss, t_g], writes=[t_hb])
        hTs = hT[st % 2]
        for half in range(2):
            pv = PSB(half).rearrange("p (a b) -> p a b", a=4)
            for dcl in range(4):
                dc = half * 4 + dcl
                for s in range(2):
                    S.op("pe", lambda e, pv=pv, dcl=dcl, dc=dc, s=s: e.transpose(pv[:, dcl, s * 128:(s + 1) * 128], hb[:, s, dc * 128:(dc + 1) * 128], K.identb),
                         reads=[t_hb, K.t_const], writes=[t_bank[half]])
            if half == 0:
                S.op("act", lambda e, pv=pv: e.activation(out=hTs[:, 0:4, off:off + 256], in_=pv, func=AF.Copy), reads=[t_bank[0]], writes=[t_hT[st % 2]])
            else:
                S.op("dve", lambda e, pv=pv: e.tensor_copy(out=hTs[:, 4:8, off:off + 256], in_=pv), reads=[t_bank[1]], writes=[t_hT[st % 2]])

    def supertile(st):
        hTs, thT = hT[st % 2], t_hT[st % 2]
        n0 = st * 512
        S.dma("sp", lambda e: e.dma_start(out=cst[:, 0, :], in_=P["cosp"][:, n0:n0 + 512]), writes=[t_cs])
        S.dma("sp", lambda e: e.dma_start(out=cst[:, 1, :], in_=P["sinp"][:, n0:n0 + 512]), writes=[t_cs])

        def step1(i):
            bk = 2 + i % 2
            for dc in range(8):
                lhsT = Wq[:, dc, i * 128:(i + 1) * 128] if i < 4 else Wkv[:, dc, 0:128]
                S.op("pe", lambda e, lhsT=lhsT, dc=dc, bk=bk: e.matmul(PS(bk), lhsT, hTs[:, dc, :], start=(dc == 0), stop=(dc == 7)),
                     reads=[t_w, thT], writes=[t_bank[bk]])
            S.op("act", lambda e, bk=bk, i=i: e.activation(out=sq[i % 2], in_=PS(bk), func=AF.Square), reads=[t_bank[bk]], writes=[t_sq[i % 2]])

        def step2(i):
            bk = 2 + i % 2
            j = i % 2
            S.op("pe", lambda e: e.matmul(PS(4), cm["onesblk"], sq[j], start=True, stop=True), reads=[t_sq[j], t_w], writes=[t_bank[4]])
            S.op("act", lambda e: e.activation(out=rq[j], in_=PS(4), func=AF.Sqrt, bias=K.eps_ap, scale=1.0), reads=[t_bank[4], K.t_const], writes=[t_rq[j]])
            S.op("dve", lambda e: e.reciprocal(out=rq[j], in_=rq[j]), reads=[t_rq[j]], writes=[t_rq[j]])
            gcol = gqk[:, 0:1] if i < 4 else gqk[:, 1:2]
            S.op("dve", lambda e: e.scalar_tensor_tensor(out=qn[j], in0=PS(bk), scalar=gcol, in1=rq[j], op0=ALU.mult, op1=ALU.mult),
                 reads=[t_bank[bk], t_rq[j], t_w], writes=[t_qn[j]])
            S.op("pool", lambda e: e.tensor_copy(out=qnb[j], in_=qn[j]), reads=[t_qn[j]], writes=[t_qnb[j]])
            S.op("dve", lambda e: e.tensor_tensor(out=qn[j], in0=qn[j], in1=cst[:, 0, :], op=ALU.mult), reads=[t_cs, t_qnb[j]], writes=[t_qn[j]])

        def step3(i):
            j = i % 2
            S.op("pe", lambda e: e.matmul(PS(5), cm["rmat"], qnb[j], start=True, stop=True), reads=[t_qnb[j], t_w], writes=[t_bank[5]])
            S.op("dve", lambda e: e.tensor_tensor(out=rq[j], in0=PS(5), in1=cst[:, 1, :], op=ALU.mult), reads=[t_bank[5], t_cs], writes=[t_rq[j]])
            if i < 4:
                S.op("pool", lambda e: e.tensor_tensor(out=qT[:, i, n0:n0 + 512], in0=qn[j], in1=rq[j], op=ALU.add), reads=[t_qn[j], t_rq[j]], writes=[t_qT])
            else:
                for kv in range(2):
                    S.op("pool", lambda e, kv=kv: e.tensor_tensor(out=kT[kv * 64:(kv + 1) * 64, kv, n0:n0 + 512], in0=qn[j][kv * 64:(kv + 1) * 64, :],
                                                                  in1=rq[j][kv * 64:(kv + 1) * 64, :], op=ALU.add), reads=[t_qn[j], t_rq[j]], writes=[t_kT])

        for kk in range(7):
            if kk < 5:
                step1(kk)
            if 0 <= kk - 1 < 5:
                step2(kk - 1)
            if 0 <= kk - 2 < 5:
                step3(kk - 2)
        for tl in range(4):
            jt = st * 4 + tl
            for dc in range(8):
                S.op("pe", lambda e, dc=dc, tl=tl: e.matmul(PS(6)[:, 0:128], hTs[:, dc, tl * 128:(tl + 1) * 128], Wkv[:, dc, 128:256],
                                                           start=(dc == 0), stop=(dc == 7)), reads=[t_w, thT], writes=[t_bank[6]])
            S.op("act", lambda e, jt=jt: e.activation(out=Vaug[:, jt, 0, 0:64], in_=PS(6)[:, 0:64], func=AF.Copy), reads=[t_bank[6]], writes=[t_V])
            S.op("act", lambda e, jt=jt: e.activation(out=Vaug[:, jt, 1, 64:128], in_=PS(6)[:, 64:128], func=AF.Copy), reads=[t_bank[6]], writes=[t_V])
        for tl in range(4):
            jt = st * 4 + tl
            r = jt % 2
            for gi in range(2):
                bk = 6 + gi
                for dc in range(8):
                    S.op("pe", lambda e, gi=gi, dc=dc, tl=tl, bk=bk: e.matmul(PS(bk), hTs[:, dc, tl * 128:(tl + 1) * 128], WY[gi][:, dc, :],
                                                                          start=(dc == 0), stop=(dc == 7)), reads=[t_w, thT], writes=[t_bank[bk]])
                if gi == 0:
                    S.op("act", lambda e, r=r, bk=bk: e.activation(out=Ys[0][r], in_=PS(bk), func=AF.Copy), reads=[t_bank[bk]], writes=[t_Ys[0][r]])
                else:
                    S.op("dve", lambda e, r=r, bk=bk: e.tensor_copy(out=Ys[1][r], in_=PS(bk)), reads=[t_bank[bk]], writes=[t_Ys[1][r]])
            S.op("pe", lambda e, r=r: e.matmul(PS(4), cm["bdc"], Ys[0][r], start=True, stop=False), reads=[t_w, t_Ys[0][r]], writes=[t_bank[4]])
            S.op("pe", lambda e, r=r: e.matmul(PS(4), cm["bds"], Ys[1][r], start=False, stop=True), reads=[t_w, t_Ys[1][r]], writes=[t_bank[4]])
            S.op("pe", lambda e, r=r: e.matmul(PS(5), cm["bdc"], Ys[1][r], start=True, stop=False), reads=[t_w, t_Ys[1][r]], writes=[t_bank[5]])
            S.op("pe", lambda e, r=r: e.matmul(PS(5), cm["bdsn"], Ys[0][r], start=False, stop=True), reads=[t_w, t_Ys[0][r]], writes=[t_bank[5]])
            S.op("act", lambda e, r=r: e.activation(out=PQs[r][:, 0, :], in_=PS(4), func=AF.Copy), reads=[t_bank[4]], writes=[t_PQs[r]])
            S.op("dve", lambda e, r=r: e.tensor_copy(out=PQs[r][:, 1, :], in_=PS(5)), reads=[t_bank[5]], writes=[t_PQs[r]])
            S.dma("sp", lambda e, r=r, jt=jt: e.dma_start(out=pqh_w[jt], in_=PQs[r]), reads=[t_PQs[r]], writes=[t_pqh])

    load_x(0)
    load_x(1)
    nh = [0]

    def prep_half():
        h = nh[0]; nh[0] += 1
        if h < 16:
            norm_T(h)
            if h + 2 < 16:
                load_x(h + 2)

    prep_half(); prep_half()
    for st in range(8):
        prep_half(); prep_half()
        supertile(st)
    S.barrier()
    A.release(m_prep)

    faT = A.alloc([128, 4, S_LEN], BF16)
    attT = A.alloc([128, 4, S_LEN], BF16)
    t_fa, t_att = T(), T()
    m_C = A.mark()
    CB = A.alloc([128, S_LEN], BF16); SB = A.alloc([128, S_LEN], BF16)
    Pp = [A.alloc([128, 4, 2, 512], BF16) for _ in range(2)]
    t_cb, t_Pp = T(), [T(), T()]
    S.dma("sp", lambda e: e.dma_start(out=CB, in_=P["cbt"]), writes=[t_cb])
    S.dma("sp", lambda e: e.dma_start(out=SB, in_=P["sbt"]), writes=[t_cb])
    pqh_r = pqh.rearrange("(b tau) q f -> b tau q f", tau=32)
    k = 0
    for tg in range(8):
        r = tg % 2
        S.dma("sp", lambda e, tg=tg, r=r: e.dma_start(out=Pp[r], in_=pqh_r[:, 4 * tg:4 * tg + 4]), reads=[t_pqh], writes=[t_Pp[r]])
        for fc in range(4):
            bk = k % 2; k += 1
            for tl in range(4):
                tau = 4 * tg + tl
                S.op("pe", lambda e, r=r, fc=fc, tl=tl, tau=tau, bk=bk: e.matmul(PS(bk)[:, tl * 128:(tl + 1) * 128], Pp[r][:, tl, 0, fc * 128:(fc + 1) * 128],
                                                                             CB[:, tau::32], start=True, stop=False), reads=[t_Pp[r], t_cb], writes=[t_bank[bk]])
                S.op("pe", lambda e, r=r, fc=fc, tl=tl, tau=tau, bk=bk: e.matmul(PS(bk)[:, tl * 128:(tl + 1) * 128], Pp[r][:, tl, 1, fc * 128:(fc + 1) * 128],
                                                                             SB[:, tau::32], start=False, stop=True), reads=[t_Pp[r], t_cb], writes=[t_bank[bk]])
            src = PS(bk).rearrange("p (tl mq mr) -> p tl mr mq", tl=4, mq=32, mr=4)
            dst = faT[:, fc, :].rearrange("p (mr tau mq) -> p tau mr mq", mr=4, tau=32, mq=32)[:, 4 * tg:4 * tg + 4]
            if k % 2:
                S.op("act", lambda e, src=src, dst=dst: e.activation(out=dst, in_=src, func=AF.Copy), reads=[t_bank[bk]], writes=[t_fa])
            else:
                S.op("dve", lambda e, src=src, dst=dst: e.tensor_copy(out=dst, in_=src), reads=[t_bank[bk]], writes=[t_fa])
    S.barrier()
    A.release(m_C)

    pT = [A.alloc([128, 1024], BF16) for _ in range(3)]
    rec = [A.alloc([128, 512], F32) for _ in range(2)]
    t_pT, t_rec = [T(), T(), T()], [T(), T()]
    t_sp = [T(), T()]
    caster = Caster(K, K.cast_jobs)
    PSP = lambda a: ps[:, a * 1024:(a + 1) * 1024]
    it = 0
    for c in range(4):
        for hh in range(2):
            lo, hi = hh * 64, hh * 64 + 64
            for qt in range(8):
                ob = 4 + it % 2
                def score(kp, c=c, qt=qt, hh=hh):
                    a = kp % 2
                    for u in range(2):
                        kt = 2 * kp + u
                        S.op("pe", lambda e, kt=kt, u=u: e.matmul(PSP(a)[:, u * 512:(u + 1) * 512], kT[:, hh, kt * 128:(kt + 1) * 128], qT[:, c, qt * 512:(qt + 1) * 512],
                                                                 start=True, stop=True), reads=[t_kT, t_qT], writes=[t_sp[a]])
                score(0)
                for kp in range(16):
                    a = kp % 2
                    pb = kp % 3
                    if kp + 1 < 16:
                        score(kp + 1)
                    for u in range(2):
                        S.op("act", lambda e, a=a, pb=pb, u=u: e.activation(out=pT[pb][:, u * 512:(u + 1) * 512], in_=PSP(a)[:, u * 512:(u + 1) * 512], func=AF.Exp, scale=0.125),
                             reads=[t_sp[a]], writes=[t_pT[pb]])
                    for u in range(2):
                        kt = 2 * kp + u
                        S.op("pe", lambda e, kt=kt, u=u, pb=pb, hh=hh, ob=ob: e.matmul(PS(ob), Vaug[:, kt, hh, :], pT[pb][:, u * 512:(u + 1) * 512], start=(kt == 0), stop=(kt == 31)),
                             reads=[t_V, t_pT[pb]], writes=[t_bank[ob]])
                rb = it % 2
                dlo, dhi = (64, 128) if hh == 0 else (0, 64)
                S.op("dve", lambda e, ob=ob, rb=rb, lo=lo, hi=hi, dlo=dlo, dhi=dhi: e.reciprocal(out=rec[rb][lo:hi, :], in_=PS(ob)[dlo:dhi, :]),
                     reads=[t_bank[ob]], writes=[t_rec[rb]])
                S.op("dve", lambda e, ob=ob, rb=rb, lo=lo, hi=hi, c=c, qt=qt: e.tensor_tensor(out=attT[lo:hi, c, qt * 512:(qt + 1) * 512], in0=PS(ob)[lo:hi, :],
                                                                                            in1=rec[rb][lo:hi, :], op=ALU.mult),
                     reads=[t_bank[ob], t_rec[rb]], writes=[t_att])
                it += 1
                caster.step(1)
    caster.flush()
    S.barrier()
    A.release(m_C)

    Wo = A.alloc([128, 8, D], BF16)
    stg2 = A.alloc([128, 4, D], F32)
    xt2 = [A.alloc([128, D], F32) for _ in range(3)]
    t_wo, t_stg2, t_xt2 = T(), T(), [T(), T(), T()]
    w_out = P["w_out"]
    S.dma("sp", lambda e: e.dma_start(out=stg2, in_=w_out[0:512, :].rearrange("(fc p) n -> p fc n", p=128)), writes=[t_stg2])
    S.op("pool", lambda e: e.tensor_copy(out=Wo[:, 0:4, :], in_=stg2), reads=[t_stg2], writes=[t_wo])
    wo_att = w_out[512:1024, :].rearrange("(hh c d) n -> hh d c n", hh=2, c=4)
    for hh in range(2):
        S.dma("sp", lambda e, hh=hh: e.dma_start(out=stg2[hh * 64:(hh + 1) * 64, :, :], in_=wo_att[hh]), writes=[t_stg2])
    S.op("pool", lambda e: e.tensor_copy(out=Wo[:, 4:8, :], in_=stg2), reads=[t_stg2], writes=[t_wo])

    def load_x2(jt):
        b = jt % 3
        for b4 in range(4):
            S.dma("sp", lambda e, b4=b4: e.dma_start(out=xt2[b][b4 * 32:(b4 + 1) * 32, :], in_=xv[jt, b4]), writes=[t_xt2[b]])

    load_x2(0)
    load_x2(1)
    k = 0
    for jt in range(32):
        b = jt % 3
        if jt + 2 < 32:
            load_x2(jt + 2)
        for dh in range(2):
            bk = k % 2; k += 1
            for fc in range(8):
                lhsT = faT[:, fc, jt * 128:(jt + 1) * 128] if fc < 4 else attT[:, fc - 4, jt * 128:(jt + 1) * 128]
                S.op("pe", lambda e, lhsT=lhsT, fc=fc, dh=dh, bk=bk: e.matmul(PS(bk), lhsT, Wo[:, fc, dh * 512:(dh + 1) * 512], start=(fc == 0), stop=(fc == 7)),
                     reads=[t_fa, t_att, t_wo], writes=[t_bank[bk]])
            S.op("dve", lambda e, b=b, dh=dh, bk=bk: e.tensor_tensor(out=xt2[b][:, dh * 512:(dh + 1) * 512], in0=PS(bk), in1=xt2[b][:, dh * 512:(dh + 1) * 512], op=ALU.add),
                 reads=[t_bank[bk]], writes=[t_xt2[b]])
        for b4 in range(4):
            S.dma("sp", lambda e, b=b, b4=b4, jt=jt: e.dma_start(out=xov[jt, b4], in_=xt2[b][b4 * 32:(b4 + 1) * 32, :]), reads=[t_xt2[b]],
                  is_out=K.is_out(xout))
    S.barrier()
    A.release(m_phase)


MAGIC = 12582912.0
TWO_PI_LO = 6.28318


def phase_mix1(K, xin, xout, P):
    S, A, ps = K.S, K.A, K.ps
    m_phase = A.mark()
    PS = lambda b: ps[:, b * 512:(b + 1) * 512]
    PSB = lambda b: psum_bf16(ps[:, b * 512:(b + 1) * 512])
    t_bank = [T() for _ in range(8)]
    Tm = A.alloc([128, 64, 128], BF16, top=True)
    Vre = A.alloc([128, 64, 128], BF16, top=True)
    Vimn = A.alloc([128, 64, 128], BF16, top=True)
    W1 = A.alloc([128, 64, 2, 128], BF16, top=True)
    r8 = A.alloc([128, 64], F32, top=True)
    y8 = A.alloc([128, 64], F32, top=True)
    iok = A.alloc([128, 512], F32, top=True)
    identf = A.alloc([128, 128], F32, top=True)
    t_U = [T() for _ in range(64)]
    t_prm = T()
    S.dma("sp", lambda e: e.dma_start(out=iok, in_=P["iotak"]), writes=[t_prm])
    S.dma("sp", lambda e: e.dma_start(out=identf, in_=P["identf"]), writes=[t_prm])
    m_prep = A.mark()

    CTr = A.alloc([128, 64, 16], F32); CTi = A.alloc([128, 64, 16], F32)
    Bbr = A.alloc([128, 64, 16], F32); Bbi = A.alloc([128, 64, 16], F32)
    S1r = A.alloc([128, 64, 8], F32); S1i = A.alloc([128, 64, 8], F32)
    S2r = A.alloc([128, 64, 8], F32); S2i = A.alloc([128, 64, 8], F32)
    S3r = A.alloc([128, 64, 8], F32); S3i = A.alloc([128, 64, 8], F32)
    Dm = A.alloc([128, 64], F32); sel = A.alloc([128, 128], F32); Dvec = A.alloc([128, 64], F32)
    mskf = A.alloc([128, 128], F32); mskb = A.alloc([128, 128], F32); dperm = A.alloc([128, 128], F32)
    m_p1 = A.mark()
    rows = A.alloc([128, 3, 128], F32)
    ldt = A.alloc([128, 2], F32)
    LR = A.alloc([128, 64], F32); LI = A.alloc([128, 64], F32); DT = A.alloc([128, 64], F32)
    BR = A.alloc([128, 64, 16], F32); BI = A.alloc([128, 64, 16], F32)
    Crow1 = A.alloc([128, 32, 128], F32)
    Crow = [Crow1, Crow1]
    t_rows, t_l, t_B, t_ct = T(), T(), T(), T()
    t_c1 = T(); t_crow = [t_c1, t_c1]
    v2 = lambda ap: ap.rearrange("dir (G2 g2) p -> (dir G2) (g2 p)", g2=2)
    S.dma("sp", lambda e: e.dma_start(out=rows[0:64, 0, :], in_=v2(P["lam_re"])), writes=[t_rows])
    S.dma("sp", lambda e: e.dma_start(out=rows[0:64, 1, :], in_=v2(P["lam_im"])), writes=[t_rows])
    S.dma("sp", lambda e: e.dma_start(out=ldt[0:64, :], in_=P["log_dt"].rearrange("dir (G2 g2) -> (dir G2) g2", g2=2)), writes=[t_rows])
    S.op("dve", lambda e: e.tensor_copy(out=rows[0:64, 2, :].rearrange("p (g2 q) -> p g2 q", g2=2),
                                        in_=ldt[0:64, :].unsqueeze(2).to_broadcast([64, 2, 64])), reads=[t_rows], writes=[t_rows])
    for i, dst in enumerate((LR, LI, DT)):
        S.op("pe", lambda e, i=i: e.transpose(PS(0)[:, i * 64:(i + 1) * 64], rows[0:64, i, :], identf[0:64, 0:64]), reads=[t_rows, t_prm], writes=[t_bank[0]])
    for i, dst in enumerate((LR, LI, DT)):
        S.op("dve", lambda e, i=i, dst=dst: e.tensor_copy(out=dst, in_=PS(0)[:, i * 64:(i + 1) * 64]), reads=[t_bank[0]], writes=[t_l])
    bsrc = lambda ap: ap.rearrange("dir (G2 g2) p c -> (g2 p) (dir G2) c", g2=2)
    S.dma("sp", lambda e: e.dma_start(out=BR, in_=bsrc(P["b_re"])), writes=[t_B])
    S.dma("sp", lambda e: e.dma_start(out=BI, in_=bsrc(P["b_im"])), writes=[t_B])
    for ri, (nm, dst) in enumerate((("c_re", CTr), ("c_im", CTi))):
        for d in range(2):
            cr = Crow[d]
            S.dma("sp", lambda e, nm=nm, d=d, cr=cr: e.dma_start(out=cr[0:16].rearrange("c G2 (g2 p) -> c G2 g2 p", g2=2),
                                                                in_=P[nm][d].rearrange("(G2 g2) c p -> c G2 g2 p", g2=2)), writes=[t_crow[d]])
            bk = 1 + d
            for G2 in range(32):
                S.op("pe", lambda e, cr=cr, G2=G2, bk=bk: e.transpose(PS(bk)[:, G2 * 16:(G2 + 1) * 16], cr[0:16, G2, :], identf[0:16, 0:16]),
                     reads=[t_crow[d], t_prm], writes=[t_bank[bk]])
            S.op("dve", lambda e, dst=dst, d=d, bk=bk: e.tensor_copy(out=dst[:, d * 32:(d + 1) * 32, :].rearrange("p a b -> p (a b)"), in_=PS(bk)),
                 reads=[t_bank[bk]], writes=[t_ct])
    t_D = T()
    S.dma("sp", lambda e: e.dma_start(out=Dm[0:16, :], in_=P["d_skip"].rearrange("(g c) -> c g", c=16), allow_slow_non_contiguous=True), writes=[t_D])
    S.dma("sp", lambda e: e.dma_start(out=sel[0:16, :], in_=P["sel16"]), writes=[t_D])
    S.dma("sp", lambda e: e.dma_start(out=mskf, in_=P["mskf"]), writes=[t_D])
    S.dma("sp", lambda e: e.dma_start(out=mskb, in_=P["mskb"]), writes=[t_D])
    S.dma("sp", lambda e: e.dma_start(out=dperm, in_=P["dperm"]), writes=[t_D])
    S.op("pe", lambda e: e.matmul(PS(3)[:, 0:64], sel[0:16, :], Dm[0:16, :], start=True, stop=True), reads=[t_D], writes=[t_bank[3]])
    S.op("dve", lambda e: e.tensor_copy(out=Dvec, in_=PS(3)[:, 0:64]), reads=[t_bank[3]], writes=[t_D])

    sm = lambda: A.alloc([128, 64], F32)
    dt_, ar, ai, yf, tmp, tmp2 = sm(), sm(), sm(), sm(), sm(), sm()
    PWr = A.alloc([128, 9, 64], F32); PWi = A.alloc([128, 9, 64], F32)
    NPr = A.alloc([128, 8, 64], F32); NPi = A.alloc([128, 8, 64], F32)
    cs_n = A.alloc([128, 9, 64], F32); sn_n = A.alloc([128, 9, 64], F32)
    t_s = T()
    V = lambda eng, fn: S.op(eng, fn, reads=[t_l, t_s], writes=[t_s])
    V("act", lambda e: e.activation(out=dt_, in_=DT, func=AF.Exp))
    V("dve", lambda e: e.tensor_tensor(out=ar, in0=LR, in1=dt_, op=ALU.mult))
    V("dve", lambda e: e.tensor_tensor(out=ai, in0=LI, in1=dt_, op=ALU.mult))
    V("dve", lambda e: e.tensor_scalar(out=yf, in0=ai, scalar1=1.0 / (2 * np.pi), scalar2=None, op0=ALU.mult))
    V("dve", lambda e: e.tensor_scalar(out=tmp, in0=yf, scalar1=MAGIC, scalar2=MAGIC, op0=ALU.add, op1=ALU.subtract))
    V("dve", lambda e: e.tensor_tensor(out=yf, in0=yf, in1=tmp, op=ALU.subtract))

    def sincos(yv, n, sdst, cdst):
        V("dve", lambda e: e.tensor_scalar(out=tmp, in0=yv, scalar1=float(n), scalar2=None, op0=ALU.mult))
        V("dve", lambda e: e.tensor_scalar(out=tmp2, in0=tmp, scalar1=MAGIC, scalar2=MAGIC, op0=ALU.add, op1=ALU.subtract))
        V("dve", lambda e: e.tensor_tensor(out=tmp2, in0=tmp, in1=tmp2, op=ALU.subtract))
        V("act", lambda e: e.activation(out=sdst, in_=tmp2, func=AF.Sin, scale=TWO_PI_LO))
        V("dve", lambda e: e.tensor_scalar(out=tmp, in0=tmp, scalar1=0.25, scalar2=None, op0=ALU.add))
        V("dve", lambda e: e.tensor_scalar(out=tmp2, in0=tmp, scalar1=MAGIC, scalar2=MAGIC, op0=ALU.add, op1=ALU.subtract))
        V("dve", lambda e: e.tensor_tensor(out=tmp2, in0=tmp, in1=tmp2, op=ALU.subtract))
        V("act", lambda e: e.activation(out=cdst, in_=tmp2, func=AF.Sin, scale=TWO_PI_LO))

    for n in range(9):
        sincos(yf, n, sn_n[:, n, :], cs_n[:, n, :])
        V("act", lambda e, n=n: e.activation(out=tmp, in_=ar, func=AF.Exp, scale=float(n)))
        V("dve", lambda e, n=n: e.tensor_tensor(out=PWr[:, n, :], in0=tmp, in1=cs_n[:, n, :], op=ALU.mult))
        V("dve", lambda e, n=n: e.tensor_tensor(out=PWi[:, n, :], in0=tmp, in1=sn_n[:, n, :], op=ALU.mult))
        if n < 8:
            V("act", lambda e, n=n: e.activation(out=tmp, in_=ar, func=AF.Exp, scale=-float(n)))
            V("dve", lambda e, n=n: e.tensor_tensor(out=NPr[:, n, :], in0=tmp, in1=cs_n[:, n, :], op=ALU.mult))
            V("dve", lambda e, n=n: e.scalar_tensor_tensor(out=NPi[:, n, :], in0=tmp, scalar=-1.0, in1=sn_n[:, n, :], op0=ALU.mult, op1=ALU.mult))
    S.op("act", lambda e: e.activation(out=r8, in_=ar, func=AF.Exp, scale=8.0), reads=[t_s], writes=[t_prm])
    V("dve", lambda e: e.tensor_scalar(out=tmp, in0=yf, scalar1=8.0, scalar2=None, op0=ALU.mult))
    V("dve", lambda e: e.tensor_scalar(out=tmp2, in0=tmp, scalar1=MAGIC, scalar2=MAGIC, op0=ALU.add, op1=ALU.subtract))
    S.op("dve", lambda e: e.tensor_tensor(out=y8, in0=tmp, in1=tmp2, op=ALU.subtract), reads=[t_s], writes=[t_prm])
    cre, cim, den, e1 = sm(), sm(), sm(), sm()
    V("dve", lambda e: e.tensor_scalar(out=e1, in0=PWr[:, 1, :], scalar1=-1.0, scalar2=None, op0=ALU.add))
    V("dve", lambda e: e.tensor_tensor(out=den, in0=LR, in1=LR, op=ALU.mult))
    V("dve", lambda e: e.tensor_tensor(out=tmp, in0=LI, in1=LI, op=ALU.mult))
    V("dve", lambda e: e.tensor_tensor(out=den, in0=den, in1=tmp, op=ALU.add))
    V("dve", lambda e: e.reciprocal(out=den, in_=den))
    V("dve", lambda e: e.tensor_tensor(out=cre, in0=e1, in1=LR, op=ALU.mult))
    V("dve", lambda e: e.tensor_tensor(out=tmp, in0=PWi[:, 1, :], in1=LI, op=ALU.mult))
    V("dve", lambda e: e.tensor_tensor(out=cre, in0=cre, in1=tmp, op=ALU.add))
    V("dve", lambda e: e.tensor_tensor(out=cre, in0=cre, in1=den, op=ALU.mult))
    V("dve", lambda e: e.tensor_tensor(out=cim, in0=PWi[:, 1, :], in1=LR, op=ALU.mult))
    V("dve", lambda e: e.tensor_tensor(out=tmp, in0=e1, in1=LI, op=ALU.mult))
    V("dve", lambda e: e.tensor_tensor(out=cim, in0=cim, in1=tmp, op=ALU.subtract))
    V("dve", lambda e: e.tensor_tensor(out=cim, in0=cim, in1=den, op=ALU.mult))
    tb = A.alloc([128, 64, 16], F32)
    bc = lambda ap: ap.unsqueeze(2).to_broadcast([128, 64, 16])
    VB = lambda eng, fn: S.op(eng, fn, reads=[t_s, t_B, t_ct], writes=[t_s])
    VB("dve", lambda e: e.tensor_tensor(out=Bbr, in0=BR, in1=bc(cre), op=ALU.mult))
    VB("dve", lambda e: e.tensor_tensor(out=tb, in0=BI, in1=bc(cim), op=ALU.mult))
    VB("dve", lambda e: e.tensor_tensor(out=Bbr, in0=Bbr, in1=tb, op=ALU.subtract))
    VB("dve", lambda e: e.tensor_tensor(out=Bbi, in0=BI, in1=bc(cre), op=ALU.mult))
    VB("dve", lambda e: e.tensor_tensor(out=tb, in0=BR, in1=bc(cim), op=ALU.mult))
    VB("dve", lambda e: e.tensor_tensor(out=Bbi, in0=Bbi, in1=tb, op=ALU.add))
    for i in range(8):
        for (dr, di, sr, si, nf, nb) in ((S1r, S1i, PWr, PWi, 7 - i, i), (S2r, S2i, PWr, PWi, i + 1, 8 - i), (S3r, S3i, NPr, NPi, 7 - i, i)):
            V("pool", lambda e, dr=dr, sr=sr, nf=nf, i=i: e.tensor_copy(out=dr[:, 0:32, i], in_=sr[:, nf, 0:32]))
            V("pool", lambda e, dr=dr, sr=sr, nb=nb, i=i: e.tensor_copy(out=dr[:, 32:64, i], in_=sr[:, nb, 32:64]))
            V("pool", lambda e, di=di, si=si, nf=nf, i=i: e.tensor_copy(out=di[:, 0:32, i], in_=si[:, nf, 0:32]))
            V("pool", lambda e, di=di, si=si, nb=nb, i=i: e.tensor_copy(out=di[:, 32:64, i], in_=si[:, nb, 32:64]))
    dump(K, "LR", LR, [t_l]); dump(K, "LI", LI, [t_l]); dump(K, "DT", DT, [t_l])
    dump(K, "PWr", PWr, [t_s]); dump(K, "PWi", PWi, [t_s]); dump(K, "NPr", NPr, [t_s]); dump(K, "NPi", NPi, [t_s])
    dump(K, "cre", cre, [t_s]); dump(K, "cim", cim, [t_s]); dump(K, "r8", r8, [t_prm]); dump(K, "y8", y8, [t_prm])
    dump(K, "Bbr", Bbr, [t_s]); dump(K, "Bbi", Bbi, [t_s]); dump(K, "CTr", CTr, [t_ct]); dump(K, "CTi", CTi, [t_ct])
    dump(K, "S1r", S1r, [t_s]); dump(K, "S2i", S2i, [t_s]); dump(K, "S3r", S3r, [t_s]); dump(K, "Dvec", Dvec, [t_D])
    S.barrier()
    A.release(m_p1)
    W1Tr = A.alloc([128, 64, 128], BF16); W1Ti = A.alloc([128, 64, 128], BF16)
    Vnr = A.alloc([128, 64, 128], BF16); Vnin = A.alloc([128, 64, 128], BF16)
    ta = A.alloc([128, 16, 128], F32); tb2 = A.alloc([128, 16, 128], F32)
    t_s = T()
    VB = lambda eng, fn: S.op(eng, fn, reads=[t_s], writes=[t_s])
    for d in range(4):
        cs = slice(d * 16, (d + 1) * 16)
        pw = lambda ap, cs=cs: ap[:, cs, :].unsqueeze(2).to_broadcast([128, 16, 16, 8])
        bb = lambda ap, cs=cs: ap[:, cs, :].unsqueeze(3).to_broadcast([128, 16, 16, 8])
        o4 = lambda ap: ap.rearrange("p a (c i) -> p a c i", c=16)
        VB("dve", lambda e, pw=pw, bb=bb: e.tensor_tensor(out=o4(ta), in0=pw(S1r), in1=bb(Bbr), op=ALU.mult))
        VB("dve", lambda e, pw=pw, bb=bb: e.tensor_tensor(out=o4(tb2), in0=pw(S1i), in1=bb(Bbi), op=ALU.mult))
        VB("dve", lambda e, cs=cs: e.tensor_tensor(out=W1Tr[:, cs, :], in0=ta, in1=tb2, op=ALU.subtract))
        VB("dve", lambda e, pw=pw, bb=bb: e.tensor_tensor(out=o4(ta), in0=pw(S1r), in1=bb(Bbi), op=ALU.mult))
        VB("dve", lambda e, pw=pw, bb=bb: e.tensor_tensor(out=o4(tb2), in0=pw(S1i), in1=bb(Bbr), op=ALU.mult))
        VB("dve", lambda e, cs=cs: e.tensor_tensor(out=W1Ti[:, cs, :], in0=ta, in1=tb2, op=ALU.add))
        pj = lambda ap, cs=cs: ap[:, cs, :].unsqueeze(3).to_broadcast([128, 16, 8, 16])
        cc = lambda ap, cs=cs: ap[:, cs, :].unsqueeze(2).to_broadcast([128, 16, 8, 16])
        o5 = lambda ap: ap.rearrange("p a (j c) -> p a j c", j=8)
        for (Sr, Si, dre, dimn) in ((S2r, S2i, Vre, Vimn), (S3r, S3i, Vnr, Vnin)):
            VB("dve", lambda e, pj=pj, cc=cc, Sr=Sr: e.tensor_tensor(out=o5(ta), in0=pj(Sr), in1=cc(CTr), op=ALU.mult))
            VB("dve", lambda e, pj=pj, cc=cc, Si=Si: e.tensor_tensor(out=o5(tb2), in0=pj(Si), in1=cc(CTi), op=ALU.mult))
            VB("dve", lambda e, cs=cs, dre=dre: e.tensor_tensor(out=dre[:, cs, :], in0=ta, in1=tb2, op=ALU.subtract))
            VB("dve", lambda e, pj=pj, cc=cc, Si=Si: e.tensor_tensor(out=o5(ta), in0=pj(Si), in1=cc(CTr), op=ALU.mult))
            VB("dve", lambda e, pj=pj, cc=cc, Sr=Sr: e.tensor_tensor(out=o5(tb2), in0=pj(Sr), in1=cc(CTi), op=ALU.mult))
            VB("dve", lambda e: e.tensor_tensor(out=ta, in0=ta, in1=tb2, op=ALU.add))
            VB("dve", lambda e, cs=cs, dimn=dimn: e.tensor_scalar(out=dimn[:, cs, :], in0=ta, scalar1=-1.0, scalar2=None, op0=ALU.mult))
    k = 0
    for ri, src in enumerate((W1Tr, W1Ti)):
        for c8 in range(8):
            bk = 4 + k % 2; k += 1
            for cl in range(8):
                col = c8 * 8 + cl
                S.op("pe", lambda e, src=src, col=col, cl=cl, bk=bk: e.transpose(PSB(bk)[:, cl * 128:(cl + 1) * 128], src[:, col, :], K.identb),
                     reads=[t_s, K.t_const], writes=[t_bank[bk]])
            S.op("act", lambda e, ri=ri, c8=c8, bk=bk: e.activation(out=W1[:, c8 * 8:(c8 + 1) * 8, ri, :], in_=PSB(bk).rearrange("p (a b) -> p a b", a=8), func=AF.Copy),
                 reads=[t_bank[bk]], writes=[t_prm])
    tq = [A.alloc([128, 128], F32) for _ in range(2)]
    t_tq = [T(), T()]
    for g in range(64):
        G2, g2 = g // 2, g % 2
        lo, hi = g2 * 64, g2 * 64 + 64
        bf_, bb_ = 6, 7
        for (bk, col) in ((bf_, G2), (bb_, 32 + G2)):
            S.op("pe", lambda e, bk=bk, col=col, lo=lo, hi=hi: e.matmul(PS(bk)[:, 0:128], W1Tr[lo:hi, col, :], Vnr[lo:hi, col, :], start=True, stop=False),
                 reads=[t_s], writes=[t_bank[bk]])
            S.op("pe", lambda e, bk=bk, col=col, lo=lo, hi=hi: e.matmul(PS(bk)[:, 0:128], W1Ti[lo:hi, col, :], Vnin[lo:hi, col, :], start=False, stop=True),
                 reads=[t_s], writes=[t_bank[bk]])
        j = g % 2
        S.op("dve", lambda e, j=j: e.tensor_tensor(out=tq[j], in0=PS(6)[:, 0:128], in1=mskf, op=ALU.mult), reads=[t_bank[6], t_D], writes=[t_tq[j]])
        S.op("dve", lambda e, j=j: e.tensor_tensor(out=PS(7)[:, 128:256], in0=PS(7)[:, 0:128], in1=mskb, op=ALU.mult), reads=[t_D], writes=[t_bank[7]])
        S.op("dve", lambda e, j=j: e.tensor_tensor(out=tq[j], in0=PS(7)[:, 128:256], in1=tq[j], op=ALU.add), reads=[t_bank[7]], writes=[t_tq[j]])
        S.op("dve", lambda e, j=j, g=g: e.scalar_tensor_tensor(out=Tm[:, g, :], in0=dperm, scalar=Dvec[:, g:g + 1], in1=tq[j], op0=ALU.mult, op1=ALU.add),
             reads=[t_tq[j], t_D, t_prm], writes=[t_prm])
    dump(K, "W1Tr", W1Tr, [t_s]); dump(K, "W1Ti", W1Ti, [t_s]); dump(K, "Vnr", Vnr, [t_s]); dump(K, "Vnin", Vnin, [t_s])
    dump(K, "Vre", Vre, [t_s]); dump(K, "Vimn", Vimn, [t_s]); dump(K, "Tm", Tm, [t_prm]); dump(K, "W1", W1, [t_prm])
    S.barrier()
    A.release(m_prep)

    U = A.alloc([128, 64, 512], BF16)
    m_U = A.mark()
    xt = A.alloc([128, 8, D], F32)
    ub = A.alloc([128, D, 8], BF16)
    gt = A.alloc([128, D], F32)
    junk = A.alloc([128, D], BF16)
    ss = A.alloc([128, 8], F32); rstd = A.alloc([128, 8], F32)
    t_xt, t_ub, t_g, t_ss = T(), T(), T(), T()
    load_bcast_row(K, "sp", gt, P["norm_mix1"], t_g)
    xk = xin.rearrange("(kb kp i) d -> kb kp i d", kp=128, i=8)
    for kb in range(4):
        S.dma("sp", lambda e, kb=kb: e.dma_start(out=xt, in_=xk[kb]), writes=[t_xt])
        emit_rmsnorm_stats(K, xt, 8, ss, rstd, t_xt, t_ss, junk)
        for i in range(8):
            S.op("dve", lambda e, i=i: e.scalar_tensor_tensor(out=ub[:, :, i], in0=xt[:, i, :], scalar=rstd[:, i:i + 1], in1=gt, op0=ALU.mult, op1=ALU.mult),
                 reads=[t_xt, t_ss, t_g], writes=[t_ub])
        ubf = ub.rearrange("p d i -> p (d i)")
        for g8 in range(8):
            bk = g8 % 2
            for gl in range(8):
                g = g8 * 8 + gl
                S.op("pe", lambda e, g=g, gl=gl, bk=bk: e.transpose(PSB(bk)[:, gl * 128:(gl + 1) * 128], ubf[:, g * 128:(g + 1) * 128], K.identb),
                     reads=[t_ub, K.t_const], writes=[t_bank[bk]])
            src = PSB(bk).rearrange("p (a b) -> p a b", a=8)
            dst = U[:, g8 * 8:(g8 + 1) * 8, kb * 128:(kb + 1) * 128]
            wr = [t_U[g8 * 8 + gl] for gl in range(8)]
            if g8 % 2 == 0:
                S.op("act", lambda e, src=src, dst=dst: e.activation(out=dst, in_=src, func=AF.Copy), reads=[t_bank[bk]], writes=wr)
            else:
                S.op("dve", lambda e, src=src, dst=dst: e.tensor_copy(out=dst, in_=src), reads=[t_bank[bk]], writes=wr)
    dump(K, "U", U, t_U)
    S.barrier()
    A.release(m_U)

    NB = 3
    mk = lambda: [A.alloc([128, 512], F32) for _ in range(NB)]
    ys_, fs_, sn_, cs_, xr_, xi_, wr_, wi_ = [mk() for _ in range(8)]
    t_t = [[T() for _ in range(8)] for _ in range(NB)]
    t_E = [T() for _ in range(4)]
    Es = [[A.alloc([128, 512], BF16) for _ in range(2)] for _ in range(4)]
    for sl in range(4):
        for ri in range(2):
            S.op("pool", lambda e, sl=sl, ri=ri: e.memset(Es[sl][ri], 0.0), writes=[t_E[sl]])
    SQ2 = float(np.sqrt(2.0))

    def level1(G2):
        for d in range(2):
            for ri in range(2):
                bk = d * 2 + ri
                for g2 in range(2):
                    g = 2 * G2 + g2
                    S.op("pe", lambda e, bk=bk, d=d, ri=ri, g2=g2, g=g: e.matmul(PS(bk)[g2 * 64:(g2 + 1) * 64, :], W1[:, d * 32 + G2, ri, g2 * 64:(g2 + 1) * 64],
                                                                             U[:, g, :], start=True, stop=True),
                         reads=[t_prm, t_U[g]], writes=[t_bank[bk]])

    def bufs(n):
        b = n % NB
        return [lst[b] for lst in (ys_, fs_, sn_, cs_, xr_, xi_, wr_, wi_)], t_t[b]

    def scan1a(G2, d):
        (ys, fs, sn, cs, xr, xi, wr, wi), tt = bufs(2 * G2 + d)
        col = d * 32 + G2
        rv = (lambda ap: ap[:, ::-1]) if d == 1 else (lambda ap: ap)
        S.op("act", lambda e: e.activation(out=xr, in_=rv(PS(d * 2 + 0)), func=AF.Copy), reads=[t_bank[d * 2]], writes=[tt[4]])
        S.op("act", lambda e: e.activation(out=xi, in_=rv(PS(d * 2 + 1)), func=AF.Copy), reads=[t_bank[d * 2 + 1]], writes=[tt[5]])
        S.op("act", lambda e: e.activation(out=ys, in_=iok, func=AF.Identity, scale=y8[:, col:col + 1]), reads=[t_prm], writes=[tt[0]])
        S.op("act", lambda e: e.activation(out=fs, in_=ys, func=AF.Identity, bias=K.magic_ap), reads=[tt[0], K.t_const], writes=[tt[1]])
        S.op("act", lambda e: e.activation(out=fs, in_=fs, func=AF.Identity, bias=K.nmagic_ap), reads=[tt[1], K.t_const], writes=[tt[1]])

    def scan1b(G2, d):
        (ys, fs, sn, cs, xr, xi, wr, wi), tt = bufs(2 * G2 + d)
        S.op("dve", lambda e: e.tensor_tensor(out=fs, in0=ys, in1=fs, op=ALU.subtract), reads=[tt[0], tt[1]], writes=[tt[1]])
        S.op("act", lambda e: e.activation(out=sn, in_=fs, func=AF.Sin, scale=TWO_PI_LO), reads=[tt[1]], writes=[tt[2]])
        S.op("act", lambda e: e.activation(out=ys, in_=fs, func=AF.Sin, scale=TWO_PI_LO / 2), reads=[tt[1]], writes=[tt[0]])
        S.op("act", lambda e: e.activation(out=cs, in_=ys, func=AF.Square, scale=SQ2), reads=[tt[0]], writes=[tt[3]])
        S.op("act", lambda e: e.activation(out=cs, in_=cs, func=AF.Identity, scale=-1.0, bias=K.one_ap), reads=[tt[3], K.t_const], writes=[tt[3]])

    def scan2a(G2, d):
        (ys, fs, sn, cs, xr, xi, wr, wi), tt = bufs(2 * G2 + d)
        S.op("dve", lambda e: e.tensor_tensor(out=ys, in0=xr, in1=cs, op=ALU.mult), reads=[tt[4], tt[3]], writes=[tt[0]])
        S.op("dve", lambda e: e.tensor_tensor(out=fs, in0=xi, in1=sn, op=ALU.mult), reads=[tt[5], tt[2]], writes=[tt[1]])
        S.op("pool", lambda e: e.tensor_tensor(out=wr, in0=ys, in1=fs, op=ALU.add), reads=[tt[0], tt[1]], writes=[tt[6]])
        S.op("dve", lambda e: e.tensor_tensor(out=ys, in0=xi, in1=cs, op=ALU.mult), reads=[tt[5], tt[3]], writes=[tt[0]])
        S.op("dve", lambda e: e.tensor_tensor(out=fs, in0=xr, in1=sn, op=ALU.mult), reads=[tt[4], tt[2]], writes=[tt[1]])
        S.op("pool", lambda e: e.tensor_tensor(out=wi, in0=ys, in1=fs, op=ALU.subtract), reads=[tt[0], tt[1]], writes=[tt[7]])

    def scan2b(G2, d):
        (ys, fs, sn, cs, xr, xi, wr, wi), tt = bufs(2 * G2 + d)
        col = d * 32 + G2
        rb = r8[:, col:col + 1].to_broadcast([128, 512])
        S.op("dve", lambda e: e.tensor_tensor_scan(out=xr, data0=rb, data1=wr, initial=0.0, op0=ALU.mult, op1=ALU.add), reads=[t_prm, tt[6]], writes=[tt[4]])
        S.op("dve", lambda e: e.tensor_tensor_scan(out=xi, data0=rb, data1=wi, initial=0.0, op0=ALU.mult, op1=ALU.add), reads=[t_prm, tt[7]], writes=[tt[5]])

    def edst(G2, d):
        sl = (G2 % 2) * 2 + d
        if d == 0:
            return sl, Es[sl][0][:, 1:512], Es[sl][1][:, 1:512]
        return sl, Es[sl][0][:, 0:511][:, ::-1], Es[sl][1][:, 0:511][:, ::-1]

    def scan3a(G2, d):
        (ys, fs, sn, cs, zr, zi, wr, wi), tt = bufs(2 * G2 + d)
        S.op("dve", lambda e: e.tensor_tensor(out=wr, in0=zr, in1=cs, op=ALU.mult), reads=[tt[4], tt[3]], writes=[tt[6]])
        S.op("dve", lambda e: e.tensor_tensor(out=wi, in0=zi, in1=sn, op=ALU.mult), reads=[tt[5], tt[2]], writes=[tt[7]])

    def scan3b(G2, d):
        (ys, fs, sn, cs, zr, zi, wr, wi), tt = bufs(2 * G2 + d)
        sl, dre, dim = edst(G2, d)
        S.op("pool", lambda e: e.tensor_tensor(out=ys, in0=zi, in1=cs, op=ALU.mult), reads=[tt[5], tt[3]], writes=[tt[0]])
        S.op("pool", lambda e: e.tensor_tensor(out=fs, in0=zr, in1=sn, op=ALU.mult), reads=[tt[4], tt[2]], writes=[tt[1]])
        S.op("pool", lambda e: e.tensor_tensor(out=dre, in0=wr[:, 0:511], in1=wi[:, 0:511], op=ALU.subtract), reads=[tt[6], tt[7]], writes=[t_E[sl]])
        S.op("pool", lambda e: e.tensor_tensor(out=dim, in0=ys[:, 0:511], in1=fs[:, 0:511], op=ALU.add), reads=[tt[0], tt[1]], writes=[t_E[sl]])

    def output(G2):
        for g2 in range(2):
            g = 2 * G2 + g2
            lo, hi = g2 * 64, g2 * 64 + 64
            bk = 4 + (g % 4)
            for kb in range(4):
                ks = slice(kb * 128, (kb + 1) * 128)
                o = PS(bk)[:, kb * 128:(kb + 1) * 128]
                S.op("pe", lambda e, o=o, g=g, ks=ks: e.matmul(o, U[:, g, ks], Tm[:, g, :], start=True, stop=False), reads=[t_U[g], t_prm], writes=[t_bank[bk]])
                n = 0
                for d in range(2):
                    sl = (G2 % 2) * 2 + d
                    col = d * 32 + G2
                    for ri, Vm in ((0, Vre), (1, Vimn)):
                        n += 1
                        S.op("pe", lambda e, o=o, sl=sl, ri=ri, Vm=Vm, col=col, ks=ks, lo=lo, hi=hi, n=n: e.matmul(o, Es[sl][ri][lo:hi, ks], Vm[lo:hi, col, :],
                                                                                                         start=False, stop=(n == 4)),
                             reads=[t_E[sl], t_prm], writes=[t_bank[bk]])
            if g % 2 == 0:
                S.op("act", lambda e, g=g, bk=bk: e.activation(out=U[:, g, :], in_=PS(bk), func=AF.Copy), reads=[t_bank[bk]], writes=[t_U[g]])
            else:
                S.op("dve", lambda e, g=g, bk=bk: e.tensor_copy(out=U[:, g, :], in_=PS(bk)), reads=[t_bank[bk]], writes=[t_U[g]])

    for t in range(64 + 2):
        if t < 64:
            if t % 2 == 0:
                level1(t // 2)
            scan1a(t // 2, t % 2)
        if 0 <= t - 1 < 64:
            scan2a((t - 1) // 2, (t - 1) % 2)
        if 0 <= t - 2 < 64:
            scan3a((t - 2) // 2, (t - 2) % 2)
        if t < 64:
            scan1b(t // 2, t % 2)
        if 0 <= t - 1 < 64:
            scan2b((t - 1) // 2, (t - 1) % 2)
        if 0 <= t - 2 < 64:
            scan3b((t - 2) // 2, (t - 2) % 2)
            if (t - 2) % 2 == 1:
                output((t - 2) // 2)
    dump(K, "Y", U, t_U)
    S.barrier()
    A.release(m_U)
    A.release_top()

    Wg = A.alloc([128, 8, D], BF16)
    stg = A.alloc([128, 4, D], F32)
    bgf = A.alloc([128, D], F32); bgb = A.alloc([128, D], BF16); onesr = A.alloc([128, 128], BF16)
    t_wg, t_stg = T(), T()
    wgv = P["w_gate"].rearrange("(dc p) n -> p dc n", p=128)
    for h in range(2):
        S.dma("sp", lambda e, h=h: e.dma_start(out=stg, in_=wgv[:, 4 * h:4 * h + 4, :]), writes=[t_stg])
        S.op("pool", lambda e, h=h: e.tensor_copy(out=Wg[:, 4 * h:4 * h + 4, :], in_=stg), reads=[t_stg], writes=[t_wg])
    S.dma("sp", lambda e: e.dma_start(out=bgf[0:1, :], in_=P["b_gate"].unsqueeze(0)), writes=[t_stg])
    S.op("pool", lambda e: e.tensor_copy(out=bgb[0:1, :], in_=bgf[0:1, :]), reads=[t_stg], writes=[t_wg])
    S.op("pool", lambda e: e.memset(onesr, 1.0), writes=[t_wg])
    NBG = 2
    sqb = [A.alloc([128, D], F32) for _ in range(NBG)]
    gel = [A.alloc([128, D], F32) for _ in range(NBG)]
    gb = [A.alloc([128, D], BF16) for _ in range(NBG)]
    gT = [A.alloc([128, D], BF16) for _ in range(NBG)]
    sg = [A.alloc([128, 512], F32) for _ in range(2)]
    xt3 = [A.alloc([128, D], F32) for _ in range(3)]
    t_y, t_sq, t_gel, t_gb, t_gT, t_sg, t_x3 = [T(), T()], [T(), T()], [T(), T()], [T(), T()], [T(), T()], [T(), T()], [T(), T(), T()]
    t_Uall = T()
    xs = xin.rearrange("(kb kp j) d -> kb j kp d", kp=128, j=8)
    xo = xout.rearrange("(kb kp j) d -> kb j kp d", kp=128, j=8)
    yall = U.rearrange("p g (kb j c) -> p kb j g c", kb=4, j=8)
    kk = [0]

    def load3(n):
        kb, j = n // 8, n % 8
        S.dma("sp", lambda e: e.dma_start(out=xt3[n % 3], in_=xs[kb, j]), writes=[t_x3[n % 3]])

    load3(0); load3(1)
    y3 = lambda ap: ap.rearrange("p (g c) -> p g c", c=16)

    def gate1(n):
        kb, j = n // 8, n % 8
        b = n % NBG
        yv = yall[:, kb, j]
        S.op("act", lambda e: e.activation(out=y3(sqb[b]), in_=yv, func=AF.Square), reads=t_U, writes=[t_sq[b]])
        S.op("dve", lambda e: e.tensor_scalar(out=sqb[b], in0=sqb[b], scalar1=0.044715, scalar2=1.0, op0=ALU.mult, op1=ALU.add), reads=[t_sq[b]], writes=[t_sq[b]])
        S.op("dve", lambda e: e.tensor_tensor(out=y3(sqb[b]), in0=y3(sqb[b]), in1=yv, op=ALU.mult), reads=t_U + [t_sq[b]], writes=[t_sq[b]])
        S.op("act", lambda e: e.activation(out=sqb[b], in_=sqb[b], func=AF.Sigmoid, scale=1.5957691216), reads=[t_sq[b]], writes=[t_sq[b]])
        S.op("dve", lambda e: e.tensor_tensor(out=y3(gel[b]), in0=y3(sqb[b]), in1=yv, op=ALU.mult), reads=t_U + [t_sq[b]], writes=[t_gel[b]])
        S.op("pool", lambda e: e.tensor_copy(out=gb[b], in_=gel[b]), reads=[t_gel[b]], writes=[t_gb[b]])

    def gate1b(n):
        b = n % NBG
        for dc in range(8):
            S.op("pe", lambda e, dc=dc: e.transpose(PSB(0)[:, dc * 128:(dc + 1) * 128], gb[b][:, dc * 128:(dc + 1) * 128], K.identb),
                 reads=[t_gb[b], K.t_const], writes=[t_bank[0]])
        S.op("act", lambda e: e.activation(out=gT[b], in_=PSB(0), func=AF.Copy), reads=[t_bank[0]], writes=[t_gT[b]])

    def gate2a(n):
        b = n % NBG
        if n + 2 < 32:
            load3(n + 2)
        for dh in range(2):
            bk = 1 + dh
            for dc in range(8):
                S.op("pe", lambda e, dc=dc, dh=dh, bk=bk: e.matmul(PS(bk), gT[b][:, dc * 128:(dc + 1) * 128], Wg[:, dc, dh * 512:(dh + 1) * 512], start=(dc == 0), stop=False),
                     reads=[t_gT[b], t_wg], writes=[t_bank[bk]])
            S.op("pe", lambda e, dh=dh, bk=bk: e.matmul(PS(bk), onesr[0:1, :], bgb[0:1, dh * 512:(dh + 1) * 512], start=False, stop=True), reads=[t_wg], writes=[t_bank[bk]])

    def gate2b(n):
        kb, j = n // 8, n % 8
        b = n % NBG
        xb = xt3[n % 3]
        for dh in range(2):
            bk = 1 + dh
            sb_ = dh
            S.op("act", lambda e, bk=bk, sb_=sb_: e.activation(out=sg[sb_], in_=PS(bk), func=AF.Sigmoid), reads=[t_bank[bk]], writes=[t_sg[sb_]])
            S.op("dve", lambda e, dh=dh, sb_=sb_: e.tensor_tensor(out=sg[sb_], in0=sg[sb_], in1=gel[b][:, dh * 512:(dh + 1) * 512], op=ALU.mult),
                 reads=[t_sg[sb_], t_gel[b]], writes=[t_sg[sb_]])
            S.op("pool", lambda e, dh=dh, sb_=sb_: e.tensor_tensor(out=xb[:, dh * 512:(dh + 1) * 512], in0=xb[:, dh * 512:(dh + 1) * 512], in1=sg[sb_], op=ALU.add),
                 reads=[t_sg[sb_]], writes=[t_x3[n % 3]])
        S.dma("sp", lambda e: e.dma_start(out=xo[kb, j], in_=xb), reads=[t_x3[n % 3]], is_out=K.is_out(xout))

    gate1(0)
    gate1b(0)
    for n in range(32):
        if n + 1 < 32:
            gate1(n + 1)
        gate2a(n)
        gate2b(n)
        if n + 1 < 32:
            gate1b(n + 1)
    S.barrier()
    A.release(m_phase)


def build_program(phases=("mix0", "mlp0", "mix1", "mlp1"), dbg=False):
    nc = bass.Bass("TRN2", target_bir_lowering=False)
    K = Ctx()
    K.dbg = dbg
    K.nc = nc
    din = lambda name, shape, dt=F32: nc.dram_tensor(name, list(shape), dt, kind="ExternalInput").ap()
    x = din("x", [S_LEN, D])
    norm_mix = din("norm_mix", [2, D]); norm_mlp = din("norm_mlp", [2, D])
    w1 = din("mlp_w1", [2, D, DFF]); w2 = din("mlp_w2", [2, DFF, D])
    final_norm = din("final_norm", [D])
    identb_d = din("identb", [128, 128], BF16)
    P = {}
    P["w_in"] = din("w_in", [1, D, 1280])[0]
    P["w_fnet"] = din("w_fnet", [1, 8, 64, 64])[0]
    P["q_norm"] = din("q_norm", [1, 64])[0]
    P["k_norm"] = din("k_norm", [1, 64])[0]
    P["w_out"] = din("w_out", [1, D, D])[0]
    for nm in ("rmat", "onesblk", "bdc", "bds", "bdsn", "c64bd", "s64bdn"):
        P[nm] = din(nm, [128, 128], BF16)
    P["cosp"] = din("cosp", [128, S_LEN]); P["sinp"] = din("sinp", [128, S_LEN])
    P["cbt"] = din("cbt", [128, S_LEN], BF16); P["sbt"] = din("sbt", [128, S_LEN], BF16)
    P["norm_mix0"] = norm_mix[0]
    P["norm_mix1"] = norm_mix[1]
    for nm, shp in (("lam_re", [1, 2, 64, 64]), ("lam_im", [1, 2, 64, 64]), ("log_dt", [1, 2, 64]), ("b_re", [1, 2, 64, 64, 16]), ("b_im", [1, 2, 64, 64, 16]),
                    ("c_re", [1, 2, 64, 16, 64]), ("c_im", [1, 2, 64, 16, 64]), ("d_skip", [1, D]), ("w_gate", [1, D, D]), ("b_gate", [1, D])):
        P[nm] = din(nm, shp)[0]
    P["iotak"] = din("iotak", [128, 512]); P["identf"] = din("identf", [128, 128])
    P["sel16"] = din("sel16", [16, 128]); P["mskf"] = din("mskf", [128, 128]); P["mskb"] = din("mskb", [128, 128]); P["dperm"] = din("dperm", [128, 128])
    P["pqh"] = nc.dram_tensor("pqh", [S_LEN, 2, 512], BF16).ap()
    out = nc.dram_tensor("out", [S_LEN, D], F32, kind="ExternalOutput").ap()
    scr = [nc.dram_tensor("scr%d" % i, [S_LEN, D], F32).ap() for i in range(3)]
    K.out = out
    K.w1 = [w1[0], w1[1]]; K.w2 = [w2[0], w2[1]]
    K.w1b = [nc.dram_tensor("w1b%d" % l, [D, DFF], BF16).ap() for l in range(2)]
    K.w2b = [nc.dram_tensor("w2b%d" % l, [DFF, D], BF16).ap() for l in range(2)]
    K.t_w1b = [[T() for _ in range(16)] for _ in range(2)]
    K.t_w2b = [[T() for _ in range(16)] for _ in range(2)]
    K.is_out = lambda ap: ap is out
    with ExitStack() as es:
        arena = es.enter_context(nc.sbuf_tensor("arena", [128, ARENA_BYTES + 512], U8))
        K.ps = es.enter_context(nc.psum_tensor("ps", [128, 4096], F32))
        K.S = Sched(nc, es)
        K.A = Arena(arena)
        K.identb = K.A.alloc([128, 128], BF16)
        K.t_const = T()
        K.S.dma("sp", lambda e: e.dma_start(out=K.identb, in_=identb_d), writes=[K.t_const])
        K.eps_ap = K.A.alloc([128, 1], F32)
        K.S.op("pool", lambda e: e.memset(K.eps_ap, EPS), writes=[K.t_const])
        K.one_ap = K.A.alloc([128, 1], F32)
        K.S.op("pool", lambda e: e.memset(K.one_ap, 1.0), writes=[K.t_const])
        K.magic_ap = K.A.alloc([128, 1], F32)
        K.S.op("pool", lambda e: e.memset(K.magic_ap, MAGIC), writes=[K.t_const])
        K.nmagic_ap = K.A.alloc([128, 1], F32)
        K.S.op("pool", lambda e: e.memset(K.nmagic_ap, -MAGIC), writes=[K.t_const])
        cur = x
        chain = {"mix0": scr[0], "mlp0": scr[1], "mix1": scr[2], "mlp1": out}
        last = phases[-1]
        layers = [l for l in range(2) if "mlp%d" % l in phases]
        K.cast_jobs = make_cast_jobs(K, layers)
        if "mix0" not in phases:
            m = K.A.mark()
            Caster(K, K.cast_jobs).flush()
            K.S.barrier()
            K.A.release(m)
        for ph in phases:
            dst = out if ph == last else chain[ph]
            if ph == "mix0":
                phase_mix0(K, cur, dst, P)
            elif ph == "mix1":
                phase_mix1(K, cur, dst, P)
            elif ph == "mlp0":
                phase_mlp(K, cur, dst, 0, norm_mlp[0])
            elif ph == "mlp1":
                phase_mlp(K, cur, dst, 1, norm_mlp[1], final_gain=final_norm)
            cur = dst
        K.S.emit()
    return nc


def make_consts():
    c = {}
    bf = ml_dtypes.bfloat16
    c["identb"] = np.eye(128, dtype=np.float32).astype(bf)
    R = np.zeros((128, 128), np.float32)
    for h in range(2):
        for sec in range(2):
            base = h * 64 + sec * 32
            for i in range(16):
                R[base + 16 + i, base + i] = -1.0
                R[base + i, base + 16 + i] = 1.0
    c["rmat"] = R.astype(bf)
    ob = np.zeros((128, 128), np.float32)
    ob[0:64, 0:64] = 1.0 / 64; ob[64:, 64:] = 1.0 / 64
    c["onesblk"] = ob.astype(bf)
    a = np.arange(32, dtype=np.float64)
    c32 = np.cos(2 * np.pi * np.outer(a, a) / 32); s32 = np.sin(2 * np.pi * np.outer(a, a) / 32)
    bdc = np.zeros((128, 128)); bds = np.zeros((128, 128))
    for b4 in range(4):
        bdc[b4 * 32:(b4 + 1) * 32, b4 * 32:(b4 + 1) * 32] = c32
        bds[b4 * 32:(b4 + 1) * 32, b4 * 32:(b4 + 1) * 32] = s32
    c["bdc"] = bdc.astype(np.float32).astype(bf); c["bds"] = bds.astype(np.float32).astype(bf); c["bdsn"] = (-bds).astype(np.float32).astype(bf)
    k64 = np.arange(64, dtype=np.float64)
    c64 = np.cos(2 * np.pi * np.outer(k64, k64) / 64) / 512.0; s64 = -np.sin(2 * np.pi * np.outer(k64, k64) / 64) / 512.0
    cb = np.zeros((128, 128)); sb_ = np.zeros((128, 128))
    for h in range(2):
        cb[h * 64:(h + 1) * 64, h * 64:(h + 1) * 64] = c64
        sb_[h * 64:(h + 1) * 64, h * 64:(h + 1) * 64] = s64
    c["c64bd"] = cb.astype(np.float32).astype(bf); c["s64bdn"] = sb_.astype(np.float32).astype(bf)
    n = np.arange(S_LEN); t = 128 * (n % 32) + n // 32
    row = (t // 64).astype(np.float64); colp = (t % 64).astype(np.float64)
    inv = 10000.0 ** (-np.arange(0, 32, 2, dtype=np.float64) / 32.0)
    cosp = np.zeros((128, S_LEN)); sinp = np.zeros((128, S_LEN))
    for p in range(128):
        d = p % 64
        pos = row if d < 32 else colp
        ang = pos * inv[d % 16]
        cosp[p] = np.cos(ang); sinp[p] = np.sin(ang)
    c["cosp"] = cosp.astype(np.float32); c["sinp"] = sinp.astype(np.float32)
    b = np.arange(128, dtype=np.float64)[:, None]; tt = np.arange(S_LEN, dtype=np.float64)[None, :]
    ph = 2 * np.pi * ((b * tt) % S_LEN) / S_LEN
    c["cbt"] = np.cos(ph).astype(np.float32).astype(bf); c["sbt"] = np.sin(ph).astype(np.float32).astype(bf)
    c["iotak"] = np.tile(np.arange(512, dtype=np.float32)[None, :], (128, 1))
    c["identf"] = np.eye(128, dtype=np.float32)
    sel = np.zeros((16, 128), np.float32)
    for cc in range(16):
        sel[cc, cc * 8:(cc + 1) * 8] = 1.0
    c["sel16"] = sel
    ii = (np.arange(128) % 8)[:, None]
    jj = (np.arange(128) // 16)[None, :]
    c["mskf"] = (ii <= jj).astype(np.float32); c["mskb"] = (ii >= jj).astype(np.float32)
    ci_ = (np.arange(128) // 8)[:, None]; co_ = (np.arange(128) % 16)[None, :]
    c["dperm"] = ((ii == jj) & (ci_ == co_)).astype(np.float32)
    return c


_USED = ("norm_mix", "norm_mlp", "mlp_w1", "mlp_w2", "final_norm", "w_in", "w_fnet", "q_norm", "k_norm", "w_out",
         "lam_re", "lam_im", "log_dt", "b_re", "b_im", "c_re", "c_im", "d_skip", "w_gate", "b_gate")


def kernel(**inputs):
    nc = build_program()
    consts = make_consts()
    x = np.ascontiguousarray(inputs["x"], dtype=np.float32)
    shared = {k: np.ascontiguousarray(inputs[k], dtype=np.float32) for k in _USED}
    shared.update(consts)
    in_maps = []
    for i in range(8):
        m = dict(shared)
        m["x"] = x[i]
        in_maps.append(m)
    res = run_bass_kernel_spmd(nc, in_maps, core_ids=list(range(8)))
    return np.stack([np.asarray(r["out"], dtype=np.float32) for r in res.results], axis=0)
```

```python
import numpy as np
import ml_dtypes
import concourse.bass as bass
import concourse.mybir as mybir
from concourse.bass_utils import run_bass_kernel_spmd
from contextlib import ExitStack

F32 = mybir.dt.float32
BF16 = mybir.dt.bfloat16
U8 = mybir.dt.uint8
ALU = mybir.AluOpType
AF = mybir.ActivationFunctionType
AX = mybir.AxisListType

S_LEN = 4096
D = 1024
DFF = 4096
EPS = 1e-6
ARENA_BYTES = 206 * 1024


class T:
    __slots__ = ("w", "r")

    def __init__(self):
        self.w = []
        self.r = []


class Sched:
    ENG = ("pe", "act", "dve", "pool", "sp")
    N_DMA_SEMS = 32

    def __init__(self, nc, es):
        self.nc = nc
        self.ops = {e: [] for e in self.ENG}
        self.sem = {e: es.enter_context(nc.semaphore("s_" + e)) for e in ("pe", "act", "dve", "pool")}
        self.dsem = [es.enter_context(nc.semaphore("d%d" % i)) for i in range(self.N_DMA_SEMS)]
        self.dcnt = [0] * self.N_DMA_SEMS
        self.dnext = 0
        self.cnt = {e: 0 for e in self.ENG}
        self.out_tokens = []
        self.barrier_deps = []

    @staticmethod
    def _joinable(t):
        return bool(t.w) and not t.r and all(w[0] == "d" for w in t.w)

    def _deps(self, reads, writes, is_dma=False):
        deps = list(self.barrier_deps)
        for t in reads:
            deps.extend(t.w)
        for t in writes:
            if not (is_dma and self._joinable(t)):
                deps.extend(t.w)
            deps.extend(t.r)
        return deps

    def _mark(self, tok, reads, writes, is_dma=False):
        for t in reads:
            t.r.append(tok)
            if len(t.r) > 64:
                t.r = self._compact(t.r)
        for t in writes:
            if is_dma and self._joinable(t):
                t.w = t.w + [tok]
            else:
                t.w = [tok]
            t.r = []

    @staticmethod
    def _compact(toks):
        best = {}
        for d in toks:
            k = (d[0], d[1])
            if k not in best or best[k][2] < d[2]:
                best[k] = d
        return list(best.values())

    def op(self, eng, fn, reads=(), writes=()):
        deps = self._deps(reads, writes)
        self.cnt[eng] += 1
        tok = ("c", eng, self.cnt[eng])
        self.ops[eng].append((fn, deps, tok))
        self._mark(tok, reads, writes)
        return tok

    def dma(self, q, fn, reads=(), writes=(), is_out=False):
        deps = self._deps(reads, writes, True)
        k = self.dnext
        self.dnext = (self.dnext + 1) % self.N_DMA_SEMS
        if self.dcnt[k]:
            deps.append(("d", k, self.dcnt[k]))
        self.dcnt[k] += 16
        tok = ("d", k, self.dcnt[k])
        self.ops[q].append((fn, deps, tok))
        self._mark(tok, reads, writes, True)
        if is_out:
            self.out_tokens.append(tok)
        return tok

    def barrier(self):
        deps = [("c", e, self.cnt[e]) for e in ("pe", "act", "dve", "pool") if self.cnt[e]]
        deps += [("d", k, self.dcnt[k]) for k in range(self.N_DMA_SEMS) if self.dcnt[k]]
        self.barrier_deps = deps

    def emit(self):
        nc = self.nc
        final_deps = list(self.out_tokens)
        with nc.Block() as block:
            def run(engname, e):
                waited = {}

                def do_wait(tok):
                    if tok[0] == "c":
                        if tok[1] == engname and engname == "pe":
                            return
                        key = ("c", tok[1]); sem = self.sem[tok[1]]
                    else:
                        key = ("d", tok[1]); sem = self.dsem[tok[1]]
                    if waited.get(key, 0) >= tok[2]:
                        return
                    waited[key] = tok[2]
                    e.wait_ge(sem, tok[2])

                for fn, deps, tok in self.ops[engname]:
                    for d in self._compact(deps):
                        do_wait(d)
                    ins = fn(e)
                    if tok[0] == "c":
                        ins.then_inc(self.sem[tok[1]], 1)
                    else:
                        ins.then_inc(self.dsem[tok[1]], 16)
                if engname == "sp":
                    for d in final_deps:
                        do_wait(d)

            @block.sync
            def _(e):
                run("sp", e)

            @block.tensor
            def _(e):
                run("pe", e)

            @block.scalar
            def _(e):
                run("act", e)

            @block.vector
            def _(e):
                run("dve", e)

            @block.gpsimd
            def _(e):
                run("pool", e)


class Arena:
    def __init__(self, ap):
        self.ap = ap
        self.off = 0
        self.top = ARENA_BYTES

    def alloc(self, shape, dt, top=False):
        esz = 2 if dt == BF16 else 4
        n = int(np.prod(shape[1:]))
        nb = n * esz
        nba = (nb + 63) // 64 * 64
        assert self.off + nba <= self.top, ("SBUF arena overflow", self.off, self.top, nb)
        if top:
            self.top -= nba
            v = self.ap[:, self.top:self.top + nb].bitcast(dt)
        else:
            v = self.ap[:, self.off:self.off + nb].bitcast(dt)
            self.off += nba
        if len(shape) > 2:
            names = " ".join("a%d" % i for i in range(len(shape) - 1))
            kw = {"a%d" % i: shape[i + 1] for i in range(len(shape) - 2)}
            v = v.rearrange("p (%s) -> p %s" % (names, names), **kw)
        return v

    def mark(self):
        return self.off

    def release(self, m):
        self.off = m

    def release_top(self):
        self.top = ARENA_BYTES


class Ctx:
    pass


def dump(K, name, ap, reads):
    if not getattr(K, "dbg", False):
        return
    t = K.nc.dram_tensor("dbg_" + name, list(ap.shape), ap.dtype, kind="ExternalOutput").ap()
    K.S.dma("sp", lambda e: e.dma_start(out=t, in_=ap), reads=reads, is_out=True)


def psum_bf16(ps_ap):
    return ps_ap.bitcast(BF16)


def emit_rmsnorm_stats(K, xt_ap, nsub, ss, rstd, t_x, t_ss, junk):
    S = K.S
    for s in range(nsub):
        S.op("act", lambda e, s=s: e.activation(out=junk, in_=xt_ap[:, s, :], func=AF.Square, accum_out=ss[:, s:s + 1]),
             reads=[t_x], writes=[t_ss])
    S.op("dve", lambda e: e.tensor_scalar(out=rstd[:, 0:nsub], in0=ss[:, 0:nsub], scalar1=1.0 / D, scalar2=EPS, op0=ALU.mult, op1=ALU.add),
         reads=[t_ss], writes=[t_ss])
    S.op("act", lambda e: e.activation(out=rstd[:, 0:nsub], in_=rstd[:, 0:nsub], func=AF.Sqrt), reads=[t_ss], writes=[t_ss])
    S.op("dve", lambda e: e.reciprocal(out=rstd[:, 0:nsub], in_=rstd[:, 0:nsub]), reads=[t_ss], writes=[t_ss])


def load_bcast_row(K, q, dst, src_row, t_dst):
    n = dst.shape[1]
    K.S.dma(q, lambda e: e.dma_start(out=dst, in_=src_row.unsqueeze(0).to_broadcast([128, n])), writes=[t_dst])


def make_cast_jobs(K, layers):
    jobs = []
    for l in layers:
        w1v = K.w1[l].rearrange("(dc p) f -> p dc f", p=128)
        w1bv = K.w1b[l].rearrange("(dc p) f -> p dc f", p=128)
        for c in range(16):
            jobs.append((w1v[:, :, c * 256:(c + 1) * 256], w1bv[:, :, c * 256:(c + 1) * 256], [128, 8, 256], K.t_w1b[l][c]))
    for l in layers:
        w2v = K.w2[l].rearrange("(fc p) d -> p fc d", p=128)
        w2bv = K.w2b[l].rearrange("(fc p) d -> p fc d", p=128)
        for c in range(16):
            jobs.append((w2v[:, 2 * c:2 * c + 2, :], w2bv[:, 2 * c:2 * c + 2, :], [128, 2, 1024], K.t_w2b[l][c]))
    return jobs


class Caster:
    def __init__(self, K, jobs):
        self.K, self.jobs, self.i = K, jobs, 0
        A = K.A
        self.sf = [A.alloc([128, 2048], F32) for _ in range(2)]
        self.sb = [A.alloc([128, 2048], BF16) for _ in range(2)]
        self.tf = [T(), T()]
        self.tb = [T(), T()]

    def step(self, n=1):
        S = self.K.S
        for _ in range(n):
            if self.i >= len(self.jobs):
                return
            src, dst, shp, tdst = self.jobs[self.i]
            b = self.i % 2
            self.i += 1
            names = "p (a b) -> p a b"
            sf = self.sf[b].rearrange(names, a=shp[1]); sb = self.sb[b].rearrange(names, a=shp[1])
            S.dma("sp", lambda e, sf=sf, src=src: e.dma_start(out=sf, in_=src), writes=[self.tf[b]])
            S.op("dve", lambda e, sf=sf, sb=sb: e.tensor_copy(out=sb, in_=sf), reads=[self.tf[b]], writes=[self.tb[b]])
            S.dma("sp", lambda e, sb=sb, dst=dst: e.dma_start(out=dst, in_=sb), reads=[self.tb[b]], writes=[tdst])

    def flush(self):
        self.step(len(self.jobs))


def phase_mlp(K, xin, xout, layer, gain, final_gain=None):
    S, A, nc = K.S, K.A, K.nc
    m0 = A.mark()
    TT = 256
    NT = S_LEN // TT
    W1b = A.alloc([128, 8, DFF], BF16)
    W2b = A.alloc([128, 32, D], BF16)
    xt = [A.alloc([128, 2, D], F32) for _ in range(2)]
    hb = A.alloc([128, 2, D], BF16)
    hT = [A.alloc([128, 8, TT], BF16) for _ in range(2)]
    a2T = A.alloc([128, 32, TT], BF16)
    rt = [A.alloc([128, TT], F32) for _ in range(2)]
    gt = A.alloc([128, D], F32)
    gft = A.alloc([128, D], F32) if final_gain is not None else None
    junk = A.alloc([128, D], BF16)
    ss = A.alloc([128, 4], F32)
    rstd = A.alloc([128, 4], F32)
    t_w1 = [T() for _ in range(16)]
    t_w2 = [T() for _ in range(16)]
    t_xt = [T(), T()]
    t_hb, t_hT, t_a2 = T(), [T(), T()], [T() for _ in range(32)]
    t_rt = [T(), T()]
    t_g, t_ss, t_junk = T(), T(), T()
    t_psT, t_psA, t_psB = [T(), T()], [T(), T()], [T(), T()]
    ps = K.ps
    psT = [psum_bf16(ps[:, 0:512]).rearrange("p (a b) -> p a b", a=4), psum_bf16(ps[:, 512:1024]).rearrange("p (a b) -> p a b", a=4)]
    psA = [ps[:, 1024:1024 + TT], ps[:, 1536:1536 + TT]]
    psB = [ps[:, 2048:2560], ps[:, 2560:3072]]

    load_bcast_row(K, "sp", gt, gain, t_g)
    if gft is not None:
        load_bcast_row(K, "sp", gft, final_gain, t_g)

    w1bv = K.w1b[layer].rearrange("(dc p) f -> p dc f", p=128)
    w2bv = K.w2b[layer].rearrange("(fc p) d -> p fc d", p=128)
    for c in range(16):
        S.dma("pool", lambda e, c=c: e.dma_start(out=W1b[:, :, c * 256:(c + 1) * 256], in_=w1bv[:, :, c * 256:(c + 1) * 256]),
              reads=[K.t_w1b[layer][c]], writes=[t_w1[c]])
    for c in range(16):
        S.dma("pool", lambda e, c=c: e.dma_start(out=W2b[:, 2 * c:2 * c + 2, :], in_=w2bv[:, 2 * c:2 * c + 2, :]),
              reads=[K.t_w2b[layer][c]], writes=[t_w2[c]])

    xin_v = xin.rearrange("(n s p) d -> n p s d", s=2, p=128)
    xout_v = xout.rearrange("(n s p) d -> n p s d", s=2, p=128)

    def load_x(n):
        b = n % 2
        S.dma("sp", lambda e: e.dma_start(out=xt[b], in_=xin_v[n]), writes=[t_xt[b]])

    load_x(0)
    load_x(1)
    def do_tile(n):
        b = n % 2
        x_ = xt[b]
        emit_rmsnorm_stats(K, x_, 2, ss, rstd, t_xt[b], t_ss, junk)
        for s in range(2):
            S.op("dve", lambda e, s=s: e.scalar_tensor_tensor(out=hb[:, s, :], in0=x_[:, s, :], scalar=rstd[:, s:s + 1], in1=gt,
                                                               op0=ALU.mult, op1=ALU.mult), reads=[t_xt[b], t_ss, t_g], writes=[t_hb])
        for half in range(2):
            for dcl in range(4):
                dc = half * 4 + dcl
                for s in range(2):
                    S.op("pe", lambda e, half=half, dcl=dcl, dc=dc, s=s: e.transpose(
                        psT[half][:, dcl, s * 128:(s + 1) * 128], hb[:, s, dc * 128:(dc + 1) * 128], K.identb),
                        reads=[t_hb, K.t_const], writes=[t_psT[half]])
            eng = "act" if half == 0 else "dve"
            if eng == "act":
                S.op("act", lambda e, half=half: e.activation(out=hT[b][:, half * 4:half * 4 + 4, :], in_=psT[half], func=AF.Copy),
                     reads=[t_psT[half]], writes=[t_hT[b]])
            else:
                S.op("dve", lambda e, half=half: e.tensor_copy(out=hT[b][:, half * 4:half * 4 + 4, :], in_=psT[half]),
                     reads=[t_psT[half]], writes=[t_hT[b]])
        for fc in range(32):
            pb = fc % 2
            for dc in range(8):
                S.op("pe", lambda e, fc=fc, dc=dc, pb=pb: e.matmul(psA[pb], W1b[:, dc, fc * 128:(fc + 1) * 128], hT[b][:, dc, :],
                                                                  start=(dc == 0), stop=(dc == 7)),
                     reads=[t_w1[fc // 2], t_hT[b]], writes=[t_psA[pb]])
            S.op("act", lambda e, pb=pb: e.activation(out=rt[pb], in_=psA[pb], func=AF.Relu), reads=[t_psA[pb]], writes=[t_rt[pb]])
            S.op("dve", lambda e, fc=fc, pb=pb: e.tensor_tensor(out=a2T[:, fc, :], in0=rt[pb], in1=rt[pb], op=ALU.mult),
                 reads=[t_rt[pb]], writes=[t_a2[fc]])
        k = 0
        for s in range(2):
            for dh in range(2):
                pb = k % 2; k += 1
                for fc in range(32):
                    S.op("pe", lambda e, fc=fc, s=s, dh=dh, pb=pb: e.matmul(psB[pb], a2T[:, fc, s * 128:(s + 1) * 128],
                                                                          W2b[:, fc, dh * 512:(dh + 1) * 512], start=(fc == 0), stop=(fc == 31)),
                         reads=[t_a2[fc], t_w2[fc // 2]], writes=[t_psB[pb]])
                S.op("dve", lambda e, s=s, dh=dh, pb=pb: e.tensor_tensor(out=x_[:, s, dh * 512:(dh + 1) * 512], in0=psB[pb],
                                                                       in1=x_[:, s, dh * 512:(dh + 1) * 512], op=ALU.add),
                     reads=[t_psB[pb]], writes=[t_xt[b]])
        if final_gain is not None:
            emit_rmsnorm_stats(K, x_, 2, ss[:, 2:4], rstd[:, 2:4], t_xt[b], t_ss, junk)
            for s in range(2):
                S.op("dve", lambda e, s=s: e.scalar_tensor_tensor(out=x_[:, s, :], in0=x_[:, s, :], scalar=rstd[:, 2 + s:3 + s], in1=gft,
                                                                   op0=ALU.mult, op1=ALU.mult), reads=[t_ss, t_g], writes=[t_xt[b]])
        S.dma("sp", lambda e, n=n: e.dma_start(out=xout_v[n], in_=x_), reads=[t_xt[b]], is_out=K.is_out(xout))
        if n + 2 < NT:
            load_x(n + 2)

    for n in range(NT):
        do_tile(n)
    S.barrier()
    A.release(m0)


def phase_mix0(K, xin, xout, P):
    S, A, ps = K.S, K.A, K.ps
    m_phase = A.mark()
    PS = lambda b: ps[:, b * 512:(b + 1) * 512]
    PSB = lambda b: psum_bf16(ps[:, b * 512:(b + 1) * 512])
    t_bank = [T() for _ in range(8)]
    qT = A.alloc([128, 4, S_LEN], BF16)
    kT = A.alloc([128, 2, S_LEN], BF16)
    Vaug = A.alloc([128, 32, 2, 128], BF16)
    t_qT, t_kT, t_V = T(), T(), T()
    m_w = A.mark()
    Wq = A.alloc([128, 8, 512], BF16)
    Wkv = A.alloc([128, 8, 256], BF16)
    WY = [A.alloc([128, 8, 512], BF16) for _ in range(2)]
    gqk = A.alloc([128, 2], F32)
    cm = {}
    for nm in ("rmat", "onesblk", "bdc", "bds", "bdsn", "c64bd", "s64bdn"):
        cm[nm] = A.alloc([128, 128], BF16)
    t_w = T()
    for nm in cm:
        S.dma("sp", lambda e, nm=nm: e.dma_start(out=cm[nm], in_=P[nm]), writes=[t_w])
    for col, src in ((0, P["q_norm"]), (1, P["k_norm"])):
        for hh in range(2):
            S.dma("sp", lambda e, col=col, src=src, hh=hh: e.dma_start(out=gqk[hh * 64:(hh + 1) * 64, col:col + 1], in_=src.unsqueeze(1)),
                  writes=[t_w])
    S.op("pool", lambda e: e.memset(kT[64:128, 0, :], 0.0), writes=[t_kT])
    S.op("pool", lambda e: e.memset(kT[0:64, 1, :], 0.0), writes=[t_kT])
    S.op("pool", lambda e: e.memset(Vaug[:, :, 0, 64:128], 1.0), writes=[t_V])
    S.op("pool", lambda e: e.memset(Vaug[:, :, 1, 0:64], 1.0), writes=[t_V])
    m_prep = A.mark()

    stg = A.alloc([128, 8, 512], F32)
    WAb = A.alloc([128, 8, 512], BF16)
    WAT = A.alloc([128, 4, D], BF16)
    Wf2 = A.alloc([128, 4, 128], F32)
    Wf2b = A.alloc([128, 4, 128], BF16)
    BDG = [A.alloc([128, 4, 128], BF16) for _ in range(2)]
    t_stg, t_wab, t_wat, t_wf, t_bdg = T(), T(), T(), T(), T()
    w_in = P["w_in"]
    for c in range(4):
        for hh in range(2):
            col = 512 + 64 * (c + 4 * hh)
            S.dma("sp", lambda e, c=c, hh=hh, col=col: e.dma_start(out=stg[:, :, c * 128 + hh * 64:c * 128 + hh * 64 + 64],
                                                                   in_=w_in[:, col:col + 64].rearrange("(dc p) d -> p dc d", p=128)),
                  writes=[t_stg])
    S.op("dve", lambda e: e.tensor_copy(out=Wq, in_=stg), reads=[t_stg], writes=[t_w])
    S.dma("sp", lambda e: e.dma_start(out=stg[:, :, 0:256], in_=w_in[:, 1024:1280].rearrange("(dc p) f -> p dc f", p=128)),
          writes=[t_stg])
    S.op("dve", lambda e: e.tensor_copy(out=Wkv, in_=stg[:, :, 0:256]), reads=[t_stg], writes=[t_w])
    S.dma("sp", lambda e: e.dma_start(out=stg, in_=w_in[:, 0:512].rearrange("(dc p) f -> p dc f", p=128)), writes=[t_stg])
    S.op("dve", lambda e: e.tensor_copy(out=WAb, in_=stg), reads=[t_stg], writes=[t_wab])
    for cc in range(4):
        bk = cc % 2
        for dc in range(8):
            S.op("pe", lambda e, cc=cc, dc=dc, bk=bk: e.transpose(PSB(bk)[:, dc * 128:(dc + 1) * 128], WAb[:, dc, cc * 128:(cc + 1) * 128], K.identb),
                 reads=[t_wab, K.t_const], writes=[t_bank[bk]])
        S.op("act", lambda e, cc=cc, bk=bk: e.activation(out=WAT[:, cc, :], in_=PSB(bk), func=AF.Copy), reads=[t_bank[bk]], writes=[t_wat])
    wf = P["w_fnet"].rearrange("(ch hh) c d -> hh c ch d", hh=2)
    S.op("pool", lambda e: e.memset(Wf2, 0.0), writes=[t_wf])
    for hh in range(2):
        S.dma("sp", lambda e, hh=hh: e.dma_start(out=Wf2[hh * 64:(hh + 1) * 64, :, hh * 64:(hh + 1) * 64], in_=wf[hh]), writes=[t_wf])
    S.op("pool", lambda e: e.tensor_copy(out=Wf2b, in_=Wf2), reads=[t_wf], writes=[t_wf])
    for gi, nm in enumerate(("c64bd", "s64bdn")):
        bk = 2 + gi
        for ch in range(4):
            S.op("pe", lambda e, ch=ch, bk=bk, nm=nm: e.matmul(PS(bk)[:, ch * 128:(ch + 1) * 128], cm[nm], Wf2b[:, ch, :], start=True, stop=True),
                 reads=[t_w, t_wf], writes=[t_bank[bk]])
        S.op("dve", lambda e, gi=gi, bk=bk: e.tensor_copy(out=BDG[gi].rearrange("p a b -> p (a b)"), in_=PS(bk)), reads=[t_bank[bk]], writes=[t_bdg])
    k = 0
    for gi in range(2):
        for dc in range(8):
            bk = 4 + (k % 2); k += 1
            for ch in range(4):
                S.op("pe", lambda e, gi=gi, dc=dc, ch=ch, bk=bk: e.matmul(PS(bk)[:, ch * 128:(ch + 1) * 128], WAT[:, ch, dc * 128:(dc + 1) * 128],
                                                                       BDG[gi][:, ch, :], start=True, stop=True),
                     reads=[t_wat, t_bdg], writes=[t_bank[bk]])
            eng = "act" if k % 2 else "dve"
            if eng == "act":
                S.op("act", lambda e, gi=gi, dc=dc, bk=bk: e.activation(out=WY[gi][:, dc, :], in_=PS(bk), func=AF.Copy), reads=[t_bank[bk]], writes=[t_w])
            else:
                S.op("dve", lambda e, gi=gi, dc=dc, bk=bk: e.tensor_copy(out=WY[gi][:, dc, :], in_=PS(bk)), reads=[t_bank[bk]], writes=[t_w])
    S.barrier()
    A.release(m_prep)

    xt = [A.alloc([128, 2, D], F32) for _ in range(2)]
    hb = A.alloc([128, 2, D], BF16)
    hT = [A.alloc([128, 8, 512], BF16) for _ in range(2)]
    gt = A.alloc([128, D], F32)
    junk = A.alloc([128, D], BF16)
    ss = A.alloc([128, 4], F32); rstd = A.alloc([128, 4], F32)
    cst = A.alloc([128, 2, 512], F32)
    sq = [A.alloc([128, 512], BF16) for _ in range(2)]
    rq = [A.alloc([128, 512], F32) for _ in range(2)]
    qn = [A.alloc([128, 512], F32) for _ in range(2)]
    qnb = [A.alloc([128, 512], BF16) for _ in range(2)]
    Ys = [[A.alloc([128, 512], BF16) for _ in range(2)] for _ in range(2)]
    PQs = [A.alloc([128, 2, 512], BF16) for _ in range(2)]
    t_xt, t_hb, t_hT, t_g, t_ss, t_cs = [T(), T()], T(), [T(), T()], T(), T(), T()
    t_sq, t_rq, t_qn, t_qnb = [T(), T()], [T(), T()], [T(), T()], [T(), T()]
    t_Ys, t_PQs, t_pqh = [[T(), T()], [T(), T()]], [T(), T()], T()
    load_bcast_row(K, "sp", gt, P["norm_mix0"], t_g)
    xv = xin.rearrange("(a j b4) d -> j b4 a d", j=32, b4=4)
    xov = xout.rearrange("(a j b4) d -> j b4 a d", j=32, b4=4)
    pqh = P["pqh"]
    pqh_w = pqh.rearrange("(j p) q f -> j p q f", p=128)

    def load_x(h):
        b = h % 2
        for s in range(2):
            for b4 in range(4):
                S.dma("sp", lambda e, s=s, b4=b4: e.dma_start(out=xt[b][b4 * 32:(b4 + 1) * 32, s, :], in_=xv[2 * h + s, b4]), writes=[t_xt[b]])

    def norm_T(h):
        b = h % 2
        st, off = h // 2, (h % 2) * 256
        emit_rmsnorm_stats(K, xt[b], 2, ss, rstd, t_xt[b], t_ss, junk)
        for s in range(2):
            S.op("dve", lambda e, s=s: e.scalar_tensor_tensor(out=hb[:, s, :], in0=xt[b][:, s, :], scalar=rstd[:, s:s + 1], in1=gt,
                                                               op0=ALU.mult, op1=ALU.mult), reads=[t_xt[b], t_ss, t_g], writes=[t_hb])
        hTs = hT[st % 2]
        for half in range(2):
            pv = PSB(half).rearrange("p (a b) -> p a b", a=4)
            for dcl in range(4):
                dc = half * 4 + dcl
                for s in range(2):
                    S.op("pe", lambda e, pv=pv, dcl=dcl, dc=dc, s=s: e.transpose(pv[:, dcl, s * 128:(s + 1) * 128], hb[:, s, dc * 128:(dc + 1) * 128], K.identb),
                         reads=[t_hb, K.t_const], writes=[t_bank[half]])
            if half == 0:
                S.op("act", lambda e, pv=pv: e.activation(out=hTs[:, 0:4, off:off + 256], in_=pv, func=AF.Copy), reads=[t_bank[0]], writes=[t_hT[st % 2]])
            else:
                S.op("dve", lambda e, pv=pv: e.tensor_copy(out=hTs[:, 4:8, off:off + 256], in_=pv), reads=[t_bank[1]], writes=[t_hT[st % 2]])

    def supertile(st):
        hTs, thT = hT[st % 2], t_hT[st % 2]
        n0 = st * 512
        S.dma("sp", lambda e: e.dma_start(out=cst[:, 0, :], in_=P["cosp"][:, n0:n0 + 512]), writes=[t_cs])
        S.dma("sp", lambda e: e.dma_start(out=cst[:, 1, :], in_=P["sinp"][:, n0:n0 + 512]), writes=[t_cs])

        def step1(i):
            bk = 2 + i % 2
            for dc in range(8):
                lhsT = Wq[:, dc, i * 128:(i + 1) * 128] if i < 4 else Wkv[:, dc, 0:128]
                S.op("pe", lambda e, lhsT=lhsT, dc=dc, bk=bk: e.matmul(PS(bk), lhsT, hTs[:, dc, :], start=(dc == 0), stop=(dc == 7)),
                     reads=[t_w, thT], writes=[t_bank[bk]])
            S.op("act", lambda e, bk=bk, i=i: e.activation(out=sq[i % 2], in_=PS(bk), func=AF.Square), reads=[t_bank[bk]], writes=[t_sq[i % 2]])

        def step2(i):
            bk = 2 + i % 2
            j = i % 2
            S.op("pe", lambda e: e.matmul(PS(4), cm["onesblk"], sq[j], start=True, stop=True), reads=[t_sq[j], t_w], writes=[t_bank[4]])
            S.op("act", lambda e: e.activation(out=rq[j], in_=PS(4), func=AF.Sqrt, bias=K.eps_ap, scale=1.0), reads=[t_bank[4], K.t_const], writes=[t_rq[j]])
            S.op("dve", lambda e: e.reciprocal(out=rq[j], in_=rq[j]), reads=[t_rq[j]], writes=[t_rq[j]])
            gcol = gqk[:, 0:1] if i < 4 else gqk[:, 1:2]
            S.op("dve", lambda e: e.scalar_tensor_tensor(out=qn[j], in0=PS(bk), scalar=gcol, in1=rq[j], op0=ALU.mult, op1=ALU.mult),
                 reads=[t_bank[bk], t_rq[j], t_w], writes=[t_qn[j]])
            S.op("pool", lambda e: e.tensor_copy(out=qnb[j], in_=qn[j]), reads=[t_qn[j]], writes=[t_qnb[j]])
            S.op("dve", lambda e: e.tensor_tensor(out=qn[j], in0=qn[j], in1=cst[:, 0, :], op=ALU.mult), reads=[t_cs, t_qnb[j]], writes=[t_qn[j]])

        def step3(i):
            j = i % 2
            S.op("pe", lambda e: e.matmul(PS(5), cm["rmat"], qnb[j], start=True, stop=True), reads=[t_qnb[j], t_w], writes=[t_bank[5]])
            S.op("dve", lambda e: e.tensor_tensor(out=rq[j], in0=PS(5), in1=cst[:, 1, :], op=ALU.mult), reads=[t_bank[5], t_cs], writes=[t_rq[j]])
            if i < 4:
                S.op("pool", lambda e: e.tensor_tensor(out=qT[:, i, n0:n0 + 512], in0=qn[j], in1=rq[j], op=ALU.add), reads=[t_qn[j], t_rq[j]], writes=[t_qT])
            else:
                for kv in range(2):
                    S.op("pool", lambda e, kv=kv: e.tensor_tensor(out=kT[kv * 64:(kv + 1) * 64, kv, n0:n0 + 512], in0=qn[j][kv * 64:(kv + 1) * 64, :],
                                                                  in1=rq[j][kv * 64:(kv + 1) * 64, :], op=ALU.add), reads=[t_qn[j], t_rq[j]], writes=[t_kT])

        for kk in range(7):
            if kk < 5:
                step1(kk)
            if 0 <= kk - 1 < 5:
                step2(kk - 1)
            if 0 <= kk - 2 < 5:
                step3(kk - 2)
        for tl in range(4):
            jt = st * 4 + tl
            for dc in range(8):
                S.op("pe", lambda e, dc=dc, tl=tl: e.matmul(PS(6)[:, 0:128], hTs[:, dc, tl * 128:(tl + 1) * 128], Wkv[:, dc, 128:256],
                                                           start=(dc == 0), stop=(dc == 7)), reads=[t_w, thT], writes=[t_bank[6]])
            S.op("act", lambda e, jt=jt: e.activation(out=Vaug[:, jt, 0, 0:64], in_=PS(6)[:, 0:64], func=AF.Copy), reads=[t_bank[6]], writes=[t_V])
            S.op("act", lambda e, jt=jt: e.activation(out=Vaug[:, jt, 1, 64:128], in_=PS(6)[:, 64:128], func=AF.Copy), reads=[t_bank[6]], writes=[t_V])
        for tl in range(4):
            jt = st * 4 + tl
            r = jt % 2
            for gi in range(2):
                bk = 6 + gi
                for dc in range(8):
                    S.op("pe", lambda e, gi=gi, dc=dc, tl=tl, bk=bk: e.matmul(PS(bk), hTs[:, dc, tl * 128:(tl + 1) * 128], WY[gi][:, dc, :],
                                                                          start=(dc == 0), stop=(dc == 7)), reads=[t_w, thT], writes=[t_bank[bk]])
                if gi == 0:
                    S.op("act", lambda e, r=r, bk=bk: e.activation(out=Ys[0][r], in_=PS(bk), func=AF.Copy), reads=[t_bank[bk]], writes=[t_Ys[0][r]])
                else:
                    S.op("dve", lambda e, r=r, bk=bk: e.tensor_copy(out=Ys[1][r], in_=PS(bk)), reads=[t_bank[bk]], writes=[t_Ys[1][r]])
            S.op("pe", lambda e, r=r: e.matmul(PS(4), cm["bdc"], Ys[0][r], start=True, stop=False), reads=[t_w, t_Ys[0][r]], writes=[t_bank[4]])
            S.op("pe", lambda e, r=r: e.matmul(PS(4), cm["bds"], Ys[1][r], start=False, stop=True), reads=[t_w, t_Ys[1][r]], writes=[t_bank[4]])
            S.op("pe", lambda e, r=r: e.matmul(PS(5), cm["bdc"], Ys[1][r], start=True, stop=False), reads=[t_w, t_Ys[1][r]], writes=[t_bank[5]])
            S.op("pe", lambda e, r=r: e.matmul(PS(5), cm["bdsn"], Ys[0][r], start=False, stop=True), reads=[t_w, t_Ys[0][r]], writes=[t_bank[5]])
            S.op("act", lambda e, r=r: e.activation(out=PQs[r][:, 0, :], in_=PS(4), func=AF.Copy), reads=[t_bank[4]], writes=[t_PQs[r]])
            S.op("dve", lambda e, r=r: e.tensor_copy(out=PQs[r][:, 1, :], in_=PS(5)), reads=[t_bank[5]], writes=[t_PQs[r]])
            S.dma("sp", lambda e, r=r, jt=jt: e.dma_start(out=pqh_w[jt], in_=PQs[r]), reads=[t_PQs[r]], writes=[t_pqh])

    load_x(0)
    load_x(1)
    nh = [0]

    def prep_half():
        h = nh[0]; nh[0] += 1
        if h < 16:
            norm_T(h)
            if h + 2 < 16:
                load_x(h + 2)

    prep_half(); prep_half()
    for st in range(8):
        prep_half(); prep_half()
        supertile(st)
    S.barrier()
    A.release(m_w)

    faT = A.alloc([128, 4, S_LEN], BF16)
    attT = A.alloc([128, 4, S_LEN], BF16)
    Wo = A.alloc([128, 8, D], BF16)
    t_fa, t_att, t_wo = T(), T(), T()
    m_C = A.mark()
    stg2 = A.alloc([128, 4, D], F32)
    t_stg2 = T()
    w_out = P["w_out"]
    S.dma("sp", lambda e: e.dma_start(out=stg2, in_=w_out[0:512, :].rearrange("(fc p) n -> p fc n", p=128)), writes=[t_stg2])
    S.op("dve", lambda e: e.tensor_copy(out=Wo[:, 0:4, :], in_=stg2), reads=[t_stg2], writes=[t_wo])
    wo_att = w_out[512:1024, :].rearrange("(hh c d) n -> hh d c n", hh=2, c=4)
    for hh in range(2):
        S.dma("sp", lambda e, hh=hh: e.dma_start(out=stg2[hh * 64:(hh + 1) * 64, :, :], in_=wo_att[hh]), writes=[t_stg2])
    S.op("dve", lambda e: e.tensor_copy(out=Wo[:, 4:8, :], in_=stg2), reads=[t_stg2], writes=[t_wo])
    CB = A.alloc([128, S_LEN], BF16); SB = A.alloc([128, S_LEN], BF16)
    Pp = [A.alloc([128, 4, 2, 512], BF16) for _ in range(2)]
    t_cb, t_Pp = T(), [T(), T()]
    S.dma("sp", lambda e: e.dma_start(out=CB, in_=P["cbt"]), writes=[t_cb])
    S.dma("sp", lambda e: e.dma_start(out=SB, in_=P["sbt"]), writes=[t_cb])
    pqh_r = pqh.rearrange("(b tau) q f -> b tau q f", tau=32)
    k = 0
    for tg in range(8):
        r = tg % 2
        S.dma("sp", lambda e, tg=tg, r=r: e.dma_start(out=Pp[r], in_=pqh_r[:, 4 * tg:4 * tg + 4]), reads=[t_pqh], writes=[t_Pp[r]])
        for fc in range(4):
            bk = k % 2; k += 1
            for tl in range(4):
                tau = 4 * tg + tl
                S.op("pe", lambda e, r=r, fc=fc, tl=tl, tau=tau, bk=bk: e.matmul(PS(bk)[:, tl * 128:(tl + 1) * 128], Pp[r][:, tl, 0, fc * 128:(fc + 1) * 128],
                                                                             CB[:, tau::32], start=True, stop=False), reads=[t_Pp[r], t_cb], writes=[t_bank[bk]])
                S.op("pe", lambda e, r=r, fc=fc, tl=tl, tau=tau, bk=bk: e.matmul(PS(bk)[:, tl * 128:(tl + 1) * 128], Pp[r][:, tl, 1, fc * 128:(fc + 1) * 128],
                                                                             SB[:, tau::32], start=False, stop=True), reads=[t_Pp[r], t_cb], writes=[t_bank[bk]])
            src = PS(bk).rearrange("p (tl mq mr) -> p tl mr mq", tl=4, mq=32, mr=4)
            dst = faT[:, fc, :].rearrange("p (mr tau mq) -> p tau mr mq", mr=4, tau=32, mq=32)[:, 4 * tg:4 * tg + 4]
            if k % 2:
                S.op("act", lambda e, src=src, dst=dst: e.activation(out=dst, in_=src, func=AF.Copy), reads=[t_bank[bk]], writes=[t_fa])
            else:
                S.op("dve", lambda e, src=src, dst=dst: e.tensor_copy(out=dst, in_=src), reads=[t_bank[bk]], writes=[t_fa])
    S.barrier()
    A.release(m_C)

    pT = [A.alloc([128, 1024], BF16) for _ in range(3)]
    rec = [A.alloc([128, 512], F32) for _ in range(2)]
    t_pT, t_rec = [T(), T(), T()], [T(), T()]
    t_sp = [T(), T()]
    caster = Caster(K, K.cast_jobs)
    PSP = lambda a: ps[:, a * 1024:(a + 1) * 1024]
    t_sp3 = [T(), T(), T()]
    steps = [(c, hh, qt, kp) for c in range(4) for hh in range(2) for qt in range(8) for kp in range(16)]

    def score(i):
        c, hh, qt, kp = steps[i]
        a = i % 3
        for u in range(2):
            kt = 2 * kp + u
            S.op("pe", lambda e, kt=kt, u=u: e.matmul(PSP(a)[:, u * 512:(u + 1) * 512], kT[:, hh, kt * 128:(kt + 1) * 128], qT[:, c, qt * 512:(qt + 1) * 512],
                                                     start=True, stop=True), reads=[t_kT, t_qT], writes=[t_sp3[a]])

    score(0)
    score(1)
    for i in range(len(steps)):
        c, hh, qt, kp = steps[i]
        it = i // 16
        ob = 6 + it % 2
        a = i % 3
        pb = i % 3
        lo, hi = hh * 64, hh * 64 + 64
        if i + 2 < len(steps):
            score(i + 2)
        S.op("act", lambda e, a=a, pb=pb: e.activation(out=pT[pb], in_=PSP(a), func=AF.Exp, scale=0.125), reads=[t_sp3[a]], writes=[t_pT[pb]])
        for u in range(2):
            kt = 2 * kp + u
            S.op("pe", lambda e, kt=kt, u=u, pb=pb, hh=hh, ob=ob: e.matmul(PS(ob), Vaug[:, kt, hh, :], pT[pb][:, u * 512:(u + 1) * 512], start=(kt == 0), stop=(kt == 31)),
                 reads=[t_V, t_pT[pb]], writes=[t_bank[ob]])
        if kp == 15:
            rb = it % 2
            dlo, dhi = (64, 128) if hh == 0 else (0, 64)
            S.op("dve", lambda e, ob=ob, rb=rb, lo=lo, hi=hi, dlo=dlo, dhi=dhi: e.reciprocal(out=rec[rb][lo:hi, :], in_=PS(ob)[dlo:dhi, :]),
                 reads=[t_bank[ob]], writes=[t_rec[rb]])
            S.op("dve", lambda e, ob=ob, rb=rb, lo=lo, hi=hi, c=c, qt=qt: e.tensor_tensor(out=attT[lo:hi, c, qt * 512:(qt + 1) * 512], in0=PS(ob)[lo:hi, :],
                                                                                        in1=rec[rb][lo:hi, :], op=ALU.mult),
                 reads=[t_bank[ob], t_rec[rb]], writes=[t_att])
            caster.step(1)
    caster.flush()
    S.barrier()
    A.release(m_C)

    xt2 = [A.alloc([128, D], F32) for _ in range(4)]
    t_xt2 = [T() for _ in range(4)]

    def load_x2(jt):
        b = jt % 4
        for b4 in range(4):
            S.dma("sp", lambda e, b4=b4: e.dma_start(out=xt2[b][b4 * 32:(b4 + 1) * 32, :], in_=xv[jt, b4]), writes=[t_xt2[b]])

    load_x2(0)
    load_x2(1)
    load_x2(2)
    k = 0
    for jt in range(32):
        b = jt % 4
        if jt + 3 < 32:
            load_x2(jt + 3)
        for dh in range(2):
            bk = k % 2; k += 1
            for fc in range(8):
                lhsT = faT[:, fc, jt * 128:(jt + 1) * 128] if fc < 4 else attT[:, fc - 4, jt * 128:(jt + 1) * 128]
                S.op("pe", lambda e, lhsT=lhsT, fc=fc, dh=dh, bk=bk: e.matmul(PS(bk), lhsT, Wo[:, fc, dh * 512:(dh + 1) * 512], start=(fc == 0), stop=(fc == 7)),
                     reads=[t_fa, t_att, t_wo], writes=[t_bank[bk]])
            S.op("dve", lambda e, b=b, dh=dh, bk=bk: e.tensor_tensor(out=xt2[b][:, dh * 512:(dh + 1) * 512], in0=PS(bk), in1=xt2[b][:, dh * 512:(dh + 1) * 512], op=ALU.add),
                 reads=[t_bank[bk]], writes=[t_xt2[b]])
        for b4 in range(4):
            S.dma("sp", lambda e, b=b, b4=b4, jt=jt: e.dma_start(out=xov[jt, b4], in_=xt2[b][b4 * 32:(b4 + 1) * 32, :]), reads=[t_xt2[b]],
                  is_out=K.is_out(xout))
    S.barrier()
    A.release(m_phase)


MAGIC = 12582912.0
TWO_PI_LO = 6.28318


def phase_mix1(K, xin, xout, P):
    S, A, ps = K.S, K.A, K.ps
    m_phase = A.mark()
    PS = lambda b: ps[:, b * 512:(b + 1) * 512]
    PSB = lambda b: psum_bf16(ps[:, b * 512:(b + 1) * 512])
    t_bank = [T() for _ in range(8)]
    Tm = A.alloc([128, 64, 128], BF16, top=True)
    Vre = A.alloc([128, 64, 128], BF16, top=True)
    Vimn = A.alloc([128, 64, 128], BF16, top=True)
    W1 = A.alloc([128, 64, 2, 128], BF16, top=True)
    r8 = A.alloc([128, 64], F32, top=True)
    y8 = A.alloc([128, 64], F32, top=True)
    iok = A.alloc([128, 512], F32, top=True)
    identf = A.alloc([128, 128], F32, top=True)
    t_U = [T() for _ in range(64)]
    t_prm = T()
    S.dma("sp", lambda e: e.dma_start(out=iok, in_=P["iotak"]), writes=[t_prm])
    S.dma("sp", lambda e: e.dma_start(out=identf, in_=P["identf"]), writes=[t_prm])
    m_prep = A.mark()

    CTr = A.alloc([128, 64, 16], F32); CTi = A.alloc([128, 64, 16], F32)
    Bbr = A.alloc([128, 64, 16], F32); Bbi = A.alloc([128, 64, 16], F32)
    S1r = A.alloc([128, 64, 8], F32); S1i = A.alloc([128, 64, 8], F32)
    S2r = A.alloc([128, 64, 8], F32); S2i = A.alloc([128, 64, 8], F32)
    S3r = A.alloc([128, 64, 8], F32); S3i = A.alloc([128, 64, 8], F32)
    Dm = A.alloc([128, 64], F32); sel = A.alloc([128, 128], F32); Dvec = A.alloc([128, 64], F32)
    mskf = A.alloc([128, 128], F32); mskb = A.alloc([128, 128], F32); dperm = A.alloc([128, 128], F32)
    m_p1 = A.mark()
    rows = A.alloc([128, 3, 128], F32)
    ldt = A.alloc([128, 2], F32)
    LR = A.alloc([128, 64], F32); LI = A.alloc([128, 64], F32); DT = A.alloc([128, 64], F32)
    BR = A.alloc([128, 64, 16], F32); BI = A.alloc([128, 64, 16], F32)
    Crow1 = A.alloc([128, 32, 128], F32)
    Crow = [Crow1, Crow1]
    t_rows, t_l, t_B, t_ct = T(), T(), T(), T()
    t_c1 = T(); t_crow = [t_c1, t_c1]
    v2 = lambda ap: ap.rearrange("dir (G2 g2) p -> (dir G2) (g2 p)", g2=2)
    S.dma("sp", lambda e: e.dma_start(out=rows[0:64, 0, :], in_=v2(P["lam_re"])), writes=[t_rows])
    S.dma("sp", lambda e: e.dma_start(out=rows[0:64, 1, :], in_=v2(P["lam_im"])), writes=[t_rows])
    S.dma("sp", lambda e: e.dma_start(out=ldt[0:64, :], in_=P["log_dt"].rearrange("dir (G2 g2) -> (dir G2) g2", g2=2)), writes=[t_rows])
    S.op("dve", lambda e: e.tensor_copy(out=rows[0:64, 2, :].rearrange("p (g2 q) -> p g2 q", g2=2),
                                        in_=ldt[0:64, :].unsqueeze(2).to_broadcast([64, 2, 64])), reads=[t_rows], writes=[t_rows])
    for i, dst in enumerate((LR, LI, DT)):
        S.op("pe", lambda e, i=i: e.transpose(PS(0)[:, i * 64:(i + 1) * 64], rows[0:64, i, :], identf[0:64, 0:64]), reads=[t_rows, t_prm], writes=[t_bank[0]])
    for i, dst in enumerate((LR, LI, DT)):
        S.op("dve", lambda e, i=i, dst=dst: e.tensor_copy(out=dst, in_=PS(0)[:, i * 64:(i + 1) * 64]), reads=[t_bank[0]], writes=[t_l])
    bsrc = lambda ap: ap.rearrange("dir (G2 g2) p c -> (g2 p) (dir G2) c", g2=2)
    S.dma("sp", lambda e: e.dma_start(out=BR, in_=bsrc(P["b_re"])), writes=[t_B])
    S.dma("sp", lambda e: e.dma_start(out=BI, in_=bsrc(P["b_im"])), writes=[t_B])
    for ri, (nm, dst) in enumerate((("c_re", CTr), ("c_im", CTi))):
        for d in range(2):
            cr = Crow[d]
            S.dma("sp", lambda e, nm=nm, d=d, cr=cr: e.dma_start(out=cr[0:16].rearrange("c G2 (g2 p) -> c G2 g2 p", g2=2),
                                                                in_=P[nm][d].rearrange("(G2 g2) c p -> c G2 g2 p", g2=2)), writes=[t_crow[d]])
            bk = 1 + d
            for G2 in range(32):
                S.op("pe", lambda e, cr=cr, G2=G2, bk=bk: e.transpose(PS(bk)[:, G2 * 16:(G2 + 1) * 16], cr[0:16, G2, :], identf[0:16, 0:16]),
                     reads=[t_crow[d], t_prm], writes=[t_bank[bk]])
            S.op("dve", lambda e, dst=dst, d=d, bk=bk: e.tensor_copy(out=dst[:, d * 32:(d + 1) * 32, :].rearrange("p a b -> p (a b)"), in_=PS(bk)),
                 reads=[t_bank[bk]], writes=[t_ct])
    t_D = T()
    S.dma("sp", lambda e: e.dma_start(out=Dm[0:16, :], in_=P["d_skip"].rearrange("(g c) -> c g", c=16), allow_slow_non_contiguous=True), writes=[t_D])
    S.dma("sp", lambda e: e.dma_start(out=sel[0:16, :], in_=P["sel16"]), writes=[t_D])
    S.dma("sp", lambda e: e.dma_start(out=mskf, in_=P["mskf"]), writes=[t_D])
    S.dma("sp", lambda e: e.dma_start(out=mskb, in_=P["mskb"]), writes=[t_D])
    S.dma("sp", lambda e: e.dma_start(out=dperm, in_=P["dperm"]), writes=[t_D])
    S.op("pe", lambda e: e.matmul(PS(3)[:, 0:64], sel[0:16, :], Dm[0:16, :], start=True, stop=True), reads=[t_D], writes=[t_bank[3]])
    S.op("dve", lambda e: e.tensor_copy(out=Dvec, in_=PS(3)[:, 0:64]), reads=[t_bank[3]], writes=[t_D])

    sm = lambda: A.alloc([128, 64], F32)
    dt_, ar, ai, yf, tmp, tmp2 = sm(), sm(), sm(), sm(), sm(), sm()
    PWr = A.alloc([128, 9, 64], F32); PWi = A.alloc([128, 9, 64], F32)
    NPr = A.alloc([128, 8, 64], F32); NPi = A.alloc([128, 8, 64], F32)
    cs_n = A.alloc([128, 9, 64], F32); sn_n = A.alloc([128, 9, 64], F32)
    t_s = T()
    V = lambda eng, fn: S.op(eng, fn, reads=[t_l, t_s], writes=[t_s])
    V("act", lambda e: e.activation(out=dt_, in_=DT, func=AF.Exp))
    V("dve", lambda e: e.tensor_tensor(out=ar, in0=LR, in1=dt_, op=ALU.mult))
    V("dve", lambda e: e.tensor_tensor(out=ai, in0=LI, in1=dt_, op=ALU.mult))
    V("dve", lambda e: e.tensor_scalar(out=yf, in0=ai, scalar1=1.0 / (2 * np.pi), scalar2=None, op0=ALU.mult))
    V("dve", lambda e: e.tensor_scalar(out=tmp, in0=yf, scalar1=MAGIC, scalar2=MAGIC, op0=ALU.add, op1=ALU.subtract))
    V("dve", lambda e: e.tensor_tensor(out=yf, in0=yf, in1=tmp, op=ALU.subtract))

    def sincos(yv, n, sdst, cdst):
        V("dve", lambda e: e.tensor_scalar(out=tmp, in0=yv, scalar1=float(n), scalar2=None, op0=ALU.mult))
        V("dve", lambda e: e.tensor_scalar(out=tmp2, in0=tmp, scalar1=MAGIC, scalar2=MAGIC, op0=ALU.add, op1=ALU.subtract))
        V("dve", lambda e: e.tensor_tensor(out=tmp2, in0=tmp, in1=tmp2, op=ALU.subtract))
        V("act", lambda e: e.activation(out=sdst, in_=tmp2, func=AF.Sin, scale=TWO_PI_LO))
        V("dve", lambda e: e.tensor_scalar(out=tmp, in0=tmp, scalar1=0.25, scalar2=None, op0=ALU.add))
        V("dve", lambda e: e.tensor_scalar(out=tmp2, in0=tmp, scalar1=MAGIC, scalar2=MAGIC, op0=ALU.add, op1=ALU.subtract))
        V("dve", lambda e: e.tensor_tensor(out=tmp2, in0=tmp, in1=tmp2, op=ALU.subtract))
        V("act", lambda e: e.activation(out=cdst, in_=tmp2, func=AF.Sin, scale=TWO_PI_LO))

    for n in range(9):
        sincos(yf, n, sn_n[:, n, :], cs_n[:, n, :])
        V("act", lambda e, n=n: e.activation(out=tmp, in_=ar, func=AF.Exp, scale=float(n)))
        V("dve", lambda e, n=n: e.tensor_tensor(out=PWr[:, n, :], in0=tmp, in1=cs_n[:, n, :], op=ALU.mult))
        V("dve", lambda e, n=n: e.tensor_tensor(out=PWi[:, n, :], in0=tmp, in1=sn_n[:, n, :], op=ALU.mult))
        if n < 8:
            V("act", lambda e, n=n: e.activation(out=tmp, in_=ar, func=AF.Exp, scale=-float(n)))
            V("dve", lambda e, n=n: e.tensor_tensor(out=NPr[:, n, :], in0=tmp, in1=cs_n[:, n, :], op=ALU.mult))
            V("dve", lambda e, n=n: e.scalar_tensor_tensor(out=NPi[:, n, :], in0=tmp, scalar=-1.0, in1=sn_n[:, n, :], op0=ALU.mult, op1=ALU.mult))
    S.op("act", lambda e: e.activation(out=r8, in_=ar, func=AF.Exp, scale=8.0), reads=[t_s], writes=[t_prm])
    V("dve", lambda e: e.tensor_scalar(out=tmp, in0=yf, scalar1=8.0, scalar2=None, op0=ALU.mult))
    V("dve", lambda e: e.tensor_scalar(out=tmp2, in0=tmp, scalar1=MAGIC, scalar2=MAGIC, op0=ALU.add, op1=ALU.subtract))
    S.op("dve", lambda e: e.tensor_tensor(out=y8, in0=tmp, in1=tmp2, op=ALU.subtract), reads=[t_s], writes=[t_prm])
    cre, cim, den, e1 = sm(), sm(), sm(), sm()
    V("dve", lambda e: e.tensor_scalar(out=e1, in0=PWr[:, 1, :], scalar1=-1.0, scalar2=None, op0=ALU.add))
    V("dve", lambda e: e.tensor_tensor(out=den, in0=LR, in1=LR, op=ALU.mult))
    V("dve", lambda e: e.tensor_tensor(out=tmp, in0=LI, in1=LI, op=ALU.mult))
    V("dve", lambda e: e.tensor_tensor(out=den, in0=den, in1=tmp, op=ALU.add))
    V("dve", lambda e: e.reciprocal(out=den, in_=den))
    V("dve", lambda e: e.tensor_tensor(out=cre, in0=e1, in1=LR, op=ALU.mult))
    V("dve", lambda e: e.tensor_tensor(out=tmp, in0=PWi[:, 1, :], in1=LI, op=ALU.mult))
    V("dve", lambda e: e.tensor_tensor(out=cre, in0=cre, in1=tmp, op=ALU.add))
    V("dve", lambda e: e.tensor_tensor(out=cre, in0=cre, in1=den, op=ALU.mult))
    V("dve", lambda e: e.tensor_tensor(out=cim, in0=PWi[:, 1, :], in1=LR, op=ALU.mult))
    V("dve", lambda e: e.tensor_tensor(out=tmp, in0=e1, in1=LI, op=ALU.mult))
    V("dve", lambda e: e.tensor_tensor(out=cim, in0=cim, in1=tmp, op=ALU.subtract))
    V("dve", lambda e: e.tensor_tensor(out=cim, in0=cim, in1=den, op=ALU.mult))
    tb = A.alloc([128, 64, 16], F32)
    bc = lambda ap: ap.unsqueeze(2).to_broadcast([128, 64, 16])
    VB = lambda eng, fn: S.op(eng, fn, reads=[t_s, t_B, t_ct], writes=[t_s])
    VB("dve", lambda e: e.tensor_tensor(out=Bbr, in0=BR, in1=bc(cre), op=ALU.mult))
    VB("dve", lambda e: e.tensor_tensor(out=tb, in0=BI, in1=bc(cim), op=ALU.mult))
    VB("dve", lambda e: e.tensor_tensor(out=Bbr, in0=Bbr, in1=tb, op=ALU.subtract))
    VB("dve", lambda e: e.tensor_tensor(out=Bbi, in0=BI, in1=bc(cre), op=ALU.mult))
    VB("dve", lambda e: e.tensor_tensor(out=tb, in0=BR, in1=bc(cim), op=ALU.mult))
    VB("dve", lambda e: e.tensor_tensor(out=Bbi, in0=Bbi, in1=tb, op=ALU.add))
    for i in range(8):
        for (dr, di, sr, si, nf, nb) in ((S1r, S1i, PWr, PWi, 7 - i, i), (S2r, S2i, PWr, PWi, i + 1, 8 - i), (S3r, S3i, NPr, NPi, 7 - i, i)):
            V("pool", lambda e, dr=dr, sr=sr, nf=nf, i=i: e.tensor_copy(out=dr[:, 0:32, i], in_=sr[:, nf, 0:32]))
            V("pool", lambda e, dr=dr, sr=sr, nb=nb, i=i: e.tensor_copy(out=dr[:, 32:64, i], in_=sr[:, nb, 32:64]))
            V("pool", lambda e, di=di, si=si, nf=nf, i=i: e.tensor_copy(out=di[:, 0:32, i], in_=si[:, nf, 0:32]))
            V("pool", lambda e, di=di, si=si, nb=nb, i=i: e.tensor_copy(out=di[:, 32:64, i], in_=si[:, nb, 32:64]))
    dump(K, "LR", LR, [t_l]); dump(K, "LI", LI, [t_l]); dump(K, "DT", DT, [t_l])
    dump(K, "PWr", PWr, [t_s]); dump(K, "PWi", PWi, [t_s]); dump(K, "NPr", NPr, [t_s]); dump(K, "NPi", NPi, [t_s])
    dump(K, "cre", cre, [t_s]); dump(K, "cim", cim, [t_s]); dump(K, "r8", r8, [t_prm]); dump(K, "y8", y8, [t_prm])
    dump(K, "Bbr", Bbr, [t_s]); dump(K, "Bbi", Bbi, [t_s]); dump(K, "CTr", CTr, [t_ct]); dump(K, "CTi", CTi, [t_ct])
    dump(K, "S1r", S1r, [t_s]); dump(K, "S2i", S2i, [t_s]); dump(K, "S3r", S3r, [t_s]); dump(K, "Dvec", Dvec, [t_D])
    S.barrier()
    A.release(m_p1)
    W1Tr = A.alloc([128, 64, 128], BF16); W1Ti = A.alloc([128, 64, 128], BF16)
    Vnr = A.alloc([128, 64, 128], BF16); Vnin = A.alloc([128, 64, 128], BF16)
    ta = A.alloc([128, 16, 128], F32); tb2 = A.alloc([128, 16, 128], F32)
    t_s = T()
    VB = lambda eng, fn: S.op(eng, fn, reads=[t_s], writes=[t_s])
    for d in range(4):
        cs = slice(d * 16, (d + 1) * 16)
        pw = lambda ap, cs=cs: ap[:, cs, :].unsqueeze(2).to_broadcast([128, 16, 16, 8])
        bb = lambda ap, cs=cs: ap[:, cs, :].unsqueeze(3).to_broadcast([128, 16, 16, 8])
        o4 = lambda ap: ap.rearrange("p a (c i) -> p a c i", c=16)
        VB("dve", lambda e, pw=pw, bb=bb: e.tensor_tensor(out=o4(ta), in0=pw(S1r), in1=bb(Bbr), op=ALU.mult))
        VB("dve", lambda e, pw=pw, bb=bb: e.tensor_tensor(out=o4(tb2), in0=pw(S1i), in1=bb(Bbi), op=ALU.mult))
        VB("dve", lambda e, cs=cs: e.tensor_tensor(out=W1Tr[:, cs, :], in0=ta, in1=tb2, op=ALU.subtract))
        VB("dve", lambda e, pw=pw, bb=bb: e.tensor_tensor(out=o4(ta), in0=pw(S1r), in1=bb(Bbi), op=ALU.mult))
        VB("dve", lambda e, pw=pw, bb=bb: e.tensor_tensor(out=o4(tb2), in0=pw(S1i), in1=bb(Bbr), op=ALU.mult))
        VB("dve", lambda e, cs=cs: e.tensor_tensor(out=W1Ti[:, cs, :], in0=ta, in1=tb2, op=ALU.add))
        pj = lambda ap, cs=cs: ap[:, cs, :].unsqueeze(3).to_broadcast([128, 16, 8, 16])
        cc = lambda ap, cs=cs: ap[:, cs, :].unsqueeze(2).to_broadcast([128, 16, 8, 16])
        o5 = lambda ap: ap.rearrange("p a (j c) -> p a j c", j=8)
        for (Sr, Si, dre, dimn) in ((S2r, S2i, Vre, Vimn), (S3r, S3i, Vnr, Vnin)):
            VB("dve", lambda e, pj=pj, cc=cc, Sr=Sr: e.tensor_tensor(out=o5(ta), in0=pj(Sr), in1=cc(CTr), op=ALU.mult))
            VB("dve", lambda e, pj=pj, cc=cc, Si=Si: e.tensor_tensor(out=o5(tb2), in0=pj(Si), in1=cc(CTi), op=ALU.mult))
            VB("dve", lambda e, cs=cs, dre=dre: e.tensor_tensor(out=dre[:, cs, :], in0=ta, in1=tb2, op=ALU.subtract))
            VB("dve", lambda e, pj=pj, cc=cc, Si=Si: e.tensor_tensor(out=o5(ta), in0=pj(Si), in1=cc(CTr), op=ALU.mult))
            VB("dve", lambda e, pj=pj, cc=cc, Sr=Sr: e.tensor_tensor(out=o5(tb2), in0=pj(Sr), in1=cc(CTi), op=ALU.mult))
            VB("dve", lambda e: e.tensor_tensor(out=ta, in0=ta, in1=tb2, op=ALU.add))
            VB("dve", lambda e, cs=cs, dimn=dimn: e.tensor_scalar(out=dimn[:, cs, :], in0=ta, scalar1=-1.0, scalar2=None, op0=ALU.mult))
    k = 0
    for ri, src in enumerate((W1Tr, W1Ti)):
        for c8 in range(8):
            bk = 4 + k % 2; k += 1
            for cl in range(8):
                col = c8 * 8 + cl
                S.op("pe", lambda e, src=src, col=col, cl=cl, bk=bk: e.transpose(PSB(bk)[:, cl * 128:(cl + 1) * 128], src[:, col, :], K.identb),
                     reads=[t_s, K.t_const], writes=[t_bank[bk]])
            S.op("act", lambda e, ri=ri, c8=c8, bk=bk: e.activation(out=W1[:, c8 * 8:(c8 + 1) * 8, ri, :], in_=PSB(bk).rearrange("p (a b) -> p a b", a=8), func=AF.Copy),
                 reads=[t_bank[bk]], writes=[t_prm])
    tq = [A.alloc([128, 128], F32) for _ in range(2)]
    t_tq = [T(), T()]
    for g in range(64):
        G2, g2 = g // 2, g % 2
        lo, hi = g2 * 64, g2 * 64 + 64
        bf_, bb_ = 6, 7
        for (bk, col) in ((bf_, G2), (bb_, 32 + G2)):
            S.op("pe", lambda e, bk=bk, col=col, lo=lo, hi=hi: e.matmul(PS(bk)[:, 0:128], W1Tr[lo:hi, col, :], Vnr[lo:hi, col, :], start=True, stop=False),
                 reads=[t_s], writes=[t_bank[bk]])
            S.op("pe", lambda e, bk=bk, col=col, lo=lo, hi=hi: e.matmul(PS(bk)[:, 0:128], W1Ti[lo:hi, col, :], Vnin[lo:hi, col, :], start=False, stop=True),
                 reads=[t_s], writes=[t_bank[bk]])
        j = g % 2
        S.op("dve", lambda e, j=j: e.tensor_tensor(out=tq[j], in0=PS(6)[:, 0:128], in1=mskf, op=ALU.mult), reads=[t_bank[6], t_D], writes=[t_tq[j]])
        S.op("dve", lambda e, j=j: e.tensor_tensor(out=PS(7)[:, 128:256], in0=PS(7)[:, 0:128], in1=mskb, op=ALU.mult), reads=[t_D], writes=[t_bank[7]])
        S.op("dve", lambda e, j=j: e.tensor_tensor(out=tq[j], in0=PS(7)[:, 128:256], in1=tq[j], op=ALU.add), reads=[t_bank[7]], writes=[t_tq[j]])
        S.op("dve", lambda e, j=j, g=g: e.scalar_tensor_tensor(out=Tm[:, g, :], in0=dperm, scalar=Dvec[:, g:g + 1], in1=tq[j], op0=ALU.mult, op1=ALU.add),
             reads=[t_tq[j], t_D, t_prm], writes=[t_prm])
    dump(K, "W1Tr", W1Tr, [t_s]); dump(K, "W1Ti", W1Ti, [t_s]); dump(K, "Vnr", Vnr, [t_s]); dump(K, "Vnin", Vnin, [t_s])
    dump(K, "Vre", Vre, [t_s]); dump(K, "Vimn", Vimn, [t_s]); dump(K, "Tm", Tm, [t_prm]); dump(K, "W1", W1, [t_prm])
    S.barrier()
    A.release(m_prep)

    U = A.alloc([128, 64, 512], BF16)
    m_U = A.mark()
    xt = A.alloc([128, 8, D], F32)
    ub = A.alloc([128, D, 8], BF16)
    gt = A.alloc([128, D], F32)
    junk = A.alloc([128, D], BF16)
    ss = A.alloc([128, 8], F32); rstd = A.alloc([128, 8], F32)
    t_xt, t_ub, t_g, t_ss = T(), T(), T(), T()
    load_bcast_row(K, "sp", gt, P["norm_mix1"], t_g)
    xk = xin.rearrange("(kb kp i) d -> kb kp i d", kp=128, i=8)
    for kb in range(4):
        S.dma("sp", lambda e, kb=kb: e.dma_start(out=xt, in_=xk[kb]), writes=[t_xt])
        emit_rmsnorm_stats(K, xt, 8, ss, rstd, t_xt, t_ss, junk)
        for i in range(8):
            S.op("dve", lambda e, i=i: e.scalar_tensor_tensor(out=ub[:, :, i], in0=xt[:, i, :], scalar=rstd[:, i:i + 1], in1=gt, op0=ALU.mult, op1=ALU.mult),
                 reads=[t_xt, t_ss, t_g], writes=[t_ub])
        ubf = ub.rearrange("p d i -> p (d i)")
        for g8 in range(8):
            bk = g8 % 2
            for gl in range(8):
                g = g8 * 8 + gl
                S.op("pe", lambda e, g=g, gl=gl, bk=bk: e.transpose(PSB(bk)[:, gl * 128:(gl + 1) * 128], ubf[:, g * 128:(g + 1) * 128], K.identb),
                     reads=[t_ub, K.t_const], writes=[t_bank[bk]])
            src = PSB(bk).rearrange("p (a b) -> p a b", a=8)
            dst = U[:, g8 * 8:(g8 + 1) * 8, kb * 128:(kb + 1) * 128]
            wr = [t_U[g8 * 8 + gl] for gl in range(8)]
            if g8 % 2 == 0:
                S.op("act", lambda e, src=src, dst=dst: e.activation(out=dst, in_=src, func=AF.Copy), reads=[t_bank[bk]], writes=wr)
            else:
                S.op("dve", lambda e, src=src, dst=dst: e.tensor_copy(out=dst, in_=src), reads=[t_bank[bk]], writes=wr)
    dump(K, "U", U, t_U)
    S.barrier()
    A.release(m_U)

    NB = 3
    mk = lambda: [A.alloc([128, 512], F32) for _ in range(NB)]
    ys_, fs_, sn_, cs_, xr_, xi_, wr_, wi_ = [mk() for _ in range(8)]
    t_t = [[T() for _ in range(8)] for _ in range(NB)]
    t_E = [T() for _ in range(4)]
    Es = [[A.alloc([128, 512], BF16) for _ in range(2)] for _ in range(4)]
    for sl in range(4):
        for ri in range(2):
            S.op("pool", lambda e, sl=sl, ri=ri: e.memset(Es[sl][ri], 0.0), writes=[t_E[sl]])
    SQ2 = float(np.sqrt(2.0))

    def level1(G2):
        for d in range(2):
            for ri in range(2):
                bk = d * 2 + ri
                for g2 in range(2):
                    g = 2 * G2 + g2
                    S.op("pe", lambda e, bk=bk, d=d, ri=ri, g2=g2, g=g: e.matmul(PS(bk)[g2 * 64:(g2 + 1) * 64, :], W1[:, d * 32 + G2, ri, g2 * 64:(g2 + 1) * 64],
                                                                             U[:, g, :], start=True, stop=True),
                         reads=[t_prm, t_U[g]], writes=[t_bank[bk]])

    def bufs(n):
        b = n % NB
        return [lst[b] for lst in (ys_, fs_, sn_, cs_, xr_, xi_, wr_, wi_)], t_t[b]

    def scan1a(G2, d):
        (ys, fs, sn, cs, xr, xi, wr, wi), tt = bufs(2 * G2 + d)
        col = d * 32 + G2
        rv = (lambda ap: ap[:, ::-1]) if d == 1 else (lambda ap: ap)
        S.op("act", lambda e: e.activation(out=xr, in_=rv(PS(d * 2 + 0)), func=AF.Copy), reads=[t_bank[d * 2]], writes=[tt[4]])
        S.op("act", lambda e: e.activation(out=xi, in_=rv(PS(d * 2 + 1)), func=AF.Copy), reads=[t_bank[d * 2 + 1]], writes=[tt[5]])
        S.op("act", lambda e: e.activation(out=ys, in_=iok, func=AF.Identity, scale=y8[:, col:col + 1]), reads=[t_prm], writes=[tt[0]])
        S.op("act", lambda e: e.activation(out=fs, in_=ys, func=AF.Identity, bias=K.magic_ap), reads=[tt[0], K.t_const], writes=[tt[1]])
        S.op("act", lambda e: e.activation(out=fs, in_=fs, func=AF.Identity, bias=K.nmagic_ap), reads=[tt[1], K.t_const], writes=[tt[1]])

    def scan1b(G2, d):
        (ys, fs, sn, cs, xr, xi, wr, wi), tt = bufs(2 * G2 + d)
        S.op("dve", lambda e: e.tensor_tensor(out=fs, in0=ys, in1=fs, op=ALU.subtract), reads=[tt[0], tt[1]], writes=[tt[1]])
        S.op("act", lambda e: e.activation(out=sn, in_=fs, func=AF.Sin, scale=TWO_PI_LO), reads=[tt[1]], writes=[tt[2]])
        S.op("act", lambda e: e.activation(out=ys, in_=fs, func=AF.Sin, scale=TWO_PI_LO / 2), reads=[tt[1]], writes=[tt[0]])
        S.op("act", lambda e: e.activation(out=cs, in_=ys, func=AF.Square, scale=SQ2), reads=[tt[0]], writes=[tt[3]])
        S.op("act", lambda e: e.activation(out=cs, in_=cs, func=AF.Identity, scale=-1.0, bias=K.one_ap), reads=[tt[3], K.t_const], writes=[tt[3]])

    def scan2a(G2, d):
        (ys, fs, sn, cs, xr, xi, wr, wi), tt = bufs(2 * G2 + d)
        S.op("dve", lambda e: e.tensor_tensor(out=ys, in0=xr, in1=cs, op=ALU.mult), reads=[tt[4], tt[3]], writes=[tt[0]])
        S.op("dve", lambda e: e.tensor_tensor(out=fs, in0=xi, in1=sn, op=ALU.mult), reads=[tt[5], tt[2]], writes=[tt[1]])
        S.op("pool", lambda e: e.tensor_tensor(out=wr, in0=ys, in1=fs, op=ALU.add), reads=[tt[0], tt[1]], writes=[tt[6]])
        S.op("dve", lambda e: e.tensor_tensor(out=ys, in0=xi, in1=cs, op=ALU.mult), reads=[tt[5], tt[3]], writes=[tt[0]])
        S.op("dve", lambda e: e.tensor_tensor(out=fs, in0=xr, in1=sn, op=ALU.mult), reads=[tt[4], tt[2]], writes=[tt[1]])
        S.op("pool", lambda e: e.tensor_tensor(out=wi, in0=ys, in1=fs, op=ALU.subtract), reads=[tt[0], tt[1]], writes=[tt[7]])

    def scan2b(G2, d):
        (ys, fs, sn, cs, xr, xi, wr, wi), tt = bufs(2 * G2 + d)
        col = d * 32 + G2
        rb = r8[:, col:col + 1].to_broadcast([128, 512])
        S.op("dve", lambda e: e.tensor_tensor_scan(out=xr, data0=rb, data1=wr, initial=0.0, op0=ALU.mult, op1=ALU.add), reads=[t_prm, tt[6]], writes=[tt[4]])
        S.op("dve", lambda e: e.tensor_tensor_scan(out=xi, data0=rb, data1=wi, initial=0.0, op0=ALU.mult, op1=ALU.add), reads=[t_prm, tt[7]], writes=[tt[5]])

    def edst(G2, d):
        sl = (G2 % 2) * 2 + d
        if d == 0:
            return sl, Es[sl][0][:, 1:512], Es[sl][1][:, 1:512]
        return sl, Es[sl][0][:, 0:511][:, ::-1], Es[sl][1][:, 0:511][:, ::-1]

    def scan3a(G2, d):
        (ys, fs, sn, cs, zr, zi, wr, wi), tt = bufs(2 * G2 + d)
        S.op("dve", lambda e: e.tensor_tensor(out=wr, in0=zr, in1=cs, op=ALU.mult), reads=[tt[4], tt[3]], writes=[tt[6]])
        S.op("dve", lambda e: e.tensor_tensor(out=wi, in0=zi, in1=sn, op=ALU.mult), reads=[tt[5], tt[2]], writes=[tt[7]])

    def scan3b(G2, d):
        (ys, fs, sn, cs, zr, zi, wr, wi), tt = bufs(2 * G2 + d)
        sl, dre, dim = edst(G2, d)
        S.op("pool", lambda e: e.tensor_tensor(out=ys, in0=zi, in1=cs, op=ALU.mult), reads=[tt[5], tt[3]], writes=[tt[0]])
        S.op("pool", lambda e: e.tensor_tensor(out=fs, in0=zr, in1=sn, op=ALU.mult), reads=[tt[4], tt[2]], writes=[tt[1]])
        S.op("pool", lambda e: e.tensor_tensor(out=dre, in0=wr[:, 0:511], in1=wi[:, 0:511], op=ALU.subtract), reads=[tt[6], tt[7]], writes=[t_E[sl]])
        S.op("pool", lambda e: e.tensor_tensor(out=dim, in0=ys[:, 0:511], in1=fs[:, 0:511], op=ALU.add), reads=[tt[0], tt[1]], writes=[t_E[sl]])

    def output(G2):
        for g2 in range(2):
            g = 2 * G2 + g2
            lo, hi = g2 * 64, g2 * 64 + 64
            bk = 4 + (g % 4)
            for kb in range(4):
                ks = slice(kb * 128, (kb + 1) * 128)
                o = PS(bk)[:, kb * 128:(kb + 1) * 128]
                S.op("pe", lambda e, o=o, g=g, ks=ks: e.matmul(o, U[:, g, ks], Tm[:, g, :], start=True, stop=False), reads=[t_U[g], t_prm], writes=[t_bank[bk]])
                n = 0
                for d in range(2):
                    sl = (G2 % 2) * 2 + d
                    col = d * 32 + G2
                    for ri, Vm in ((0, Vre), (1, Vimn)):
                        n += 1
                        S.op("pe", lambda e, o=o, sl=sl, ri=ri, Vm=Vm, col=col, ks=ks, lo=lo, hi=hi, n=n: e.matmul(o, Es[sl][ri][lo:hi, ks], Vm[lo:hi, col, :],
                                                                                                         start=False, stop=(n == 4)),
                             reads=[t_E[sl], t_prm], writes=[t_bank[bk]])
            if g % 2 == 0:
                S.op("act", lambda e, g=g, bk=bk: e.activation(out=U[:, g, :], in_=PS(bk), func=AF.Copy), reads=[t_bank[bk]], writes=[t_U[g]])
            else:
                S.op("dve", lambda e, g=g, bk=bk: e.tensor_copy(out=U[:, g, :], in_=PS(bk)), reads=[t_bank[bk]], writes=[t_U[g]])

    for t in range(64 + 2):
        if t < 64:
            if t % 2 == 0:
                level1(t // 2)
            scan1a(t // 2, t % 2)
        if 0 <= t - 1 < 64:
            scan2a((t - 1) // 2, (t - 1) % 2)
        if 0 <= t - 2 < 64:
            scan3a((t - 2) // 2, (t - 2) % 2)
        if t < 64:
            scan1b(t // 2, t % 2)
        if 0 <= t - 1 < 64:
            scan2b((t - 1) // 2, (t - 1) % 2)
        if 0 <= t - 2 < 64:
            scan3b((t - 2) // 2, (t - 2) % 2)
            if (t - 2) % 2 == 1:
                output((t - 2) // 2)
    dump(K, "Y", U, t_U)
    S.barrier()
    A.release(m_U)
    A.release_top()

    Wg = A.alloc([128, 8, D], BF16)
    stg = A.alloc([128, 4, D], F32)
    bgf = A.alloc([128, D], F32); bgb = A.alloc([128, D], BF16); onesr = A.alloc([128, 128], BF16)
    t_wg, t_stg = T(), T()
    wgv = P["w_gate"].rearrange("(dc p) n -> p dc n", p=128)
    for h in range(2):
        S.dma("sp", lambda e, h=h: e.dma_start(out=stg, in_=wgv[:, 4 * h:4 * h + 4, :]), writes=[t_stg])
        S.op("pool", lambda e, h=h: e.tensor_copy(out=Wg[:, 4 * h:4 * h + 4, :], in_=stg), reads=[t_stg], writes=[t_wg])
    S.dma("sp", lambda e: e.dma_start(out=bgf[0:1, :], in_=P["b_gate"].unsqueeze(0)), writes=[t_stg])
    S.op("pool", lambda e: e.tensor_copy(out=bgb[0:1, :], in_=bgf[0:1, :]), reads=[t_stg], writes=[t_wg])
    S.op("pool", lambda e: e.memset(onesr, 1.0), writes=[t_wg])
    NBG = 2
    sqb = [A.alloc([128, D], F32) for _ in range(NBG)]
    gel = [A.alloc([128, D], F32) for _ in range(NBG)]
    gb = [A.alloc([128, D], BF16) for _ in range(NBG)]
    gT = [A.alloc([128, D], BF16) for _ in range(NBG)]
    sg = [A.alloc([128, 512], F32) for _ in range(2)]
    xt3 = [A.alloc([128, D], F32) for _ in range(3)]
    t_y, t_sq, t_gel, t_gb, t_gT, t_sg, t_x3 = [T(), T()], [T(), T()], [T(), T()], [T(), T()], [T(), T()], [T(), T()], [T(), T(), T()]
    t_Uall = T()
    xs = xin.rearrange("(kb kp j) d -> kb j kp d", kp=128, j=8)
    xo = xout.rearrange("(kb kp j) d -> kb j kp d", kp=128, j=8)
    yall = U.rearrange("p g (kb j c) -> p kb j g c", kb=4, j=8)
    kk = [0]

    def load3(n):
        kb, j = n // 8, n % 8
        S.dma("sp", lambda e: e.dma_start(out=xt3[n % 3], in_=xs[kb, j]), writes=[t_x3[n % 3]])

    load3(0); load3(1)
    y3 = lambda ap: ap.rearrange("p (g c) -> p g c", c=16)

    def gate1(n):
        kb, j = n // 8, n % 8
        b = n % NBG
        yv = yall[:, kb, j]
        S.op("act", lambda e: e.activation(out=y3(sqb[b]), in_=yv, func=AF.Square), reads=t_U, writes=[t_sq[b]])
        S.op("dve", lambda e: e.tensor_scalar(out=sqb[b], in0=sqb[b], scalar1=0.044715, scalar2=1.0, op0=ALU.mult, op1=ALU.add), reads=[t_sq[b]], writes=[t_sq[b]])
        S.op("dve", lambda e: e.tensor_tensor(out=y3(sqb[b]), in0=y3(sqb[b]), in1=yv, op=ALU.mult), reads=t_U + [t_sq[b]], writes=[t_sq[b]])
        S.op("act", lambda e: e.activation(out=sqb[b], in_=sqb[b], func=AF.Sigmoid, scale=1.5957691216), reads=[t_sq[b]], writes=[t_sq[b]])
        S.op("dve", lambda e: e.tensor_tensor(out=y3(gel[b]), in0=y3(sqb[b]), in1=yv, op=ALU.mult), reads=t_U + [t_sq[b]], writes=[t_gel[b]])
        S.op("pool", lambda e: e.tensor_copy(out=gb[b], in_=gel[b]), reads=[t_gel[b]], writes=[t_gb[b]])

    def gate1b(n):
        b = n % NBG
        for dc in range(8):
            S.op("pe", lambda e, dc=dc: e.transpose(PSB(0)[:, dc * 128:(dc + 1) * 128], gb[b][:, dc * 128:(dc + 1) * 128], K.identb),
                 reads=[t_gb[b], K.t_const], writes=[t_bank[0]])
        S.op("act", lambda e: e.activation(out=gT[b], in_=PSB(0), func=AF.Copy), reads=[t_bank[0]], writes=[t_gT[b]])

    def gate2a(n):
        b = n % NBG
        if n + 2 < 32:
            load3(n + 2)
        for dh in range(2):
            bk = 1 + dh
            for dc in range(8):
                S.op("pe", lambda e, dc=dc, dh=dh, bk=bk: e.matmul(PS(bk), gT[b][:, dc * 128:(dc + 1) * 128], Wg[:, dc, dh * 512:(dh + 1) * 512], start=(dc == 0), stop=False),
                     reads=[t_gT[b], t_wg], writes=[t_bank[bk]])
            S.op("pe", lambda e, dh=dh, bk=bk: e.matmul(PS(bk), onesr[0:1, :], bgb[0:1, dh * 512:(dh + 1) * 512], start=False, stop=True), reads=[t_wg], writes=[t_bank[bk]])

    def gate2b(n):
        kb, j = n // 8, n % 8
        b = n % NBG
        xb = xt3[n % 3]
        for dh in range(2):
            bk = 1 + dh
            sb_ = dh
            S.op("act", lambda e, bk=bk, sb_=sb_: e.activation(out=sg[sb_], in_=PS(bk), func=AF.Sigmoid), reads=[t_bank[bk]], writes=[t_sg[sb_]])
            S.op("dve", lambda e, dh=dh, sb_=sb_: e.tensor_tensor(out=sg[sb_], in0=sg[sb_], in1=gel[b][:, dh * 512:(dh + 1) * 512], op=ALU.mult),
                 reads=[t_sg[sb_], t_gel[b]], writes=[t_sg[sb_]])
            S.op("pool", lambda e, dh=dh, sb_=sb_: e.tensor_tensor(out=xb[:, dh * 512:(dh + 1) * 512], in0=xb[:, dh * 512:(dh + 1) * 512], in1=sg[sb_], op=ALU.add),
                 reads=[t_sg[sb_]], writes=[t_x3[n % 3]])
        S.dma("sp", lambda e: e.dma_start(out=xo[kb, j], in_=xb), reads=[t_x3[n % 3]], is_out=K.is_out(xout))

    gate1(0)
    gate1b(0)
    for n in range(32):
        if n + 1 < 32:
            gate1(n + 1)
        gate2a(n)
        gate2b(n)
        if n + 1 < 32:
            gate1b(n + 1)
    S.barrier()
    A.release(m_phase)


def build_program(phases=("mix0", "mlp0", "mix1", "mlp1"), dbg=False):
    nc = bass.Bass("TRN2", target_bir_lowering=False)
    K = Ctx()
    K.dbg = dbg
    K.nc = nc
    din = lambda name, shape, dt=F32: nc.dram_tensor(name, list(shape), dt, kind="ExternalInput").ap()
    x = din("x", [S_LEN, D])
    norm_mix = din("norm_mix", [2, D]); norm_mlp = din("norm_mlp", [2, D])
    w1 = din("mlp_w1", [2, D, DFF]); w2 = din("mlp_w2", [2, DFF, D])
    final_norm = din("final_norm", [D])
    identb_d = din("identb", [128, 128], BF16)
    P = {}
    P["w_in"] = din("w_in", [1, D, 1280])[0]
    P["w_fnet"] = din("w_fnet", [1, 8, 64, 64])[0]
    P["q_norm"] = din("q_norm", [1, 64])[0]
    P["k_norm"] = din("k_norm", [1, 64])[0]
    P["w_out"] = din("w_out", [1, D, D])[0]
    for nm in ("rmat", "onesblk", "bdc", "bds", "bdsn", "c64bd", "s64bdn"):
        P[nm] = din(nm, [128, 128], BF16)
    P["cosp"] = din("cosp", [128, S_LEN]); P["sinp"] = din("sinp", [128, S_LEN])
    P["cbt"] = din("cbt", [128, S_LEN], BF16); P["sbt"] = din("sbt", [128, S_LEN], BF16)
    P["norm_mix0"] = norm_mix[0]
    P["norm_mix1"] = norm_mix[1]
    for nm, shp in (("lam_re", [1, 2, 64, 64]), ("lam_im", [1, 2, 64, 64]), ("log_dt", [1, 2, 64]), ("b_re", [1, 2, 64, 64, 16]), ("b_im", [1, 2, 64, 64, 16]),
                    ("c_re", [1, 2, 64, 16, 64]), ("c_im", [1, 2, 64, 16, 64]), ("d_skip", [1, D]), ("w_gate", [1, D, D]), ("b_gate", [1, D])):
        P[nm] = din(nm, shp)[0]
    P["iotak"] = din("iotak", [128, 512]); P["identf"] = din("identf", [128, 128])
    P["sel16"] = din("sel16", [16, 128]); P["mskf"] = din("mskf", [128, 128]); P["mskb"] = din("mskb", [128, 128]); P["dperm"] = din("dperm", [128, 128])
    P["pqh"] = nc.dram_tensor("pqh", [S_LEN, 2, 512], BF16).ap()
    out = nc.dram_tensor("out", [S_LEN, D], F32, kind="ExternalOutput").ap()
    scr = [nc.dram_tensor("scr%d" % i, [S_LEN, D], F32).ap() for i in range(3)]
    K.out = out
    K.w1 = [w1[0], w1[1]]; K.w2 = [w2[0], w2[1]]
    K.w1b = [nc.dram_tensor("w1b%d" % l, [D, DFF], BF16).ap() for l in range(2)]
    K.w2b = [nc.dram_tensor("w2b%d" % l, [DFF, D], BF16).ap() for l in range(2)]
    K.t_w1b = [[T() for _ in range(16)] for _ in range(2)]
    K.t_w2b = [[T() for _ in range(16)] for _ in range(2)]
    K.is_out = lambda ap: ap is out
    with ExitStack() as es:
        arena = es.enter_context(nc.sbuf_tensor("arena", [128, ARENA_BYTES + 512], U8))
        K.ps = es.enter_context(nc.psum_tensor("ps", [128, 4096], F32))
        K.S = Sched(nc, es)
        K.A = Arena(arena)
        K.identb = K.A.alloc([128, 128], BF16)
        K.t_const = T()
        K.S.dma("sp", lambda e: e.dma_start(out=K.identb, in_=identb_d), writes=[K.t_const])
        K.eps_ap = K.A.alloc([128, 1], F32)
        K.S.op("pool", lambda e: e.memset(K.eps_ap, EPS), writes=[K.t_const])
        K.one_ap = K.A.alloc([128, 1], F32)
        K.S.op("pool", lambda e: e.memset(K.one_ap, 1.0), writes=[K.t_const])
        K.magic_ap = K.A.alloc([128, 1], F32)
        K.S.op("pool", lambda e: e.memset(K.magic_ap, MAGIC), writes=[K.t_const])
        K.nmagic_ap = K.A.alloc([128, 1], F32)
        K.S.op("pool", lambda e: e.memset(K.nmagic_ap, -MAGIC), writes=[K.t_const])
        cur = x
        chain = {"mix0": scr[0], "mlp0": scr[1], "mix1": scr[2], "mlp1": out}
        last = phases[-1]
        layers = [l for l in range(2) if "mlp%d" % l in phases]
        K.cast_jobs = make_cast_jobs(K, layers)
        if "mix0" not in phases:
            m = K.A.mark()
            Caster(K, K.cast_jobs).flush()
            K.S.barrier()
            K.A.release(m)
        for ph in phases:
            dst = out if ph == last else chain[ph]
            if ph == "mix0":
                phase_mix0(K, cur, dst, P)
            elif ph == "mix1":
                phase_mix1(K, cur, dst, P)
            elif ph == "mlp0":
                phase_mlp(K, cur, dst, 0, norm_mlp[0])
            elif ph == "mlp1":
                phase_mlp(K, cur, dst, 1, norm_mlp[1], final_gain=final_norm)
            cur = dst
        K.S.emit()
    return nc


def make_consts():
    c = {}
    bf = ml_dtypes.bfloat16
    c["identb"] = np.eye(128, dtype=np.float32).astype(bf)
    R = np.zeros((128, 128), np.float32)
    for h in range(2):
        for sec in range(2):
            base = h * 64 + sec * 32
            for i in range(16):
                R[base + 16 + i, base + i] = -1.0
                R[base + i, base + 16 + i] = 1.0
    c["rmat"] = R.astype(bf)
    ob = np.zeros((128, 128), np.float32)
    ob[0:64, 0:64] = 1.0 / 64; ob[64:, 64:] = 1.0 / 64
    c["onesblk"] = ob.astype(bf)
    a = np.arange(32, dtype=np.float64)
    c32 = np.cos(2 * np.pi * np.outer(a, a) / 32); s32 = np.sin(2 * np.pi * np.outer(a, a) / 32)
    bdc = np.zeros((128, 128)); bds = np.zeros((128, 128))
    for b4 in range(4):
        bdc[b4 * 32:(b4 + 1) * 32, b4 * 32:(b4 + 1) * 32] = c32
        bds[b4 * 32:(b4 + 1) * 32, b4 * 32:(b4 + 1) * 32] = s32
    c["bdc"] = bdc.astype(np.float32).astype(bf); c["bds"] = bds.astype(np.float32).astype(bf); c["bdsn"] = (-bds).astype(np.float32).astype(bf)
    k64 = np.arange(64, dtype=np.float64)
    c64 = np.cos(2 * np.pi * np.outer(k64, k64) / 64) / 512.0; s64 = -np.sin(2 * np.pi * np.outer(k64, k64) / 64) / 512.0
    cb = np.zeros((128, 128)); sb_ = np.zeros((128, 128))
    for h in range(2):
        cb[h * 64:(h + 1) * 64, h * 64:(h + 1) * 64] = c64
        sb_[h * 64:(h + 1) * 64, h * 64:(h + 1) * 64] = s64
    c["c64bd"] = cb.astype(np.float32).astype(bf); c["s64bdn"] = sb_.astype(np.float32).astype(bf)
    n = np.arange(S_LEN); t = 128 * (n % 32) + n // 32
    row = (t // 64).astype(np.float64); colp = (t % 64).astype(np.float64)
    inv = 10000.0 ** (-np.arange(0, 32, 2, dtype=np.float64) / 32.0)
    cosp = np.zeros((128, S_LEN)); sinp = np.zeros((128, S_LEN))
    for p in range(128):
        d = p % 64
        pos = row if d < 32 else colp
        ang = pos * inv[d % 16]
        cosp[p] = np.cos(ang); sinp[p] = np.sin(ang)
    c["cosp"] = cosp.astype(np.float32); c["sinp"] = sinp.astype(np.float32)
    b = np.arange(128, dtype=np.float64)[:, None]; tt = np.arange(S_LEN, dtype=np.float64)[None, :]
    ph = 2 * np.pi * ((b * tt) % S_LEN) / S_LEN
    c["cbt"] = np.cos(ph).astype(np.float32).astype(bf); c["sbt"] = np.sin(ph).astype(np.float32).astype(bf)
    c["iotak"] = np.tile(np.arange(512, dtype=np.float32)[None, :], (128, 1))
    c["identf"] = np.eye(128, dtype=np.float32)
    sel = np.zeros((16, 128), np.float32)
    for cc in range(16):
        sel[cc, cc * 8:(cc + 1) * 8] = 1.0
    c["sel16"] = sel
    ii = (np.arange(128) % 8)[:, None]
    jj = (np.arange(128) // 16)[None, :]
    c["mskf"] = (ii <= jj).astype(np.float32); c["mskb"] = (ii >= jj).astype(np.float32)
    ci_ = (np.arange(128) // 8)[:, None]; co_ = (np.arange(128) % 16)[None, :]
    c["dperm"] = ((ii == jj) & (ci_ == co_)).astype(np.float32)
    return c


_USED = ("norm_mix", "norm_mlp", "mlp_w1", "mlp_w2", "final_norm", "w_in", "w_fnet", "q_norm", "k_norm", "w_out",
         "lam_re", "lam_im", "log_dt", "b_re", "b_im", "c_re", "c_im", "d_skip", "w_gate", "b_gate")


def kernel(**inputs):
    nc = build_program()
    consts = make_consts()
    x = np.ascontiguousarray(inputs["x"], dtype=np.float32)
    shared = {k: np.ascontiguousarray(inputs[k], dtype=np.float32) for k in _USED}
    shared.update(consts)
    in_maps = []
    for i in range(8):
        m = dict(shared)
        m["x"] = x[i]
        in_maps.append(m)
    res = run_bass_kernel_spmd(nc, in_maps, core_ids=list(range(8)))
    return np.stack([np.asarray(r["out"], dtype=np.float32) for r in res.results], axis=0)
```
